# Optimizing a Trainium2 kernel written in Bass

```python
import jax, jax.numpy as jnp
from jax import lax
import numpy as np

D_MODEL = 1024
BATCH = 8
SEQ = 8192
DEPTH = 1

CHUNK = 64
HEAD_SIZE = 64
D_RWKV = D_MODEL
N_RWKV_HEADS = D_RWKV // HEAD_SIZE
D_CONV = D_MODEL
CONV_WIDTH = 31
D_DECAY_LORA = 64
D_AAA_LORA = 64
D_GATE_LORA = 128
D_FF = 4 * D_MODEL
N_BRANCHES = 2
RMS_EPS = 1e-6
LN_EPS = 1e-5
GN_EPS = 64e-5
DECAY_OFFSET = 0.5
COL_RKV = 3 * D_RWKV
COL_GLU = 2 * D_CONV
COL_GATE = N_BRANCHES * D_MODEL
IN_COLS = COL_RKV + COL_GLU + COL_GATE

kernel_name = "hybrid_rwkv7_conformer_gated_block"


def _rmsnorm(x, g):
    xf = x.astype(jnp.float32)
    y = xf * lax.rsqrt(jnp.mean(xf * xf, axis=-1, keepdims=True) + RMS_EPS)
    return (y * g.astype(jnp.float32)).astype(x.dtype)


def _layernorm(x, g, b):
    xf = x.astype(jnp.float32)
    mu = jnp.mean(xf, axis=-1, keepdims=True)
    var = jnp.mean(jnp.square(xf - mu), axis=-1, keepdims=True)
    y = (xf - mu) * lax.rsqrt(var + LN_EPS)
    return (y * g.astype(jnp.float32) + b.astype(jnp.float32)).astype(x.dtype)


def _token_shift(z):
    return jnp.pad(z, ((0, 0), (1, 0), (0, 0)))[:, :-1]


def _heads(z):
    b, t, _ = z.shape
    return z.reshape(b, t, N_RWKV_HEADS, HEAD_SIZE)


def _wkv7_scan(r, w, k, v, kk, a):
    b, t, h, n = r.shape
    n_chunks = t // CHUNK

    def to_chunks(z):
        return jnp.moveaxis(z.astype(jnp.float32).reshape(b, n_chunks, CHUNK, h, n), (1, 2), (0, 1))

    def frame_step(S, inp):
        r_t, w_t, k_t, v_t, kk_t, a_t = inp
        sa = jnp.einsum('bhvk,bhk->bhv', S, -kk_t)
        S = (S * w_t[:, :, None, :]
             + sa[..., None] * (kk_t * a_t)[:, :, None, :]
             + v_t[..., None] * k_t[:, :, None, :])
        o_t = jnp.einsum('bhvk,bhk->bhv', S, r_t)
        return S, o_t

    def chunk_step(S, chunk_inp):
        return lax.scan(frame_step, S, chunk_inp)

    S0 = jnp.zeros((b, h, n, n), jnp.float32)
    inputs = (to_chunks(r), to_chunks(w), to_chunks(k), to_chunks(v), to_chunks(kk), to_chunks(a))
    _, o = lax.scan(chunk_step, S0, inputs)
    return jnp.moveaxis(o, (0, 1), (1, 2)).reshape(b, t, h, n)


def _rwkv7_mixer(h, r, k, v, mu_lora, w0, w1, w2, a0, a1, a2, g1, g2, k_k, k_a, r_k, ln_g, ln_b):
    dh = _token_shift(h) - h
    xw = h + dh * mu_lora[0]
    xa = h + dh * mu_lora[1]
    xg = h + dh * mu_lora[2]
    w_log = -jax.nn.softplus(-(w0 + jnp.tanh(xw @ w1) @ w2)) - DECAY_OFFSET
    decay = jnp.exp(-jnp.exp(w_log.astype(jnp.float32)))
    a = jax.nn.sigmoid(a0 + (xa @ a1) @ a2)
    g = jax.nn.sigmoid(xg @ g1) @ g2
    kk = _heads(k * k_k).astype(jnp.float32)
    kk = kk / jnp.maximum(jnp.linalg.norm(kk, axis=-1, keepdims=True), 1e-12)
    k = k * (1.0 + (a - 1.0) * k_a)
    rh, kh, vh, ah = _heads(r), _heads(k), _heads(v), _heads(a)
    o = _wkv7_scan(rh, _heads(decay), kh, vh, kk, ah)
    mu = jnp.mean(o, axis=-1, keepdims=True)
    var = jnp.mean(jnp.square(o - mu), axis=-1, keepdims=True)
    o = ((o - mu) * lax.rsqrt(var + GN_EPS)).reshape(h.shape[0], h.shape[1], D_RWKV)
    o = (o * ln_g.astype(jnp.float32) + ln_b.astype(jnp.float32)).astype(h.dtype)
    bonus = jnp.sum(rh * kh * r_k, axis=-1, keepdims=True) * vh
    o = o + bonus.reshape(o.shape)
    return o * g


def _conformer_conv_mixer(glu_in, conv_w, conv_b, ln_g, ln_b):
    u = glu_in[..., :D_CONV] * jax.nn.sigmoid(glu_in[..., D_CONV:])
    u = lax.conv_general_dilated(
        u, conv_w[:, None, :], window_strides=(1,), padding=((CONV_WIDTH - 1, 0),),
        dimension_numbers=('NWC', 'WIO', 'NWC'), feature_group_count=D_CONV) + conv_b
    u = _layernorm(u, ln_g, ln_b)
    return jax.nn.silu(u)


def setup_inputs(seed: int = 0) -> dict:
    key = jax.random.key(seed)
    ks = iter(jax.random.split(key, 40))
    L = DEPTH
    f32 = jnp.float32

    def nrm(shape, scale):
        return jax.random.normal(next(ks), shape, f32) * scale

    def unif(shape, lo, hi):
        return jax.random.uniform(next(ks), shape, f32, minval=lo, maxval=hi)

    return {
        "x": nrm((BATCH, SEQ, D_MODEL), 1.0),
        "norm_mix_g": 1.0 + nrm((L, D_MODEL), 0.05),
        "w_in": nrm((L, D_MODEL, IN_COLS), D_MODEL ** -0.5),
        "b_gate": nrm((L, COL_GATE), 0.1),
        "mu_rkv": unif((L, COL_RKV), 0.0, 1.0),
        "mu_lora": unif((L, 3, D_MODEL), 0.0, 1.0),
        "decay_w0": unif((L, D_RWKV), -3.0, 2.0),
        "decay_w1": nrm((L, D_MODEL, D_DECAY_LORA), D_MODEL ** -0.5),
        "decay_w2": nrm((L, D_DECAY_LORA, D_RWKV), 0.1),
        "aaa_a0": nrm((L, D_RWKV), 0.1),
        "aaa_a1": nrm((L, D_MODEL, D_AAA_LORA), D_MODEL ** -0.5),
        "aaa_a2": nrm((L, D_AAA_LORA, D_RWKV), 0.5 * D_AAA_LORA ** -0.5),
        "gate_g1": nrm((L, D_MODEL, D_GATE_LORA), D_MODEL ** -0.5),
        "gate_g2": nrm((L, D_GATE_LORA, D_RWKV), D_GATE_LORA ** -0.5),
        "k_k": 0.85 + nrm((L, D_RWKV), 0.05),
        "k_a": 1.0 + nrm((L, D_RWKV), 0.05),
        "r_k": nrm((L, N_RWKV_HEADS, HEAD_SIZE), 0.1),
        "ln_x_g": 1.0 + nrm((L, D_RWKV), 0.05),
        "ln_x_b": nrm((L, D_RWKV), 0.02),
        "w_rwkv_proj": nrm((L, D_RWKV, D_MODEL), D_RWKV ** -0.5),
        "conv_w": nrm((L, CONV_WIDTH, D_CONV), CONV_WIDTH ** -0.5),
        "conv_b": nrm((L, D_CONV), 0.02),
        "conv_ln_g": 1.0 + nrm((L, D_CONV), 0.05),
        "conv_ln_b": nrm((L, D_CONV), 0.02),
        "w_conv_proj": nrm((L, D_CONV, D_MODEL), D_CONV ** -0.5),
        "w_out": nrm((L, D_MODEL, D_MODEL), D_MODEL ** -0.5),
        "norm_ff_g": 1.0 + nrm((L, D_MODEL), 0.05),
        "w_ff1": nrm((L, D_MODEL, D_FF), D_MODEL ** -0.5),
        "w_ff2": nrm((L, D_FF, D_MODEL), 0.5 * D_FF ** -0.5),
        "norm_final_g": 1.0 + nrm((D_MODEL,), 0.05),
    }


def reference(x, norm_mix_g, w_in, b_gate, mu_rkv, mu_lora, decay_w0, decay_w1, decay_w2,
              aaa_a0, aaa_a1, aaa_a2, gate_g1, gate_g2, k_k, k_a, r_k, ln_x_g, ln_x_b,
              w_rwkv_proj, conv_w, conv_b, conv_ln_g, conv_ln_b, w_conv_proj, w_out,
              norm_ff_g, w_ff1, w_ff2, norm_final_g):
    b, t, _ = x.shape
    for l in range(DEPTH):
        h = _rmsnorm(x, norm_mix_g[l])
        proj = h @ w_in[l]
        rkv = proj[..., :COL_RKV]
        glu_in = proj[..., COL_RKV:COL_RKV + COL_GLU]
        gate_logits = proj[..., COL_RKV + COL_GLU:] + b_gate[l]
        rkv = rkv + (_token_shift(rkv) - rkv) * mu_rkv[l]
        r = rkv[..., :D_RWKV]
        k = rkv[..., D_RWKV:2 * D_RWKV]
        v = rkv[..., 2 * D_RWKV:]
        a_out = _rwkv7_mixer(h, r, k, v, mu_lora[l], decay_w0[l], decay_w1[l], decay_w2[l],
                             aaa_a0[l], aaa_a1[l], aaa_a2[l], gate_g1[l], gate_g2[l],
                             k_k[l], k_a[l], r_k[l], ln_x_g[l], ln_x_b[l])
        c_out = _conformer_conv_mixer(glu_in, conv_w[l], conv_b[l], conv_ln_g[l], conv_ln_b[l])
        gates = jax.nn.sigmoid(gate_logits).reshape(b, t, N_BRANCHES, D_MODEL)
        merged = gates[:, :, 0] * (a_out @ w_rwkv_proj[l]) + gates[:, :, 1] * (c_out @ w_conv_proj[l])
        x = x + merged @ w_out[l]
        h2 = _rmsnorm(x, norm_ff_g[l])
        x = x + jnp.square(jax.nn.relu(h2 @ w_ff1[l])) @ w_ff2[l]
    return _rmsnorm(x, norm_final_g)
```

```python
import contextlib
import numpy as np
import concourse.bass as bass
import concourse.mybir as mybir
from concourse.bass_utils import run_bass_kernel_spmd

F32 = mybir.dt.float32
BF16 = mybir.dt.bfloat16
AF = mybir.ActivationFunctionType
ALU = mybir.AluOpType

D = 1024
T_FULL = 8192
NCH = 8
TT = 128
NCST = 22
(C_GMIX, C_MUR, C_MUK, C_MUV, C_MUW, C_MUA, C_MUG, C_KK, C_KA, C_RK, C_LNG, C_LNB,
 C_CB, C_CLG, C_CLB, C_GFF, C_OMR, C_OMK, C_OMV, C_OMKA, C_CLG2, C_CLB2) = range(22)
NSLAB = 18
NSEM_DMA = 6
import os as _os
_STAGE = int(_os.environ.get("KSTAGE", "99"))
_SKIP = _os.environ.get("KSKIP", "").split(",")
_NOCONV = int(_os.environ.get("KNOCONV", "0"))


class _Op:
    __slots__ = ("eng", "fn", "waits", "count", "dma", "sem", "semval", "prewait")


class Sched:
    def __init__(self):
        self.ops = {k: [] for k in ("pe", "act", "dve", "pool", "sp")}
        self.last_w = {}
        self.readers = {}
        self.ndma = {"sp": 0, "pool": 0}

    def op(self, eng, fn, reads=(), writes=(), dma=False):
        o = _Op()
        o.eng, o.fn, o.dma = eng, fn, dma
        deps = []
        for r in reads:
            w = self.last_w.get(r)
            if w is not None:
                deps.append(w)
        for w_ in writes:
            w = self.last_w.get(w_)
            if w is not None:
                deps.append(w)
            deps.extend(self.readers.get(w_, ()))
        o.waits = deps
        self.ops[eng].append(o)
        o.count = len(self.ops[eng])
        o.prewait = None
        if dma:
            n = self.ndma[eng]
            self.ndma[eng] = n + 1
            o.sem = (eng, n % NSEM_DMA)
            o.semval = 16 * (n // NSEM_DMA + 1)
            if n >= NSEM_DMA:
                o.prewait = (o.sem, 16 * (n // NSEM_DMA))
        for w_ in writes:
            self.last_w[w_] = o
            self.readers[w_] = []
        for r in reads:
            self.readers.setdefault(r, []).append(o)
        return o

    def emit(self, nc, block, sems, dsems):
        engs = {"pe": block.tensor, "act": block.scalar, "dve": block.vector,
                "pool": block.gpsimd, "sp": block.sync}
        for name, deco in engs.items():
            ops = self.ops[name]

            def body(e, ops=ops, name=name):
                waited = {}
                for o in ops:
                    need = {}
                    for d in o.waits:
                        if d.dma:
                            key, val = ("d",) + d.sem, d.semval
                        else:
                            key, val = ("e", d.eng), d.count
                        if need.get(key, 0) < val:
                            need[key] = val
                    if o.prewait is not None:
                        key, val = ("d",) + o.prewait[0], o.prewait[1]
                        if need.get(key, 0) < val:
                            need[key] = val
                    for key, val in need.items():
                        if waited.get(key, 0) >= val:
                            continue
                        waited[key] = val
                        s = dsems[key[1:]] if key[0] == "d" else sems[key[1]]
                        e.wait_ge(s, val)
                    ins = o.fn(e)
                    if o.dma:
                        ins.then_inc(dsems[o.sem], 16)
                    else:
                        ins.then_inc(sems[name], 1)
                if name == "sp":
                    last = {}
                    for o in ops:
                        if o.dma:
                            last[o.sem] = o.semval
                    for k, v in last.items():
                        e.wait_ge(dsems[k], v)

            deco(body)


def _consts_np():
    i = np.arange(128)
    s, t = i[:, None], i[None, :]
    c = {}
    c["ident"] = (s == t).astype(np.float32)
    c["tri"] = np.where(s <= t, -0.5 * np.exp(-0.5), 0.0).astype(np.float32)
    c["msu"] = (t > s).astype(np.float32)
    c["mui"] = (t >= s).astype(np.float32)
    c["msl"] = (t < s).astype(np.float32)
    c["bones"] = ((s // 64) == (t // 64)).astype(np.float32)
    return np.stack([c[k] for k in ("ident", "tri", "msu", "mui", "msl", "bones")], 0)


def build(ntiles):
    nc = bass.Bass("TRN2", target_bir_lowering=False)
    T = ntiles * TT
    dt_in = {}

    def din(name, shape):
        dt_in[name] = nc.dram_tensor(name, list(shape), F32, kind="ExternalInput").ap()
        return dt_in[name]

    x = din("x", [T, D])
    w_in = din("w_in", [D, 7 * D])
    w_a = din("w_rwkv_proj", [D, D])
    w_c = din("w_conv_proj", [D, D])
    w_o = din("w_out", [D, D])
    w_f1 = din("w_ff1", [D, 4 * D])
    w_f2 = din("w_ff2", [4 * D, D])
    vrows = din("vrows", [16, D])
    convw = din("conv_w", [31, D])
    w0row = din("decay_w0", [1, D])
    a0row = din("aaa_a0", [1, D])
    bgrow = din("b_gate", [1, 2 * D])
    gfin = din("norm_final_g", [1, D])
    dw1 = din("decay_w1", [D, 64])
    dw2 = din("decay_w2", [64, D])
    aa1 = din("aaa_a1", [D, 64])
    aa2 = din("aaa_a2", [64, D])
    gg1 = din("gate_g1", [D, 128])
    gg2 = din("gate_g2", [128, D])
    cmat = din("cmat", [6, 128, 128])
    out = nc.dram_tensor("out", [T, D], F32, kind="ExternalOutput").ap()
    wbf = nc.dram_tensor("wbf", [NSLAB, 128, NCH * D], BF16, kind="ExternalOutput").ap()

    S = Sched()
    es = contextlib.ExitStack()
    with es:
        def sb(name, shape, dt):
            return es.enter_context(nc.sbuf_tensor(name, list(shape), dt))

        def ps(name, shape, dt=F32):
            return es.enter_context(nc.psum_tensor(name, list(shape), dt))

        Fs = [sb(f"F{i}", [128, NCH, TT], F32) for i in range(12) if i not in (4, 5, 8)]
        Fs = {i: t for i, t in zip([i for i in range(12) if i not in (4, 5, 8)], Fs)}
        Hs = [sb(f"H{i}", [128, NCH, TT], BF16) for i in range(16)]
        Ms = {k: sb(k, [128, 16, TT], BF16) for k in
              ("Mka", "Mbr", "Mkr", "YA", "YTA", "YB", "YTB", "PA")}
        ring = [sb(f"ring{i}", [128, NCH, D], BF16) for i in range(3)]
        Qs = [ps(f"Q{i}", [128, NCH, TT]) for i in range(4)]
        x_tm = sb("x_tm", [128, D], F32)
        h_ext = sb("h_ext", [128, NCH, TT + 1], BF16)
        u_ext = sb("u_ext", [128, NCH, TT + 30], BF16)
        lora_sb = sb("lora_sb", [128, TT], BF16)
        sg_sb = sb("sg_sb", [128, TT], BF16)
        tg_sb = sb("tg_sb", [128, TT], F32)
        r2 = sb("r2", [128, 32, TT], BF16)
        S_f = sb("S_f", [128, NCH, 64], F32)
        S_b = sb("S_b", [128, NCH, 64], BF16)
        halo = sb("halo", [128, 3, NCH, 1], F32)
        st = sb("st", [128, 8], F32)
        lnt = [sb(f"lnt{i}", [128, TT], F32) for i in range(4)]
        cst = sb("cst", [128, NCH, NCST], F32)
        cw = sb("cw", [128, NCH, 31], F32)
        cm_f = sb("cm_f", [128, 6, 128], F32)
        ident_b = sb("ident_b", [128, 128], BF16)
        bones_b = sb("bones_b", [128, 128], BF16)
        bones64_b = sb("bones64_b", [128, 128], BF16)
        onesdiv = sb("onesdiv", [128, 128], F32)
        gfin_bc = sb("gfin_bc", [128, D], F32)
        w0_row = sb("w0_row", [1, D], F32)
        a0_row = sb("a0_row", [1, D], BF16)
        bg_row = sb("bg_row", [1, 2 * D], BF16)
        ones_f = sb("ones_f", [1, 128], F32)
        ones_b = sb("ones_b", [1, 128], BF16)
        wa1 = sb("wa1", [128, NCH, 128], BF16)
        wa1mu = sb("wa1mu", [128, NCH, 128], BF16)
        g1 = sb("g1", [128, NCH, 128], BF16)
        g1mu = sb("g1mu", [128, NCH, 128], BF16)
        l2 = sb("l2", [128, D], BF16)
        g2 = sb("g2", [128, D], BF16)

        rows = Fs[0][:, :, :].rearrange("p c t -> p (c t)")
        hT = Hs[9][:, :, :].rearrange("p c t -> p (c t)")
        junk = Hs[15][:, :, :].rearrange("p c t -> p (c t)")
        ident_f = cm_f[:, 0, :]
        tri_f = cm_f[:, 1, :]
        msu, mui, msl = cm_f[:, 2, :], cm_f[:, 3, :], cm_f[:, 4, :]

        qi = [0]
        in_setup = [True]

        def nextQ():
            q = Qs[qi[0] % 4]
            qi[0] += 1
            return q, f"Q{(qi[0] - 1) % 4}"

        def cb(idx, n=TT):
            return cst[:, :, idx:idx + 1].to_broadcast([128, NCH, n])

        def mm(out_ap, lhsT, rhs, start, stop, reads, writes):
            if "pesetup" in _SKIP and in_setup[0]:
                return
            S.op("pe", lambda e: e.matmul(out_ap, lhsT, rhs, start=start, stop=stop),
                 reads=reads, writes=writes)

        def act(out_ap, in_ap, func, reads, writes, scale=1.0, bias=0.0, accum=None):
            if accum is None:
                S.op("act", lambda e: e.activation(out=out_ap, in_=in_ap, func=func,
                                                   bias=bias, scale=scale),
                     reads=reads, writes=writes)
            else:
                S.op("act", lambda e: e.activation(out=out_ap, in_=in_ap, func=func,
                                                   bias=bias, scale=scale, accum_out=accum),
                     reads=reads, writes=writes)

        def tt(eng, out_ap, a, b, op, reads, writes):
            S.op(eng, lambda e: e.tensor_tensor(out=out_ap, in0=a, in1=b, op=op),
                 reads=reads, writes=writes)

        def ts(eng, out_ap, a, s1, s2, op0, op1, reads, writes):
            if op1 is None:
                S.op(eng, lambda e: e.tensor_scalar(out=out_ap, in0=a, scalar1=s1, scalar2=None,
                                                    op0=op0), reads=reads, writes=writes)
            else:
                S.op(eng, lambda e: e.tensor_scalar(out=out_ap, in0=a, scalar1=s1, scalar2=s2,
                                                    op0=op0, op1=op1), reads=reads, writes=writes)

        def stt(out_ap, a, sc, b, op0, op1, reads, writes):
            S.op("dve", lambda e: e.scalar_tensor_tensor(out=out_ap, in0=a, scalar=sc, in1=b,
                                                         op0=op0, op1=op1),
                 reads=reads, writes=writes)

        def cp(eng, out_ap, in_ap, reads, writes):
            if eng == "act":
                act(out_ap, in_ap, AF.Copy, reads, writes)
            else:
                S.op(eng, lambda e: e.tensor_copy(out=out_ap, in_=in_ap), reads=reads, writes=writes)

        stg = {"n": 0}

        def dma(q, out_ap, in_ap, reads, writes, part=None, cols=None):
            if q == "pool":
                i = stg["n"] % 2
                stg["n"] += 1
                sf = flat(Fs[10 + i])
                p0, p1 = part
                if len(cols) == 2:
                    sview = sf[p0:p1, 0:cols[0] * cols[1]].rearrange("p (a b) -> p a b", a=cols[0])
                else:
                    sview = sf[p0:p1, 0:cols[0]]
                dma("sp", sview, in_ap, [], [f"F{10 + i}"])
                if out_ap.tensor.name.startswith("wbf"):
                    hv = flat(Hs[10 + i])[p0:p1, 0:cols[0]]
                    cp(("act", "dve")[i], hv, sview, [f"F{10 + i}"], [f"H{10 + i}"])
                    dma("sp", out_ap, hv, [f"H{10 + i}"], writes)
                else:
                    cp(("act", "dve")[i], out_ap, sview, [f"F{10 + i}"], writes)
                return
            S.op(q, lambda e: e.dma_start(out=out_ap, in_=in_ap), reads=reads, writes=writes, dma=True)

        def flat(tile3):
            return tile3[:, :, :].rearrange("p c t -> p (c t)")

        def slab_src(si):
            if si < 7:
                order = [0, 1, 2, 4, 3, 5, 6]
                g = order[si]
                return w_in[:, g * D:(g + 1) * D]
            if si == 7:
                return w_a
            if si == 8:
                return w_c
            if si == 9:
                return w_o
            if si < 14:
                g = si - 10
                return w_f1[:, g * D:(g + 1) * D]
            g = si - 14
            return w_f2[g * D:(g + 1) * D, :]

        for si in range(NSLAB if not _NOCONV else 0):
            src = slab_src(si).rearrange("(kc p) n -> p kc n", p=128)
            dst = wbf[si].rearrange("p (kc n) -> p kc n", kc=NCH)
            for kc in range(NCH):
                dma("pool", dst[:, kc, :], src[:, kc, :], reads=[], writes=[f"wbf{si}_{kc}"], part=(0, 128), cols=(D,))

        dma("pool", wa1[:, :, 0:64], dw1.rearrange("(kc p) n -> p kc n", p=128), [], ["wa1"], part=(0, 128), cols=(NCH, 64))
        dma("pool", wa1[:, :, 64:128], aa1.rearrange("(kc p) n -> p kc n", p=128), [], ["wa1"], part=(0, 128), cols=(NCH, 64))
        dma("pool", g1[:, :, :], gg1.rearrange("(kc p) n -> p kc n", p=128), [], ["g1"], part=(0, 128), cols=(NCH, 128))
        dma("pool", l2[0:64, :], dw2, [], ["l2"], part=(0, 64), cols=(D,))
        dma("pool", l2[64:128, :], aa2, [], ["l2"], part=(64, 128), cols=(D,))
        dma("pool", g2[:, :], gg2, [], ["g2"], part=(0, 128), cols=(D,))
        dma("pool", a0_row[:, :], a0row, [], ["a0_row"], part=(0, 1), cols=(D,))
        dma("pool", bg_row[:, 0:D], bgrow[:, 0:D], [], ["bg_row"], part=(0, 1), cols=(D,))
        dma("pool", bg_row[:, D:2 * D], bgrow[:, D:2 * D], [], ["bg_row"], part=(0, 1), cols=(D,))
        dma("sp", w0_row[:, :], w0row, [], ["w0_row"])
        dma("sp", cm_f[:, :, :], cmat.rearrange("k p n -> p k n"), [], ["cm_f"])
        dma("sp", rows[0:1, :], gfin, ["F0"], ["F0"])

        cp("dve", ident_b[:, :], ident_f, ["cm_f"], ["ident_b"])
        cp("dve", bones_b[:, :], cm_f[:, 5, :], ["cm_f"], ["bones_b"])
        ts("dve", bones64_b[:, :], cm_f[:, 5, :], 1.0 / 64, None, ALU.mult, None, ["cm_f"], ["bones64_b"])
        S.op("pool", lambda e: e.memset(onesdiv[:, :], 1.0 / D), writes=["onesdiv"])
        S.op("pool", lambda e: e.memset(ones_b[:, :], 1.0), writes=["ones_b"])
        S.op("pool", lambda e: e.memset(h_ext[:, :, :], 0.0), writes=["h_ext"])
        S.op("pool", lambda e: e.memset(u_ext[:, :, :], 0.0), writes=["u_ext"])
        S.op("pool", lambda e: e.memset(halo[:, :, :, :], 0.0), writes=["halo"])
        S.op("pool", lambda e: e.memset(S_f[:, :, :], 0.0), writes=["S_f"])
        S.op("pool", lambda e: e.memset(S_b[:, :, :], 0.0), writes=["S_b"])

        S.op("pool", lambda e: e.memset(ones_f[:, :], 1.0), writes=["ones_f"])
        q, qn = nextQ()
        qf = flat(q)
        for hf in range(2):
            mm(qf[:, hf * 512:(hf + 1) * 512], ones_f[0:1, :], rows[0:1, hf * 512:(hf + 1) * 512], True, True,
               ["ones_f", "F0"], [qn])
        cp("dve", gfin_bc[:, :], qf, [qn], ["gfin_bc"])
        dma("sp", rows[0:16, :], vrows, ["F0"], ["F0"])
        q, qn = nextQ()
        qf = flat(q)
        for c in range(NCH):
            mm(qf[:, c * 32:c * 32 + 16], rows[0:16, c * 128:(c + 1) * 128], cm_f[0:16, 0, 0:16],
               True, True, ["F0", "cm_f"], [qn])
        cp("dve", cst[:, :, 0:16], qf[:, 0:256].rearrange("p (c k) -> p c k", c=NCH)[:, :, 0:16],
           [qn], ["cst"])
        for (dst_i, src_i) in ((C_OMR, C_MUR), (C_OMK, C_MUK), (C_OMV, C_MUV), (C_OMKA, C_KA)):
            ts("dve", cst[:, :, dst_i:dst_i + 1], cst[:, :, src_i:src_i + 1], -1.0, 1.0,
               ALU.mult, ALU.add, ["cst"], ["cst"])
        for (dst_i, src_i) in ((C_CLG2, C_CLG), (C_CLB2, C_CLB)):
            ts("dve", cst[:, :, dst_i:dst_i + 1], cst[:, :, src_i:src_i + 1], 0.5, None,
               ALU.mult, None, ["cst"], ["cst"])
        dma("sp", rows[0:31, :], convw, ["F0"], ["F0"])
        q, qn = nextQ()
        qf = flat(q)
        for c in range(NCH):
            mm(qf[:, c * 32:c * 32 + 31], rows[0:31, c * 128:(c + 1) * 128], cm_f[0:31, 0, 0:31],
               True, True, ["F0", "cm_f"], [qn])
        ts("dve", cw[:, :, :], qf[:, 0:256].rearrange("p (c k) -> p c k", c=NCH)[:, :, 0:31],
           0.5, None, ALU.mult, None, [qn], ["cw"])
        tt("dve", wa1mu[:, :, 0:64], wa1[:, :, 0:64],
           cst[:, :, C_MUW:C_MUW + 1].to_broadcast([128, NCH, 64]), ALU.mult, ["wa1", "cst"], ["wa1mu"])
        tt("dve", wa1mu[:, :, 64:128], wa1[:, :, 64:128],
           cst[:, :, C_MUA:C_MUA + 1].to_broadcast([128, NCH, 64]), ALU.mult, ["wa1", "cst"], ["wa1mu"])
        tt("dve", g1mu[:, :, :], g1[:, :, :],
           cst[:, :, C_MUG:C_MUG + 1].to_broadcast([128, NCH, 128]), ALU.mult, ["g1", "cst"], ["g1mu"])

        gs = {"issued": 0, "used": 0}
        total_slabs = NSLAB * ntiles

        def issue_to(n):
            while gs["issued"] < min(n, total_slabs):
                g = gs["issued"]
                si, ri = g % NSLAB, g % 3
                dma("sp", ring[ri][:, :, :].rearrange("p c n -> p (c n)"), wbf[si],
                    [f"wbf{si}_{kc}" for kc in range(NCH)], [f"ring{ri}"])
                gs["issued"] += 1

        def use_slab():
            g = gs["used"]
            issue_to(g + 2)
            gs["used"] += 1
            return ring[g % 3], f"ring{g % 3}"

        def after_slab():
            issue_to(gs["used"] + 2)

        def rstd_from_ss(col, eps):
            ts("dve", st[:, col:col + 1], st[:, col:col + 1], 1.0 / D, eps, ALU.mult, ALU.add, ["st"], ["st"])
            act(st[:, col:col + 1], st[:, col:col + 1], AF.Sqrt, ["st"], ["st"])
            S.op("dve", lambda e: e.reciprocal(out=st[:, col:col + 1], in_=st[:, col:col + 1]),
                 reads=["st"], writes=["st"])

        def norm_to_fm(gidx, dst_ap, dst_key, col):
            S.op("pool", lambda e: e.memset(st[:, col:col + 1], 0.0), writes=["st"])
            act(junk[:, :], x_tm[:, :], AF.Square, ["x_tm"], ["H15", "st"], accum=st[:, col:col + 1])
            rstd_from_ss(col, 1e-6)
            act(hT[:, :], x_tm[:, :], AF.Identity, ["x_tm", "st"], ["H9"], scale=st[:, col:col + 1])
            q, qn = nextQ()
            for c in range(NCH):
                mm(q[:, c, :], hT[:, c * 128:(c + 1) * 128], ident_b[:, :], True, True,
                   ["H9", "ident_b"], [qn])
            tt("dve", dst_ap, q[:, :, :], cb(gidx), ALU.mult, [qn, "cst"], [dst_key])

        F = Fs
        H = Hs

        def fm_proj(q, qn, slab, slabn, rhs_fn, rhs_keys, bias_row=None, bias_off=0, bias_key=None):
            for mc in range(NCH):
                for kc in range(NCH):
                    mm(q[:, mc, :], slab[:, kc, mc * 128:(mc + 1) * 128], rhs_fn(kc),
                       kc == 0, (kc == NCH - 1) and bias_row is None, [slabn] + rhs_keys, [qn])
                if bias_row is not None:
                    mm(q[:, mc, :], bias_row[0:1, bias_off + mc * 128:bias_off + (mc + 1) * 128],
                       ones_b[0:1, :], False, True, [bias_key, "ones_b"], [qn])

        in_setup[0] = False
        for it in range(ntiles):
            if it > 0:
                cp("pool", h_ext[:, :, 0:1], h_ext[:, :, TT:TT + 1], ["h_ext"], ["h_ext"])
                cp("pool", u_ext[:, :, 0:30], u_ext[:, :, TT:TT + 30], ["u_ext"], ["u_ext"])
            dma("sp", x_tm[:, :], x[it * TT:(it + 1) * TT, :], [], ["x_tm"])
            if _STAGE == 0:
                dma("sp", out[it * TT:(it + 1) * TT, :], x_tm[:, :], ["x_tm"], ["out"])
                continue
            norm_to_fm(C_GMIX, h_ext[:, :, 1:TT + 1], "h_ext", 0)
            dh = H[0]
            tt("dve", dh[:, :, :], h_ext[:, :, 0:TT], h_ext[:, :, 1:TT + 1], ALU.subtract, ["h_ext"], ["H0"])
            hc = lambda kc: h_ext[:, kc, 1:TT + 1]

            q, qn = nextQ()
            qf = flat(q)
            for kc in range(NCH):
                mm(qf[:, 0:128], wa1[:, kc, :], hc(kc), kc == 0, False, ["wa1", "h_ext"], [qn])
            for kc in range(NCH):
                mm(qf[:, 0:128], wa1mu[:, kc, :], dh[:, kc, :], False, kc == NCH - 1, ["wa1mu", "H0"], [qn])
            for kc in range(NCH):
                mm(qf[:, 128:256], g1[:, kc, :], hc(kc), kc == 0, False, ["g1", "h_ext"], [qn])
            for kc in range(NCH):
                mm(qf[:, 128:256], g1mu[:, kc, :], dh[:, kc, :], False, kc == NCH - 1, ["g1mu", "H0"], [qn])
            act(lora_sb[0:64, :], qf[0:64, 0:128], AF.Tanh, [qn], ["lora_sb"])
            act(lora_sb[64:128, :], qf[64:128, 0:128], AF.Copy, [qn], ["lora_sb"])
            act(tg_sb[:, :], qf[:, 128:256], AF.Tanh, [qn], ["tg_sb"], scale=0.5)
            ts("dve", sg_sb[:, :], tg_sb[:, :], 0.5, 0.5, ALU.mult, ALU.add, ["tg_sb"], ["sg_sb"])

            q, qn = nextQ()
            qf = flat(q)
            for hf in range(2):
                mm(qf[:, hf * 512:(hf + 1) * 512], lora_sb[0:64, :], l2[0:64, hf * 512:(hf + 1) * 512],
                   True, False, ["lora_sb", "l2"], [qn])
                mm(qf[:, hf * 512:(hf + 1) * 512], ones_f[0:1, :], w0_row[0:1, hf * 512:(hf + 1) * 512],
                   False, True, ["ones_f", "w0_row"], [qn])
            s_tm = flat(F[0])
            act(s_tm, qf, AF.Tanh, [qn], ["F0"], scale=0.5)
            ts("dve", s_tm, s_tm, 1.0, None, ALU.add, None, ["F0"], ["F0"])
            q, qn = nextQ()
            for c in range(NCH):
                mm(q[:, c, :], s_tm[:, c * 128:(c + 1) * 128], tri_f, True, True, ["F0", "cm_f"], [qn])
            E1, E3 = F[1], F[2]
            act(E1[:, :, :], q[:, :, :], AF.Exp, [qn], ["F1"])
            act(E3[:, :, :], q[:, :, :], AF.Exp, [qn], ["F2"], scale=-1.0)

            q, qn = nextQ()
            for c in range(NCH):
                mm(q[:, c, :], l2[64:128, c * 128:(c + 1) * 128], lora_sb[64:128, :], True, False,
                   ["l2", "lora_sb"], [qn])
                mm(q[:, c, :], a0_row[0:1, c * 128:(c + 1) * 128], ones_b[0:1, :], False, True,
                   ["a0_row", "ones_b"], [qn])
            a_t = F[3]
            act(a_t[:, :, :], q[:, :, :], AF.Tanh, [qn], ["F3"], scale=0.5)
            ts("dve", a_t[:, :, :], a_t[:, :, :], 0.5, 0.5, ALU.mult, ALU.add, ["F3"], ["F3"])
            q, qn = nextQ()
            for c in range(NCH):
                mm(q[:, c, :], g2[:, c * 128:(c + 1) * 128], sg_sb[:, :], True, True, ["g2", "sg_sb"], [qn])
            g_sb = H[2]
            cp("act", g_sb[:, :, :], q[:, :, :], [qn], ["H2"])

            rkv = [F[6], F[7], H[3]]
            for j in range(3):
                slab, slabn = use_slab()
                q, qn = nextQ()
                fm_proj(q, qn, slab, slabn, hc, ["h_ext"])
                after_slab()
                tmp, t2 = F[0], F[9]
                tt("dve", tmp[:, :, :], q[:, :, :], cb(C_OMR + j), ALU.mult, [qn, "cst"], ["F0"])
                tt("dve", t2[:, :, 1:TT], q[:, :, 0:TT - 1], cb(C_MUR + j, TT - 1), ALU.mult, [qn, "cst"], ["F9"])
                tt("dve", t2[:, :, 0:1], halo[:, j, :, :], cst[:, :, C_MUR + j:C_MUR + j + 1], ALU.mult,
                   ["halo", "cst"], ["F9"])
                cp("act", halo[:, j, :, :], q[:, :, TT - 1:TT], [qn], ["halo"])
                tt("pool", rkv[j][:, :, :], tmp[:, :, :], t2[:, :, :], ALU.add, ["F0", "F9"], [("F6", "F7", "H3")[j]])
            r_t, k_t, v_t = rkv
            v_bf = v_t

            slab, slabn = use_slab()
            q, qn = nextQ()
            fm_proj(q, qn, slab, slabn, hc, ["h_ext"])
            after_slab()
            tb = H[4]
            act(tb[:, :, :], q[:, :, :], AF.Tanh, [qn], ["H4"], scale=0.5)
            slab, slabn = use_slab()
            q, qn = nextQ()
            fm_proj(q, qn, slab, slabn, hc, ["h_ext"])
            after_slab()
            stt(u_ext[:, :, 30:30 + TT], tb[:, :, :], 1.0, q[:, :, :], ALU.add, ALU.mult, ["H4", qn], ["u_ext"])
            tG = [H[5], H[6]]
            for j in range(2):
                slab, slabn = use_slab()
                q, qn = nextQ()
                fm_proj(q, qn, slab, slabn, hc, ["h_ext"], bias_row=bg_row, bias_off=j * D, bias_key="bg_row")
                after_slab()
                act(tG[j][:, :, :], q[:, :, :], AF.Tanh, [qn], [f"H{5 + j}"], scale=0.5)

            acc, prod = F[9], F[10]
            for j in range(31):
                wj = cw[:, :, j:j + 1].to_broadcast([128, NCH, TT])
                if j == 0:
                    tt("pool", acc[:, :, :], u_ext[:, :, 0:TT], wj, ALU.mult, ["u_ext", "cw"], ["F9"])
                else:
                    tt("pool", prod[:, :, :], u_ext[:, :, j:j + TT], wj, ALU.mult, ["u_ext", "cw"], ["F10"])
                    tt("pool", acc[:, :, :], acc[:, :, :], prod[:, :, :], ALU.add, ["F9", "F10"], ["F9"])
            tt("pool", acc[:, :, :], acc[:, :, :], cb(C_CB), ALU.add, ["F9", "cst"], ["F9"])
            sq = F[10]
            act(sq[:, :, :], acc[:, :, :], AF.Square, ["F9"], ["F10"])
            q, qn = nextQ()
            qf = flat(q)
            for c in range(NCH):
                mm(qf[:, 0:128], onesdiv[:, :], acc[:, c, :], c == 0, c == NCH - 1, ["onesdiv", "F9"], [qn])
            for c in range(NCH):
                mm(qf[:, 128:256], onesdiv[:, :], sq[:, c, :], c == 0, c == NCH - 1, ["onesdiv", "F10"], [qn])
            mean_s, nm2, var_s = lnt[0], lnt[1], lnt[2]
            cp("act", mean_s[:, :], qf[:, 0:128], [qn], ["lnt0"])
            stt(nm2[:, :], mean_s[:, :], -1.0, mean_s[:, :], ALU.mult, ALU.mult, ["lnt0"], ["lnt1"])
            stt(var_s[:, :], qf[:, 128:256], 1e-5, nm2[:, :], ALU.add, ALU.add, [qn, "lnt1"], ["lnt2"])
            act(var_s[:, :], var_s[:, :], AF.Sqrt, ["lnt2"], ["lnt2"])
            S.op("dve", lambda e: e.reciprocal(out=var_s[:, :], in_=var_s[:, :]), reads=["lnt2"], writes=["lnt2"])
            tt("dve", acc[:, :, :], acc[:, :, :], mean_s[:, :].unsqueeze(1).to_broadcast([128, NCH, TT]),
               ALU.subtract, ["F9", "lnt0"], ["F9"])
            tt("dve", acc[:, :, :], acc[:, :, :], var_s[:, :].unsqueeze(1).to_broadcast([128, NCH, TT]),
               ALU.mult, ["F9", "lnt2"], ["F9"])
            tt("dve", acc[:, :, :], acc[:, :, :], cb(C_CLG2), ALU.mult, ["F9", "cst"], ["F9"])
            tt("dve", acc[:, :, :], acc[:, :, :], cb(C_CLB2), ALU.add, ["F9", "cst"], ["F9"])
            th = H[7]
            act(th[:, :, :], acc[:, :, :], AF.Tanh, ["F9"], ["H7"])
            C_fm = H[8]
            stt(C_fm[:, :, :], th[:, :, :], 1.0, acc[:, :, :], ALU.add, ALU.mult, ["H7", "F9"], ["H8"])

            kkr, ksq = F[0], H[9]
            tt("dve", kkr[:, :, :], k_t[:, :, :], cb(C_KK), ALU.mult, ["F7", "cst"], ["F0"])
            act(ksq[:, :, :], kkr[:, :, :], AF.Square, ["F0"], ["H9"])
            q, qn = nextQ()
            for c in range(NCH):
                mm(q[:, c, :], bones_b[:, :], ksq[:, c, :], True, True, ["bones_b", "H9"], [qn])
            rn = F[9]
            act(rn[:, :, :], q[:, :, :], AF.Sqrt, [qn], ["F9"], bias=1e-24)
            S.op("dve", lambda e: e.reciprocal(out=rn[:, :, :], in_=rn[:, :, :]), reads=["F9"], writes=["F9"])
            kkn = F[0]
            tt("dve", kkn[:, :, :], kkr[:, :, :], rn[:, :, :], ALU.mult, ["F0", "F9"], ["F0"])
            t1 = F[9]
            tt("dve", t1[:, :, :], a_t[:, :, :], cb(C_KA), ALU.mult, ["F3", "cst"], ["F9"])
            tt("pool", t1[:, :, :], t1[:, :, :], cb(C_OMKA), ALU.add, ["F9", "cst"], ["F9"])
            kmod = F[10]
            tt("pool", kmod[:, :, :], k_t[:, :, :], t1[:, :, :], ALU.mult, ["F7", "F9"], ["F10"])
            beta = F[11]
            tt("dve", beta[:, :, :], kkn[:, :, :], a_t[:, :, :], ALU.mult, ["F0", "F3"], ["F11"])
            Rt, Bt, Kt, At, Bh, Kh = H[10], H[11], H[12], H[13], H[14], H[15]
            tt("dve", Rt[:, :, :], r_t[:, :, :], E1[:, :, :], ALU.mult, ["F6", "F1"], ["H10"])
            tt("dve", Bt[:, :, :], beta[:, :, :], E3[:, :, :], ALU.mult, ["F11", "F2"], ["H11"])
            tt("pool", Kt[:, :, :], kmod[:, :, :], E3[:, :, :], ALU.mult, ["F10", "F2"], ["H12"])
            stt(At[:, :, 1:TT], kkn[:, :, 1:TT], -1.0, E1[:, :, 0:TT - 1], ALU.mult, ALU.mult, ["F0", "F1"], ["H13"])
            ts("dve", At[:, :, 0:1], kkn[:, :, 0:1], -1.0, None, ALU.mult, None, ["F0"], ["H13"])
            wcb = E1[:, :, TT - 1:TT].to_broadcast([128, NCH, TT])
            tt("pool", Bh[:, :, :], Bt[:, :, :], wcb, ALU.mult, ["H11", "F1"], ["H14"])
            tt("pool", Kh[:, :, :], Kt[:, :, :], wcb, ALU.mult, ["H12", "F1"], ["H15"])
            rk = H[9]
            tt("dve", F[9][:, :, :], r_t[:, :, :], cb(C_RK), ALU.mult, ["F6", "cst"], ["F9"])
            tt("dve", rk[:, :, :], F[9][:, :, :], kmod[:, :, :], ALU.mult, ["F9", "F10"], ["H9"])
            q, qn = nextQ()
            for c in range(NCH):
                mm(q[:, c, :], bones_b[:, :], rk[:, c, :], True, True, ["bones_b", "H9"], [qn])
            bv = F[9]
            tt("dve", bv[:, :, :], q[:, :, :], v_t[:, :, :], ALU.mult, [qn, "H3"], ["F9"])
            V_tm, Bh_tm, Kh_tm = H[0], H[1], H[4]
            for (src, srck, dst, dstk, eng) in ((v_bf, "H3", V_tm, "H0", "act"), (Bh, "H14", Bh_tm, "H1", "act"),
                                               (Kh, "H15", Kh_tm, "H4", "dve")):
                q, qn = nextQ()
                for c in range(NCH):
                    mm(q[:, c, :], src[:, c, :], ident_b[:, :], True, True, [srck, "ident_b"], [qn])
                cp(eng, dst[:, :, :], q[:, :, :], [qn], [dstk])
            Vf, Bhf, Khf = flat(V_tm), flat(Bh_tm), flat(Kh_tm)

            def hs(h):
                return (h % 2) * 64, h // 2

            def score(lh, lk, rh_, rk_, mask, dst, dstk, eng):
                for half in range(2):
                    q, qn = nextQ()
                    for hh in range(8):
                        h = half * 8 + hh
                        rb, c = hs(h)
                        mm(q[:, hh, :], lh[rb:rb + 64, c, :], rh_[rb:rb + 64, c, :], True, True, [lk, rk_], [qn])
                    tt(eng, dst[:, half * 8:(half + 1) * 8, :], q[:, :, :],
                       mask.unsqueeze(1).to_broadcast([128, 8, TT]), ALU.mult, [qn, "cm_f"], [dstk])

            score(Bt, "H11", At, "H13", msu, Ms["YA"], "YA", "dve")
            score(At, "H13", Bt, "H11", msl, Ms["YTA"], "YTA", "dve")
            score(Kt, "H12", At, "H13", msu, Ms["Mka"], "Mka", "dve")
            score(Bt, "H11", Rt, "H10", mui, Ms["Mbr"], "Mbr", "dve")
            score(Kt, "H12", Rt, "H10", mui, Ms["Mkr"], "Mkr", "dve")

            Y, YT, Yn, YTn = "YA", "YTA", "YB", "YTB"
            P, Pn = "PA", "PA"
            tt("pool", Ms[P][:, :, :], Ms[Y][:, :, :], ident_b[:, :].unsqueeze(1).to_broadcast([128, 16, TT]),
               ALU.add, [Y, "ident_b"], [P])
            for lvl in range(1, 7):
                for half in range(2):
                    q, qn = nextQ()
                    for hh in range(8):
                        h = half * 8 + hh
                        mm(q[:, hh, :], Ms[Y][:, h, :], Ms[YT][:, h, :], True, True, [Y, YT], [qn])
                    cp("act", Ms[YTn][:, half * 8:(half + 1) * 8, :], q[:, :, :], [qn], [YTn])
                if lvl < 6:
                    for half in range(2):
                        q, qn = nextQ()
                        for hh in range(8):
                            h = half * 8 + hh
                            mm(q[:, hh, :], Ms[YT][:, h, :], Ms[Y][:, h, :], True, True, [Y, YT], [qn])
                        cp("act", Ms[Yn][:, half * 8:(half + 1) * 8, :], q[:, :, :], [qn], [Yn])
                for half in range(2):
                    q, qn = nextQ()
                    for hh in range(8):
                        h = half * 8 + hh
                        mm(q[:, hh, :], Ms[YTn][:, h, :], Ms[P][:, h, :], True, True, [YTn, P], [qn])
                    tt("dve", Ms[Pn][:, half * 8:(half + 1) * 8, :], q[:, :, :],
                       Ms[P][:, half * 8:(half + 1) * 8, :], ALU.add, [qn, P], [Pn])
                Y, Yn = Yn, Y
                YT, YTn = YTn, YT
                P, Pn = Pn, P
            Tm, Tk = Ms[P], P

            X_bf, U_bf = H[7], H[9]
            Xf, Uf = flat(X_bf), flat(U_bf)
            q, qn = nextQ()
            qf = flat(q)
            for h in range(16):
                rb, c = hs(h)
                mm(qf[:, h * 64:(h + 1) * 64], At[rb:rb + 64, c, :], S_b[rb:rb + 64, c, :], True, False,
                   ["H13", "S_b"], [qn])
                mm(qf[:, h * 64:(h + 1) * 64], Ms["Mka"][:, h, :], Vf[:, h * 64:(h + 1) * 64], False, True,
                   ["Mka", "H0"], [qn])
            cp("act", Xf, qf, [qn], ["H7"])
            q, qn = nextQ()
            qf = flat(q)
            for h in range(16):
                mm(qf[:, h * 64:(h + 1) * 64], Tm[:, h, :], Xf[:, h * 64:(h + 1) * 64], True, True,
                   [Tk, "H7"], [qn])
            cp("act", Uf, qf, [qn], ["H9"])
            qo, qon = nextQ()
            for h in range(16):
                rb, c = hs(h)
                mm(qo[rb:rb + 64, c, :], S_b[rb:rb + 64, c, :], Rt[rb:rb + 64, c, :], True, False,
                   ["S_b", "H10"], [qon])
                mm(qo[rb:rb + 64, c, :], Uf[:, h * 64:(h + 1) * 64], Ms["Mbr"][:, h, :], False, False,
                   ["H9", "Mbr"], [qon])
                mm(qo[rb:rb + 64, c, :], Vf[:, h * 64:(h + 1) * 64], Ms["Mkr"][:, h, :], False, True,
                   ["H0", "Mkr"], [qon])
            qd, qdn = nextQ()
            qdf = flat(qd)[:, 0:512].rearrange("p (c v) -> p c v", c=NCH)
            for h in range(16):
                rb, c = hs(h)
                mm(qdf[rb:rb + 64, c, :], Bhf[:, c * 128 + rb:c * 128 + rb + 64], Uf[:, h * 64:(h + 1) * 64],
                   True, False, ["H1", "H9"], [qdn])
                mm(qdf[rb:rb + 64, c, :], Khf[:, c * 128 + rb:c * 128 + rb + 64], Vf[:, h * 64:(h + 1) * 64],
                   False, True, ["H4", "H0"], [qdn])
            tt("dve", S_f[:, :, :], S_f[:, :, :], E1[:, :, TT - 1:TT].to_broadcast([128, NCH, 64]), ALU.mult,
               ["S_f", "F1"], ["S_f"])
            tt("dve", S_f[:, :, :], qdf, S_f[:, :, :], ALU.add, [qdn, "S_f"], ["S_f"])
            cp("act", S_b[:, :, :], S_f[:, :, :], ["S_f"], ["S_b"])

            o_sb, osq = H[10], H[11]
            cp("act", o_sb[:, :, :], qo[:, :, :], [qon], ["H10"])
            act(osq[:, :, :], qo[:, :, :], AF.Square, [qon], ["H11"])
            qm, qmn = nextQ()
            for c in range(NCH):
                mm(qm[:, c, :], bones64_b[:, :], o_sb[:, c, :], True, True, ["bones64_b", "H10"], [qmn])
            qq, qqn = nextQ()
            for c in range(NCH):
                mm(qq[:, c, :], bones64_b[:, :], osq[:, c, :], True, True, ["bones64_b", "H11"], [qqn])
            gm, gv, gd = F[0], F[10], F[11]
            cp("act", gm[:, :, :], qm[:, :, :], [qmn], ["F0"])
            stt(gv[:, :, :], gm[:, :, :], -1.0, gm[:, :, :], ALU.mult, ALU.mult, ["F0"], ["F10"])
            stt(gv[:, :, :], qq[:, :, :], 64e-5, gv[:, :, :], ALU.add, ALU.add, [qqn, "F10"], ["F10"])
            act(gv[:, :, :], gv[:, :, :], AF.Sqrt, ["F10"], ["F10"])
            S.op("dve", lambda e: e.reciprocal(out=gv[:, :, :], in_=gv[:, :, :]), reads=["F10"], writes=["F10"])
            tt("dve", gd[:, :, :], qo[:, :, :], gm[:, :, :], ALU.subtract, [qon, "F0"], ["F11"])
            tt("dve", gd[:, :, :], gd[:, :, :], gv[:, :, :], ALU.mult, ["F11", "F10"], ["F11"])
            tt("pool", gd[:, :, :], gd[:, :, :], cb(C_LNG), ALU.mult, ["F11", "cst"], ["F11"])
            tt("pool", gd[:, :, :], gd[:, :, :], cb(C_LNB), ALU.add, ["F11", "cst"], ["F11"])
            tt("pool", gd[:, :, :], gd[:, :, :], bv[:, :, :], ALU.add, ["F11", "F9"], ["F11"])
            A_fm = H[12]
            tt("dve", A_fm[:, :, :], gd[:, :, :], g_sb[:, :, :], ALU.mult, ["F11", "H2"], ["H12"])

            slab, slabn = use_slab()
            qa, qan = nextQ()
            fm_proj(qa, qan, slab, slabn, lambda kc: A_fm[:, kc, :], ["H12"])
            after_slab()
            m1, m2 = F[0], F[10]
            stt(m1[:, :, :], tG[0][:, :, :], 1.0, qa[:, :, :], ALU.add, ALU.mult, ["H5", qan], ["F0"])
            slab, slabn = use_slab()
            qc, qcn = nextQ()
            fm_proj(qc, qcn, slab, slabn, lambda kc: C_fm[:, kc, :], ["H8"])
            after_slab()
            stt(m2[:, :, :], tG[1][:, :, :], 1.0, qc[:, :, :], ALU.add, ALU.mult, ["H6", qcn], ["F10"])
            merged = H[13]
            tt("pool", merged[:, :, :], m1[:, :, :], m2[:, :, :], ALU.add, ["F0", "F10"], ["H13"])
            slab, slabn = use_slab()
            q, qn = nextQ()
            qf = flat(q)
            for hf in range(2):
                for kc in range(NCH):
                    mm(qf[:, hf * 512:(hf + 1) * 512], merged[:, kc, :], slab[:, kc, hf * 512:(hf + 1) * 512],
                       kc == 0, kc == NCH - 1, ["H13", slabn], [qn])
            after_slab()
            stt(x_tm[:, :], qf, 0.5, x_tm[:, :], ALU.mult, ALU.add, [qn, "x_tm"], ["x_tm"])

            h2 = H[14]
            norm_to_fm(C_GFF, h2[:, :, :], "H14", 1)
            rl = H[15]
            for g in range(4):
                slab, slabn = use_slab()
                q, qn = nextQ()
                fm_proj(q, qn, slab, slabn, lambda kc: h2[:, kc, :], ["H14"])
                after_slab()
                act(rl[:, :, :], q[:, :, :], AF.Relu, [qn], ["H15"])
                tt("pool", r2[:, g * 8:(g + 1) * 8, :], rl[:, :, :], rl[:, :, :], ALU.mult, ["H15"], ["r2"])
            q, qn = nextQ()
            qf = flat(q)
            for g in range(4):
                slab, slabn = use_slab()
                for hf in range(2):
                    for kc in range(NCH):
                        mm(qf[:, hf * 512:(hf + 1) * 512], r2[:, g * 8 + kc, :], slab[:, kc, hf * 512:(hf + 1) * 512],
                           g == 0 and kc == 0, g == 3 and kc == NCH - 1, ["r2", slabn], [qn])
                after_slab()
            tt("dve", x_tm[:, :], qf, x_tm[:, :], ALU.add, [qn, "x_tm"], ["x_tm"])

            ytmp, osb = flat(F[0]), flat(F[11])
            S.op("pool", lambda e: e.memset(st[:, 2:3], 0.0), writes=["st"])
            act(junk[:, :], x_tm[:, :], AF.Square, ["x_tm"], ["H15", "st"], accum=st[:, 2:3])
            rstd_from_ss(2, 1e-6)
            act(ytmp, x_tm[:, :], AF.Identity, ["x_tm", "st"], ["F0"], scale=st[:, 2:3])
            tt("dve", osb, ytmp, gfin_bc[:, :], ALU.mult, ["F0", "gfin_bc"], ["F11"])
            dma("sp", out[it * TT:(it + 1) * TT, :], osb, ["F11"], ["out"])

        sems = {k: es.enter_context(nc.semaphore(f"s_{k}")) for k in ("pe", "act", "dve", "pool", "sp")}
        dsems = {(qn_, i): es.enter_context(nc.semaphore(f"d_{qn_}{i}"))
                 for qn_ in ("sp", "pool") for i in range(NSEM_DMA)}
        with nc.Block() as block:
            S.emit(nc, block, sems, dsems)
    return nc


_NC_CACHE = {}


def _prep_inputs(inp, b, ntiles):
    T = ntiles * TT
    f = lambda a: np.ascontiguousarray(np.asarray(a, dtype=np.float32))
    mu_rkv = f(inp["mu_rkv"])[0]
    mu_lora = f(inp["mu_lora"])[0]
    vrows = np.stack([
        f(inp["norm_mix_g"])[0], mu_rkv[0:D], mu_rkv[D:2 * D], mu_rkv[2 * D:3 * D],
        mu_lora[0], mu_lora[1], mu_lora[2], f(inp["k_k"])[0], f(inp["k_a"])[0],
        f(inp["r_k"])[0].reshape(-1), f(inp["ln_x_g"])[0], f(inp["ln_x_b"])[0],
        f(inp["conv_b"])[0], f(inp["conv_ln_g"])[0], f(inp["conv_ln_b"])[0], f(inp["norm_ff_g"])[0]], 0)
    m = {
        "x": f(inp["x"][b, :T]),
        "w_in": f(inp["w_in"])[0], "w_rwkv_proj": f(inp["w_rwkv_proj"])[0],
        "w_conv_proj": f(inp["w_conv_proj"])[0], "w_out": f(inp["w_out"])[0],
        "w_ff1": f(inp["w_ff1"])[0], "w_ff2": f(inp["w_ff2"])[0],
        "vrows": f(vrows), "conv_w": f(inp["conv_w"])[0],
        "decay_w0": f(inp["decay_w0"]), "aaa_a0": f(inp["aaa_a0"]), "b_gate": f(inp["b_gate"]),
        "norm_final_g": f(inp["norm_final_g"]).reshape(1, D),
        "decay_w1": f(inp["decay_w1"])[0], "decay_w2": f(inp["decay_w2"])[0],
        "aaa_a1": f(inp["aaa_a1"])[0], "aaa_a2": f(inp["aaa_a2"])[0],
        "gate_g1": f(inp["gate_g1"])[0], "gate_g2": f(inp["gate_g2"])[0],
        "cmat": _consts_np(),
    }
    return m


def run(inputs, ntiles, cores):
    if ntiles not in _NC_CACHE:
        _NC_CACHE[ntiles] = build(ntiles)
    nc = _NC_CACHE[ntiles]
    in_maps = [_prep_inputs(inputs, b, ntiles) for b in range(cores)]
    res = run_bass_kernel_spmd(nc, in_maps, core_ids=list(range(cores)))
    return np.stack([np.asarray(r["out"], dtype=np.float32) for r in res.results], 0)


def kernel(**inputs):
    return run(inputs, T_FULL // TT, 8)
```

```python
import contextlib
import numpy as np
import concourse.bass as bass
import concourse.mybir as mybir
from concourse.bass_utils import run_bass_kernel_spmd

F32 = mybir.dt.float32
BF16 = mybir.dt.bfloat16
AF = mybir.ActivationFunctionType
ALU = mybir.AluOpType

D = 1024
T_FULL = 8192
NCH = 8
TT = 128
NCST = 22
(C_GMIX, C_MUR, C_MUK, C_MUV, C_MUW, C_MUA, C_MUG, C_KK, C_KA, C_RK, C_LNG, C_LNB,
 C_CB, C_CLG, C_CLB, C_GFF, C_OMR, C_OMK, C_OMV, C_OMKA, C_CLG2, C_CLB2) = range(22)
NSLAB = 18
NSEM_DMA = 6
import os as _os
_STAGE = int(_os.environ.get("KSTAGE", "99"))
_SKIP = _os.environ.get("KSKIP", "").split(",")
_NOCONV = int(_os.environ.get("KNOCONV", "0"))


class _Op:
    __slots__ = ("eng", "fn", "waits", "count", "dma", "sem", "semval", "prewait", "needed", "f32", "free")


class Sched:
    def __init__(self):
        self.ops = {k: [] for k in ("pe", "act", "dve", "pool", "sp")}
        self.last_w = {}
        self.readers = {}
        self.ndma = {"sp": 0, "pool": 0}

    def op(self, eng, fn, reads=(), writes=(), dma=False):
        o = _Op()
        o.eng, o.fn, o.dma = eng, fn, dma
        o.f32 = False
        o.free = False
        deps = []
        for r in reads:
            w = self.last_w.get(r)
            if w is not None:
                deps.append(w)
        for w_ in writes:
            w = self.last_w.get(w_)
            if w is not None:
                deps.append(w)
            deps.extend(self.readers.get(w_, ()))
        o.waits = deps
        self.ops[eng].append(o)
        o.count = len(self.ops[eng])
        o.prewait = None
        if dma:
            n = self.ndma[eng]
            self.ndma[eng] = n + 1
            o.sem = (eng, n % NSEM_DMA)
            o.semval = 16 * (n // NSEM_DMA + 1)
            if n >= NSEM_DMA:
                o.prewait = (o.sem, 16 * (n // NSEM_DMA))
        for w_ in writes:
            self.last_w[w_] = o
            self.readers[w_] = []
        for r in reads:
            self.readers.setdefault(r, []).append(o)
        return o

    def emit(self, nc, block, sems, dsems):
        engs = {"pe": block.tensor, "act": block.scalar, "dve": block.vector,
                "pool": block.gpsimd, "sp": block.sync}
        for name, ops in self.ops.items():
            for o in ops:
                o.needed = False
        for name, ops in self.ops.items():
            prev = None
            for o in ops:
                keep = []
                for d in o.waits:
                    if name == "pe" and o.free and (not d.dma) and d.eng == "pe":
                        continue
                    d.needed = True
                    keep.append(d)
                if name == "pe" and prev is not None and (o.f32 != prev.f32):
                    prev.needed = True
                    keep.append(prev)
                o.waits = keep
                prev = o
        for name, ops in self.ops.items():
            c = 0
            for o in ops:
                if o.needed and not o.dma:
                    c += 1
                o.count = c
        for name, deco in engs.items():
            ops = self.ops[name]

            def body(e, ops=ops, name=name):
                waited = {}
                for o in ops:
                    need = {}
                    for d in o.waits:
                        if d.dma:
                            key, val = ("d",) + d.sem, d.semval
                        else:
                            key, val = ("e", d.eng), d.count
                        if need.get(key, 0) < val:
                            need[key] = val
                    if o.prewait is not None:
                        key, val = ("d",) + o.prewait[0], o.prewait[1]
                        if need.get(key, 0) < val:
                            need[key] = val
                    for key, val in need.items():
                        if waited.get(key, 0) >= val:
                            continue
                        waited[key] = val
                        s = dsems[key[1:]] if key[0] == "d" else sems[key[1]]
                        e.wait_ge(s, val)
                    ins = o.fn(e)
                    if o.dma:
                        ins.then_inc(dsems[o.sem], 16)
                    elif o.needed:
                        ins.then_inc(sems[name], 1)
                if name == "sp":
                    last = {}
                    for o in ops:
                        if o.dma:
                            last[o.sem] = o.semval
                    for k, v in last.items():
                        e.wait_ge(dsems[k], v)

            deco(body)


def _consts_np():
    i = np.arange(128)
    s, t = i[:, None], i[None, :]
    c = {}
    c["ident"] = (s == t).astype(np.float32)
    c["tri"] = np.where(s <= t, -0.5 * np.exp(-0.5), 0.0).astype(np.float32)
    c["msu"] = (t > s).astype(np.float32)
    c["mui"] = (t >= s).astype(np.float32)
    c["msl"] = (t < s).astype(np.float32)
    c["bones"] = ((s // 64) == (t // 64)).astype(np.float32)
    return np.stack([c[k] for k in ("ident", "tri", "msu", "mui", "msl", "bones")], 0)


def build(ntiles):
    nc = bass.Bass("TRN2", target_bir_lowering=False)
    T = ntiles * TT
    dt_in = {}

    def din(name, shape):
        dt_in[name] = nc.dram_tensor(name, list(shape), F32, kind="ExternalInput").ap()
        return dt_in[name]

    x = din("x", [T, D])
    w_in = din("w_in", [D, 7 * D])
    w_a = din("w_rwkv_proj", [D, D])
    w_c = din("w_conv_proj", [D, D])
    w_o = din("w_out", [D, D])
    w_f1 = din("w_ff1", [D, 4 * D])
    w_f2 = din("w_ff2", [4 * D, D])
    vrows = din("vrows", [16, D])
    convw = din("conv_w", [31, D])
    w0row = din("decay_w0", [1, D])
    a0row = din("aaa_a0", [1, D])
    bgrow = din("b_gate", [1, 2 * D])
    gfin = din("norm_final_g", [1, D])
    dw1 = din("decay_w1", [D, 64])
    dw2 = din("decay_w2", [64, D])
    aa1 = din("aaa_a1", [D, 64])
    aa2 = din("aaa_a2", [64, D])
    gg1 = din("gate_g1", [D, 128])
    gg2 = din("gate_g2", [128, D])
    cmat = din("cmat", [6, 128, 128])
    out = nc.dram_tensor("out", [T, D], F32, kind="ExternalOutput").ap()
    wbf = nc.dram_tensor("wbf", [NSLAB, 128, NCH * D], BF16, kind="ExternalOutput").ap()

    S = Sched()
    es = contextlib.ExitStack()
    with es:
        def sb(name, shape, dt):
            return es.enter_context(nc.sbuf_tensor(name, list(shape), dt))

        def ps(name, shape, dt=F32):
            return es.enter_context(nc.psum_tensor(name, list(shape), dt))

        Fs = [sb(f"F{i}", [128, NCH, TT], F32) for i in range(12) if i not in (4, 5, 8)]
        Fs = {i: t for i, t in zip([i for i in range(12) if i not in (4, 5, 8)], Fs)}
        Hs = [sb(f"H{i}", [128, NCH, TT], BF16) for i in range(16)]
        Ms = {k: sb(k, [128, 16, TT], BF16) for k in
              ("Mka", "Mbr", "Mkr", "YA", "YTA", "YB", "YTB", "PA")}
        ring = [sb(f"ring{i}", [128, NCH, D], BF16) for i in range(3)]
        Qs = [ps(f"Q{i}", [128, NCH, TT]) for i in range(4)]
        x_tm = sb("x_tm", [128, D], F32)
        h_ext = sb("h_ext", [128, NCH, TT + 1], BF16)
        u_ext = sb("u_ext", [128, NCH, TT + 30], BF16)
        lora_sb = sb("lora_sb", [128, TT], BF16)
        sg_sb = sb("sg_sb", [128, TT], BF16)
        tg_sb = sb("tg_sb", [128, TT], F32)
        r2 = sb("r2", [128, 32, TT], BF16)
        S_f = sb("S_f", [128, NCH, 64], F32)
        S_b = sb("S_b", [128, NCH, 64], BF16)
        halo = sb("halo", [128, 3, NCH, 1], F32)
        st = sb("st", [128, 8], F32)
        lnt = [sb(f"lnt{i}", [128, TT], F32) for i in range(4)]
        cst = sb("cst", [128, NCH, NCST], F32)
        cw = sb("cw", [128, NCH, 31], F32)
        cm_f = sb("cm_f", [128, 6, 128], F32)
        ident_b = sb("ident_b", [128, 128], BF16)
        bones_b = sb("bones_b", [128, 128], BF16)
        bones64_b = sb("bones64_b", [128, 128], BF16)
        onesdiv = sb("onesdiv", [128, 128], F32)
        gfin_bc = sb("gfin_bc", [128, D], F32)
        w0_row = sb("w0_row", [1, D], F32)
        a0_row = sb("a0_row", [1, D], BF16)
        bg_row = sb("bg_row", [1, 2 * D], BF16)
        ones_f = sb("ones_f", [1, 128], F32)
        ones_b = sb("ones_b", [1, 128], BF16)
        wa1 = sb("wa1", [128, NCH, 128], BF16)
        wa1mu = sb("wa1mu", [128, NCH, 128], BF16)
        g1 = sb("g1", [128, NCH, 128], BF16)
        g1mu = sb("g1mu", [128, NCH, 128], BF16)
        l2 = sb("l2", [128, D], BF16)
        g2 = sb("g2", [128, D], BF16)

        rows = Fs[0][:, :, :].rearrange("p c t -> p (c t)")
        hT = Hs[9][:, :, :].rearrange("p c t -> p (c t)")
        junk = Hs[15][:, :, :].rearrange("p c t -> p (c t)")
        ident_f = cm_f[:, 0, :]
        tri_f = cm_f[:, 1, :]
        msu, mui, msl = cm_f[:, 2, :], cm_f[:, 3, :], cm_f[:, 4, :]

        qi = [0]
        in_setup = [True]

        def nextQ():
            q = Qs[qi[0] % 4]
            qi[0] += 1
            return q, f"Q{(qi[0] - 1) % 4}"

        def cb(idx, n=TT):
            return cst[:, :, idx:idx + 1].to_broadcast([128, NCH, n])

        def mm(out_ap, lhsT, rhs, start, stop, reads, writes, free=False):
            if "pesetup" in _SKIP and in_setup[0]:
                return
            o_ = S.op("pe", lambda e: e.matmul(out_ap, lhsT, rhs, start=start, stop=stop),
                       reads=reads, writes=writes)
            o_.f32 = (lhsT.dtype == F32)
            o_.free = free

        def act(out_ap, in_ap, func, reads, writes, scale=1.0, bias=0.0, accum=None):
            if accum is None:
                S.op("act", lambda e: e.activation(out=out_ap, in_=in_ap, func=func,
                                                   bias=bias, scale=scale),
                     reads=reads, writes=writes)
            else:
                S.op("act", lambda e: e.activation(out=out_ap, in_=in_ap, func=func,
                                                   bias=bias, scale=scale, accum_out=accum),
                     reads=reads, writes=writes)

        def tt(eng, out_ap, a, b, op, reads, writes):
            S.op(eng, lambda e: e.tensor_tensor(out=out_ap, in0=a, in1=b, op=op),
                 reads=reads, writes=writes)

        def ts(eng, out_ap, a, s1, s2, op0, op1, reads, writes):
            if op1 is None:
                S.op(eng, lambda e: e.tensor_scalar(out=out_ap, in0=a, scalar1=s1, scalar2=None,
                                                    op0=op0), reads=reads, writes=writes)
            else:
                S.op(eng, lambda e: e.tensor_scalar(out=out_ap, in0=a, scalar1=s1, scalar2=s2,
                                                    op0=op0, op1=op1), reads=reads, writes=writes)

        def stt(out_ap, a, sc, b, op0, op1, reads, writes):
            S.op("dve", lambda e: e.scalar_tensor_tensor(out=out_ap, in0=a, scalar=sc, in1=b,
                                                         op0=op0, op1=op1),
                 reads=reads, writes=writes)

        def cp(eng, out_ap, in_ap, reads, writes):
            if eng == "act":
                act(out_ap, in_ap, AF.Copy, reads, writes)
            else:
                S.op(eng, lambda e: e.tensor_copy(out=out_ap, in_=in_ap), reads=reads, writes=writes)

        stg = {"n": 0}

        def dma(q, out_ap, in_ap, reads, writes, part=None, cols=None):
            if q == "pool":
                i = stg["n"] % 2
                stg["n"] += 1
                sf = flat(Fs[10 + i])
                p0, p1 = part
                if len(cols) == 2:
                    sview = sf[p0:p1, 0:cols[0] * cols[1]].rearrange("p (a b) -> p a b", a=cols[0])
                else:
                    sview = sf[p0:p1, 0:cols[0]]
                dma("sp", sview, in_ap, [], [f"F{10 + i}"])
                if out_ap.tensor.name.startswith("wbf"):
                    hv = flat(Hs[10 + i])[p0:p1, 0:cols[0]]
                    cp(("act", "dve")[i], hv, sview, [f"F{10 + i}"], [f"H{10 + i}"])
                    dma("sp", out_ap, hv, [f"H{10 + i}"], writes)
                else:
                    cp(("act", "dve")[i], out_ap, sview, [f"F{10 + i}"], writes)
                return
            S.op(q, lambda e: e.dma_start(out=out_ap, in_=in_ap), reads=reads, writes=writes, dma=True)

        def flat(tile3):
            return tile3[:, :, :].rearrange("p c t -> p (c t)")

        def slab_src(si):
            if si < 7:
                order = [0, 1, 2, 4, 3, 5, 6]
                g = order[si]
                return w_in[:, g * D:(g + 1) * D]
            if si == 7:
                return w_a
            if si == 8:
                return w_c
            if si == 9:
                return w_o
            if si < 14:
                g = si - 10
                return w_f1[:, g * D:(g + 1) * D]
            g = si - 14
            return w_f2[g * D:(g + 1) * D, :]

        for si in range(NSLAB if not _NOCONV else 0):
            src = slab_src(si).rearrange("(kc p) n -> p kc n", p=128)
            dst = wbf[si].rearrange("p (kc n) -> p kc n", kc=NCH)
            for kc in range(NCH):
                dma("pool", dst[:, kc, :], src[:, kc, :], reads=[], writes=[f"wbf{si}_{kc}"], part=(0, 128), cols=(D,))

        dma("pool", wa1[:, :, 0:64], dw1.rearrange("(kc p) n -> p kc n", p=128), [], ["wa1"], part=(0, 128), cols=(NCH, 64))
        dma("pool", wa1[:, :, 64:128], aa1.rearrange("(kc p) n -> p kc n", p=128), [], ["wa1"], part=(0, 128), cols=(NCH, 64))
        dma("pool", g1[:, :, :], gg1.rearrange("(kc p) n -> p kc n", p=128), [], ["g1"], part=(0, 128), cols=(NCH, 128))
        dma("pool", l2[0:64, :], dw2, [], ["l2"], part=(0, 64), cols=(D,))
        dma("pool", l2[64:128, :], aa2, [], ["l2"], part=(64, 128), cols=(D,))
        dma("pool", g2[:, :], gg2, [], ["g2"], part=(0, 128), cols=(D,))
        dma("pool", a0_row[:, :], a0row, [], ["a0_row"], part=(0, 1), cols=(D,))
        dma("pool", bg_row[:, 0:D], bgrow[:, 0:D], [], ["bg_row"], part=(0, 1), cols=(D,))
        dma("pool", bg_row[:, D:2 * D], bgrow[:, D:2 * D], [], ["bg_row"], part=(0, 1), cols=(D,))
        dma("sp", w0_row[:, :], w0row, [], ["w0_row"])
        dma("sp", cm_f[:, :, :], cmat.rearrange("k p n -> p k n"), [], ["cm_f"])
        dma("sp", rows[0:1, :], gfin, ["F0"], ["F0"])

        cp("dve", ident_b[:, :], ident_f, ["cm_f"], ["ident_b"])
        cp("dve", bones_b[:, :], cm_f[:, 5, :], ["cm_f"], ["bones_b"])
        ts("dve", bones64_b[:, :], cm_f[:, 5, :], 1.0 / 64, None, ALU.mult, None, ["cm_f"], ["bones64_b"])
        S.op("pool", lambda e: e.memset(onesdiv[:, :], 1.0 / D), writes=["onesdiv"])
        S.op("pool", lambda e: e.memset(ones_b[:, :], 1.0), writes=["ones_b"])
        S.op("pool", lambda e: e.memset(h_ext[:, :, :], 0.0), writes=["h_ext"])
        S.op("pool", lambda e: e.memset(u_ext[:, :, :], 0.0), writes=["u_ext"])
        S.op("pool", lambda e: e.memset(halo[:, :, :, :], 0.0), writes=["halo"])
        S.op("pool", lambda e: e.memset(S_f[:, :, :], 0.0), writes=["S_f"])
        S.op("pool", lambda e: e.memset(S_b[:, :, :], 0.0), writes=["S_b"])

        S.op("pool", lambda e: e.memset(ones_f[:, :], 1.0), writes=["ones_f"])
        q, qn = nextQ()
        qf = flat(q)
        for hf in range(2):
            mm(qf[:, hf * 512:(hf + 1) * 512], ones_f[0:1, :], rows[0:1, hf * 512:(hf + 1) * 512], True, True,
               ["ones_f", "F0"], [qn])
        cp("dve", gfin_bc[:, :], qf, [qn], ["gfin_bc"])
        dma("sp", rows[0:16, :], vrows, ["F0"], ["F0"])
        q, qn = nextQ()
        qf = flat(q)
        for c in range(NCH):
            mm(qf[:, c * 32:c * 32 + 16], rows[0:16, c * 128:(c + 1) * 128], cm_f[0:16, 0, 0:16],
               True, True, ["F0", "cm_f"], [qn])
        cp("dve", cst[:, :, 0:16], qf[:, 0:256].rearrange("p (c k) -> p c k", c=NCH)[:, :, 0:16],
           [qn], ["cst"])
        for (dst_i, src_i) in ((C_OMR, C_MUR), (C_OMK, C_MUK), (C_OMV, C_MUV), (C_OMKA, C_KA)):
            ts("dve", cst[:, :, dst_i:dst_i + 1], cst[:, :, src_i:src_i + 1], -1.0, 1.0,
               ALU.mult, ALU.add, ["cst"], ["cst"])
        for (dst_i, src_i) in ((C_CLG2, C_CLG), (C_CLB2, C_CLB)):
            ts("dve", cst[:, :, dst_i:dst_i + 1], cst[:, :, src_i:src_i + 1], 0.5, None,
               ALU.mult, None, ["cst"], ["cst"])
        dma("sp", rows[0:31, :], convw, ["F0"], ["F0"])
        q, qn = nextQ()
        qf = flat(q)
        for c in range(NCH):
            mm(qf[:, c * 32:c * 32 + 31], rows[0:31, c * 128:(c + 1) * 128], cm_f[0:31, 0, 0:31],
               True, True, ["F0", "cm_f"], [qn])
        ts("dve", cw[:, :, :], qf[:, 0:256].rearrange("p (c k) -> p c k", c=NCH)[:, :, 0:31],
           0.5, None, ALU.mult, None, [qn], ["cw"])
        tt("dve", wa1mu[:, :, 0:64], wa1[:, :, 0:64],
           cst[:, :, C_MUW:C_MUW + 1].to_broadcast([128, NCH, 64]), ALU.mult, ["wa1", "cst"], ["wa1mu"])
        tt("dve", wa1mu[:, :, 64:128], wa1[:, :, 64:128],
           cst[:, :, C_MUA:C_MUA + 1].to_broadcast([128, NCH, 64]), ALU.mult, ["wa1", "cst"], ["wa1mu"])
        tt("dve", g1mu[:, :, :], g1[:, :, :],
           cst[:, :, C_MUG:C_MUG + 1].to_broadcast([128, NCH, 128]), ALU.mult, ["g1", "cst"], ["g1mu"])

        gs = {"issued": 0, "used": 0}
        total_slabs = NSLAB * ntiles

        def issue_to(n):
            while gs["issued"] < min(n, total_slabs):
                g = gs["issued"]
                si, ri = g % NSLAB, g % 3
                dma("sp", ring[ri][:, :, :].rearrange("p c n -> p (c n)"), wbf[si],
                    [f"wbf{si}_{kc}" for kc in range(NCH)], [f"ring{ri}"])
                gs["issued"] += 1

        def use_slab():
            g = gs["used"]
            issue_to(g + 2)
            gs["used"] += 1
            return ring[g % 3], f"ring{g % 3}"

        def after_slab():
            issue_to(gs["used"] + 2)

        def rstd_from_ss(col, eps):
            ts("dve", st[:, col:col + 1], st[:, col:col + 1], 1.0 / D, eps, ALU.mult, ALU.add, ["st"], ["st"])
            act(st[:, col:col + 1], st[:, col:col + 1], AF.Sqrt, ["st"], ["st"])
            S.op("dve", lambda e: e.reciprocal(out=st[:, col:col + 1], in_=st[:, col:col + 1]),
                 reads=["st"], writes=["st"])

        def norm_to_fm(gidx, dst_ap, dst_key, col):
            S.op("pool", lambda e: e.memset(st[:, col:col + 1], 0.0), writes=["st"])
            act(junk[:, :], x_tm[:, :], AF.Square, ["x_tm"], ["H15", "st"], accum=st[:, col:col + 1])
            rstd_from_ss(col, 1e-6)
            act(hT[:, :], x_tm[:, :], AF.Identity, ["x_tm", "st"], ["H9"], scale=st[:, col:col + 1])
            q, qn = nextQ()
            for c in range(NCH):
                mm(q[:, c, :], hT[:, c * 128:(c + 1) * 128], ident_b[:, :], True, True,
                   ["H9", "ident_b"], [qn])
            tt("dve", dst_ap, q[:, :, :], cb(gidx), ALU.mult, [qn, "cst"], [dst_key])

        F = Fs
        H = Hs

        def fm_proj(q, qn, slab, slabn, rhs_fn, rhs_keys, bias_row=None, bias_off=0, bias_key=None):
            for mc in range(NCH):
                for kc in range(NCH):
                    mm(q[:, mc, :], slab[:, kc, mc * 128:(mc + 1) * 128], rhs_fn(kc),
                       kc == 0, (kc == NCH - 1) and bias_row is None, [slabn] + rhs_keys, [qn], free=True)
                if bias_row is not None:
                    mm(q[:, mc, :], bias_row[0:1, bias_off + mc * 128:bias_off + (mc + 1) * 128],
                       ones_b[0:1, :], False, True, [bias_key, "ones_b"], [qn])

        in_setup[0] = False
        for it in range(ntiles):
            if it > 0:
                cp("pool", h_ext[:, :, 0:1], h_ext[:, :, TT:TT + 1], ["h_ext"], ["h_ext"])
                cp("pool", u_ext[:, :, 0:30], u_ext[:, :, TT:TT + 30], ["u_ext"], ["u_ext"])
            dma("sp", x_tm[:, :], x[it * TT:(it + 1) * TT, :], [], ["x_tm"])
            if _STAGE == 0:
                dma("sp", out[it * TT:(it + 1) * TT, :], x_tm[:, :], ["x_tm"], ["out"])
                continue
            norm_to_fm(C_GMIX, h_ext[:, :, 1:TT + 1], "h_ext", 0)
            dh = H[0]
            tt("dve", dh[:, :, :], h_ext[:, :, 0:TT], h_ext[:, :, 1:TT + 1], ALU.subtract, ["h_ext"], ["H0"])
            hc = lambda kc: h_ext[:, kc, 1:TT + 1]

            q, qn = nextQ()
            qf = flat(q)
            for kc in range(NCH):
                mm(qf[:, 0:128], wa1[:, kc, :], hc(kc), kc == 0, False, ["wa1", "h_ext"], [qn])
            for kc in range(NCH):
                mm(qf[:, 0:128], wa1mu[:, kc, :], dh[:, kc, :], False, kc == NCH - 1, ["wa1mu", "H0"], [qn])
            for kc in range(NCH):
                mm(qf[:, 128:256], g1[:, kc, :], hc(kc), kc == 0, False, ["g1", "h_ext"], [qn])
            for kc in range(NCH):
                mm(qf[:, 128:256], g1mu[:, kc, :], dh[:, kc, :], False, kc == NCH - 1, ["g1mu", "H0"], [qn])
            act(lora_sb[0:64, :], qf[0:64, 0:128], AF.Tanh, [qn], ["lora_sb"])
            act(lora_sb[64:128, :], qf[64:128, 0:128], AF.Copy, [qn], ["lora_sb"])
            act(tg_sb[:, :], qf[:, 128:256], AF.Tanh, [qn], ["tg_sb"], scale=0.5)
            ts("dve", sg_sb[:, :], tg_sb[:, :], 0.5, 0.5, ALU.mult, ALU.add, ["tg_sb"], ["sg_sb"])

            q, qn = nextQ()
            qf = flat(q)
            for hf in range(2):
                mm(qf[:, hf * 512:(hf + 1) * 512], lora_sb[0:64, :], l2[0:64, hf * 512:(hf + 1) * 512],
                   True, False, ["lora_sb", "l2"], [qn])
                mm(qf[:, hf * 512:(hf + 1) * 512], ones_f[0:1, :], w0_row[0:1, hf * 512:(hf + 1) * 512],
                   False, True, ["ones_f", "w0_row"], [qn])
            s_tm = flat(F[0])
            act(s_tm, qf, AF.Tanh, [qn], ["F0"], scale=0.5)
            ts("dve", s_tm, s_tm, 1.0, None, ALU.add, None, ["F0"], ["F0"])
            q, qn = nextQ()
            for c in range(NCH):
                mm(q[:, c, :], s_tm[:, c * 128:(c + 1) * 128], tri_f, True, True, ["F0", "cm_f"], [qn])
            E1, E3 = F[1], F[2]
            act(E1[:, :, :], q[:, :, :], AF.Exp, [qn], ["F1"])
            act(E3[:, :, :], q[:, :, :], AF.Exp, [qn], ["F2"], scale=-1.0)

            q, qn = nextQ()
            for c in range(NCH):
                mm(q[:, c, :], l2[64:128, c * 128:(c + 1) * 128], lora_sb[64:128, :], True, False,
                   ["l2", "lora_sb"], [qn])
                mm(q[:, c, :], a0_row[0:1, c * 128:(c + 1) * 128], ones_b[0:1, :], False, True,
                   ["a0_row", "ones_b"], [qn])
            a_t = F[3]
            act(a_t[:, :, :], q[:, :, :], AF.Tanh, [qn], ["F3"], scale=0.5)
            ts("dve", a_t[:, :, :], a_t[:, :, :], 0.5, 0.5, ALU.mult, ALU.add, ["F3"], ["F3"])
            q, qn = nextQ()
            for c in range(NCH):
                mm(q[:, c, :], g2[:, c * 128:(c + 1) * 128], sg_sb[:, :], True, True, ["g2", "sg_sb"], [qn])
            g_sb = H[2]
            cp("act", g_sb[:, :, :], q[:, :, :], [qn], ["H2"])

            rkv = [F[6], F[7], H[3]]
            for j in range(3):
                slab, slabn = use_slab()
                q, qn = nextQ()
                fm_proj(q, qn, slab, slabn, hc, ["h_ext"])
                after_slab()
                tmp, t2 = F[0], F[9]
                tt("dve", tmp[:, :, :], q[:, :, :], cb(C_OMR + j), ALU.mult, [qn, "cst"], ["F0"])
                tt("dve", t2[:, :, 1:TT], q[:, :, 0:TT - 1], cb(C_MUR + j, TT - 1), ALU.mult, [qn, "cst"], ["F9"])
                tt("dve", t2[:, :, 0:1], halo[:, j, :, :], cst[:, :, C_MUR + j:C_MUR + j + 1], ALU.mult,
                   ["halo", "cst"], ["F9"])
                cp("act", halo[:, j, :, :], q[:, :, TT - 1:TT], [qn], ["halo"])
                tt("pool", rkv[j][:, :, :], tmp[:, :, :], t2[:, :, :], ALU.add, ["F0", "F9"], [("F6", "F7", "H3")[j]])
            r_t, k_t, v_t = rkv
            v_bf = v_t

            slab, slabn = use_slab()
            q, qn = nextQ()
            fm_proj(q, qn, slab, slabn, hc, ["h_ext"])
            after_slab()
            tb = H[4]
            act(tb[:, :, :], q[:, :, :], AF.Tanh, [qn], ["H4"], scale=0.5)
            slab, slabn = use_slab()
            q, qn = nextQ()
            fm_proj(q, qn, slab, slabn, hc, ["h_ext"])
            after_slab()
            stt(u_ext[:, :, 30:30 + TT], tb[:, :, :], 1.0, q[:, :, :], ALU.add, ALU.mult, ["H4", qn], ["u_ext"])
            tG = [H[5], H[6]]
            for j in range(2):
                slab, slabn = use_slab()
                q, qn = nextQ()
                fm_proj(q, qn, slab, slabn, hc, ["h_ext"], bias_row=bg_row, bias_off=j * D, bias_key="bg_row")
                after_slab()
                act(tG[j][:, :, :], q[:, :, :], AF.Tanh, [qn], [f"H{5 + j}"], scale=0.5)

            acc, prod = F[9], F[10]
            for j in range(31):
                wj = cw[:, :, j:j + 1].to_broadcast([128, NCH, TT])
                if j == 0:
                    tt("pool", acc[:, :, :], u_ext[:, :, 0:TT], wj, ALU.mult, ["u_ext", "cw"], ["F9"])
                else:
                    tt("pool", prod[:, :, :], u_ext[:, :, j:j + TT], wj, ALU.mult, ["u_ext", "cw"], ["F10"])
                    tt("pool", acc[:, :, :], acc[:, :, :], prod[:, :, :], ALU.add, ["F9", "F10"], ["F9"])
            tt("pool", acc[:, :, :], acc[:, :, :], cb(C_CB), ALU.add, ["F9", "cst"], ["F9"])
            sq = F[10]
            act(sq[:, :, :], acc[:, :, :], AF.Square, ["F9"], ["F10"])
            q, qn = nextQ()
            qf = flat(q)
            for c in range(NCH):
                mm(qf[:, 0:128], onesdiv[:, :], acc[:, c, :], c == 0, c == NCH - 1, ["onesdiv", "F9"], [qn])
            for c in range(NCH):
                mm(qf[:, 128:256], onesdiv[:, :], sq[:, c, :], c == 0, c == NCH - 1, ["onesdiv", "F10"], [qn])
            mean_s, nm2, var_s = lnt[0], lnt[1], lnt[2]
            cp("act", mean_s[:, :], qf[:, 0:128], [qn], ["lnt0"])
            stt(nm2[:, :], mean_s[:, :], -1.0, mean_s[:, :], ALU.mult, ALU.mult, ["lnt0"], ["lnt1"])
            stt(var_s[:, :], qf[:, 128:256], 1e-5, nm2[:, :], ALU.add, ALU.add, [qn, "lnt1"], ["lnt2"])
            act(var_s[:, :], var_s[:, :], AF.Sqrt, ["lnt2"], ["lnt2"])
            S.op("dve", lambda e: e.reciprocal(out=var_s[:, :], in_=var_s[:, :]), reads=["lnt2"], writes=["lnt2"])
            tt("dve", acc[:, :, :], acc[:, :, :], mean_s[:, :].unsqueeze(1).to_broadcast([128, NCH, TT]),
               ALU.subtract, ["F9", "lnt0"], ["F9"])
            tt("dve", acc[:, :, :], acc[:, :, :], var_s[:, :].unsqueeze(1).to_broadcast([128, NCH, TT]),
               ALU.mult, ["F9", "lnt2"], ["F9"])
            tt("dve", acc[:, :, :], acc[:, :, :], cb(C_CLG2), ALU.mult, ["F9", "cst"], ["F9"])
            tt("dve", acc[:, :, :], acc[:, :, :], cb(C_CLB2), ALU.add, ["F9", "cst"], ["F9"])
            th = H[7]
            act(th[:, :, :], acc[:, :, :], AF.Tanh, ["F9"], ["H7"])
            C_fm = H[8]
            stt(C_fm[:, :, :], th[:, :, :], 1.0, acc[:, :, :], ALU.add, ALU.mult, ["H7", "F9"], ["H8"])

            kkr, ksq = F[0], H[9]
            tt("dve", kkr[:, :, :], k_t[:, :, :], cb(C_KK), ALU.mult, ["F7", "cst"], ["F0"])
            act(ksq[:, :, :], kkr[:, :, :], AF.Square, ["F0"], ["H9"])
            q, qn = nextQ()
            for c in range(NCH):
                mm(q[:, c, :], bones_b[:, :], ksq[:, c, :], True, True, ["bones_b", "H9"], [qn])
            rn = F[9]
            act(rn[:, :, :], q[:, :, :], AF.Sqrt, [qn], ["F9"], bias=1e-24)
            S.op("dve", lambda e: e.reciprocal(out=rn[:, :, :], in_=rn[:, :, :]), reads=["F9"], writes=["F9"])
            kkn = F[0]
            tt("dve", kkn[:, :, :], kkr[:, :, :], rn[:, :, :], ALU.mult, ["F0", "F9"], ["F0"])
            t1 = F[9]
            tt("dve", t1[:, :, :], a_t[:, :, :], cb(C_KA), ALU.mult, ["F3", "cst"], ["F9"])
            tt("pool", t1[:, :, :], t1[:, :, :], cb(C_OMKA), ALU.add, ["F9", "cst"], ["F9"])
            kmod = F[10]
            tt("pool", kmod[:, :, :], k_t[:, :, :], t1[:, :, :], ALU.mult, ["F7", "F9"], ["F10"])
            beta = F[11]
            tt("dve", beta[:, :, :], kkn[:, :, :], a_t[:, :, :], ALU.mult, ["F0", "F3"], ["F11"])
            Rt, Bt, Kt, At, Bh, Kh = H[10], H[11], H[12], H[13], H[14], H[15]
            tt("dve", Rt[:, :, :], r_t[:, :, :], E1[:, :, :], ALU.mult, ["F6", "F1"], ["H10"])
            tt("dve", Bt[:, :, :], beta[:, :, :], E3[:, :, :], ALU.mult, ["F11", "F2"], ["H11"])
            tt("pool", Kt[:, :, :], kmod[:, :, :], E3[:, :, :], ALU.mult, ["F10", "F2"], ["H12"])
            stt(At[:, :, 1:TT], kkn[:, :, 1:TT], -1.0, E1[:, :, 0:TT - 1], ALU.mult, ALU.mult, ["F0", "F1"], ["H13"])
            ts("dve", At[:, :, 0:1], kkn[:, :, 0:1], -1.0, None, ALU.mult, None, ["F0"], ["H13"])
            wcb = E1[:, :, TT - 1:TT].to_broadcast([128, NCH, TT])
            tt("pool", Bh[:, :, :], Bt[:, :, :], wcb, ALU.mult, ["H11", "F1"], ["H14"])
            tt("pool", Kh[:, :, :], Kt[:, :, :], wcb, ALU.mult, ["H12", "F1"], ["H15"])
            rk = H[9]
            tt("dve", F[9][:, :, :], r_t[:, :, :], cb(C_RK), ALU.mult, ["F6", "cst"], ["F9"])
            tt("dve", rk[:, :, :], F[9][:, :, :], kmod[:, :, :], ALU.mult, ["F9", "F10"], ["H9"])
            q, qn = nextQ()
            for c in range(NCH):
                mm(q[:, c, :], bones_b[:, :], rk[:, c, :], True, True, ["bones_b", "H9"], [qn])
            bv = F[9]
            tt("dve", bv[:, :, :], q[:, :, :], v_t[:, :, :], ALU.mult, [qn, "H3"], ["F9"])
            V_tm, Bh_tm, Kh_tm = H[0], H[1], H[4]
            for (src, srck, dst, dstk, eng) in ((v_bf, "H3", V_tm, "H0", "act"), (Bh, "H14", Bh_tm, "H1", "act"),
                                               (Kh, "H15", Kh_tm, "H4", "dve")):
                q, qn = nextQ()
                for c in range(NCH):
                    mm(q[:, c, :], src[:, c, :], ident_b[:, :], True, True, [srck, "ident_b"], [qn])
                cp(eng, dst[:, :, :], q[:, :, :], [qn], [dstk])
            Vf, Bhf, Khf = flat(V_tm), flat(Bh_tm), flat(Kh_tm)

            def hs(h):
                return (h % 2) * 64, h // 2

            def score(lh, lk, rh_, rk_, mask, dst, dstk, eng):
                for half in range(2):
                    q, qn = nextQ()
                    for hh in range(8):
                        h = half * 8 + hh
                        rb, c = hs(h)
                        mm(q[:, hh, :], lh[rb:rb + 64, c, :], rh_[rb:rb + 64, c, :], True, True, [lk, rk_], [qn])
                    tt(eng, dst[:, half * 8:(half + 1) * 8, :], q[:, :, :],
                       mask.unsqueeze(1).to_broadcast([128, 8, TT]), ALU.mult, [qn, "cm_f"], [dstk])

            score(Bt, "H11", At, "H13", msu, Ms["YA"], "YA", "dve")
            score(At, "H13", Bt, "H11", msl, Ms["YTA"], "YTA", "dve")
            score(Kt, "H12", At, "H13", msu, Ms["Mka"], "Mka", "dve")
            score(Bt, "H11", Rt, "H10", mui, Ms["Mbr"], "Mbr", "dve")
            score(Kt, "H12", Rt, "H10", mui, Ms["Mkr"], "Mkr", "dve")

            Y, YT, Yn, YTn = "YA", "YTA", "YB", "YTB"
            P, Pn = "PA", "PA"
            tt("pool", Ms[P][:, :, :], Ms[Y][:, :, :], ident_b[:, :].unsqueeze(1).to_broadcast([128, 16, TT]),
               ALU.add, [Y, "ident_b"], [P])
            for lvl in range(1, 7):
                for half in range(2):
                    q, qn = nextQ()
                    for hh in range(8):
                        h = half * 8 + hh
                        mm(q[:, hh, :], Ms[Y][:, h, :], Ms[YT][:, h, :], True, True, [Y, YT], [qn])
                    cp("act", Ms[YTn][:, half * 8:(half + 1) * 8, :], q[:, :, :], [qn], [YTn])
                if lvl < 6:
                    for half in range(2):
                        q, qn = nextQ()
                        for hh in range(8):
                            h = half * 8 + hh
                            mm(q[:, hh, :], Ms[YT][:, h, :], Ms[Y][:, h, :], True, True, [Y, YT], [qn])
                        cp("act", Ms[Yn][:, half * 8:(half + 1) * 8, :], q[:, :, :], [qn], [Yn])
                for half in range(2):
                    q, qn = nextQ()
                    for hh in range(8):
                        h = half * 8 + hh
                        mm(q[:, hh, :], Ms[YTn][:, h, :], Ms[P][:, h, :], True, True, [YTn, P], [qn])
                    tt("dve", Ms[Pn][:, half * 8:(half + 1) * 8, :], q[:, :, :],
                       Ms[P][:, half * 8:(half + 1) * 8, :], ALU.add, [qn, P], [Pn])
                Y, Yn = Yn, Y
                YT, YTn = YTn, YT
                P, Pn = Pn, P
            Tm, Tk = Ms[P], P

            X_bf, U_bf = H[7], H[9]
            Xf, Uf = flat(X_bf), flat(U_bf)
            q, qn = nextQ()
            qf = flat(q)
            for h in range(16):
                rb, c = hs(h)
                mm(qf[:, h * 64:(h + 1) * 64], At[rb:rb + 64, c, :], S_b[rb:rb + 64, c, :], True, False,
                   ["H13", "S_b"], [qn])
                mm(qf[:, h * 64:(h + 1) * 64], Ms["Mka"][:, h, :], Vf[:, h * 64:(h + 1) * 64], False, True,
                   ["Mka", "H0"], [qn])
            cp("act", Xf, qf, [qn], ["H7"])
            q, qn = nextQ()
            qf = flat(q)
            for h in range(16):
                mm(qf[:, h * 64:(h + 1) * 64], Tm[:, h, :], Xf[:, h * 64:(h + 1) * 64], True, True,
                   [Tk, "H7"], [qn])
            cp("act", Uf, qf, [qn], ["H9"])
            qo, qon = nextQ()
            for h in range(16):
                rb, c = hs(h)
                mm(qo[rb:rb + 64, c, :], S_b[rb:rb + 64, c, :], Rt[rb:rb + 64, c, :], True, False,
                   ["S_b", "H10"], [qon])
                mm(qo[rb:rb + 64, c, :], Uf[:, h * 64:(h + 1) * 64], Ms["Mbr"][:, h, :], False, False,
                   ["H9", "Mbr"], [qon])
                mm(qo[rb:rb + 64, c, :], Vf[:, h * 64:(h + 1) * 64], Ms["Mkr"][:, h, :], False, True,
                   ["H0", "Mkr"], [qon])
            qd, qdn = nextQ()
            qdf = flat(qd)[:, 0:512].rearrange("p (c v) -> p c v", c=NCH)
            for h in range(16):
                rb, c = hs(h)
                mm(qdf[rb:rb + 64, c, :], Bhf[:, c * 128 + rb:c * 128 + rb + 64], Uf[:, h * 64:(h + 1) * 64],
                   True, False, ["H1", "H9"], [qdn])
                mm(qdf[rb:rb + 64, c, :], Khf[:, c * 128 + rb:c * 128 + rb + 64], Vf[:, h * 64:(h + 1) * 64],
                   False, True, ["H4", "H0"], [qdn])
            tt("dve", S_f[:, :, :], S_f[:, :, :], E1[:, :, TT - 1:TT].to_broadcast([128, NCH, 64]), ALU.mult,
               ["S_f", "F1"], ["S_f"])
            tt("dve", S_f[:, :, :], qdf, S_f[:, :, :], ALU.add, [qdn, "S_f"], ["S_f"])
            cp("act", S_b[:, :, :], S_f[:, :, :], ["S_f"], ["S_b"])

            o_sb, osq = H[10], H[11]
            cp("act", o_sb[:, :, :], qo[:, :, :], [qon], ["H10"])
            act(osq[:, :, :], qo[:, :, :], AF.Square, [qon], ["H11"])
            qm, qmn = nextQ()
            for c in range(NCH):
                mm(qm[:, c, :], bones64_b[:, :], o_sb[:, c, :], True, True, ["bones64_b", "H10"], [qmn])
            qq, qqn = nextQ()
            for c in range(NCH):
                mm(qq[:, c, :], bones64_b[:, :], osq[:, c, :], True, True, ["bones64_b", "H11"], [qqn])
            gm, gv, gd = F[0], F[10], F[11]
            cp("act", gm[:, :, :], qm[:, :, :], [qmn], ["F0"])
            stt(gv[:, :, :], gm[:, :, :], -1.0, gm[:, :, :], ALU.mult, ALU.mult, ["F0"], ["F10"])
            stt(gv[:, :, :], qq[:, :, :], 64e-5, gv[:, :, :], ALU.add, ALU.add, [qqn, "F10"], ["F10"])
            act(gv[:, :, :], gv[:, :, :], AF.Sqrt, ["F10"], ["F10"])
            S.op("dve", lambda e: e.reciprocal(out=gv[:, :, :], in_=gv[:, :, :]), reads=["F10"], writes=["F10"])
            tt("dve", gd[:, :, :], qo[:, :, :], gm[:, :, :], ALU.subtract, [qon, "F0"], ["F11"])
            tt("dve", gd[:, :, :], gd[:, :, :], gv[:, :, :], ALU.mult, ["F11", "F10"], ["F11"])
            tt("pool", gd[:, :, :], gd[:, :, :], cb(C_LNG), ALU.mult, ["F11", "cst"], ["F11"])
            tt("pool", gd[:, :, :], gd[:, :, :], cb(C_LNB), ALU.add, ["F11", "cst"], ["F11"])
            tt("pool", gd[:, :, :], gd[:, :, :], bv[:, :, :], ALU.add, ["F11", "F9"], ["F11"])
            A_fm = H[12]
            tt("dve", A_fm[:, :, :], gd[:, :, :], g_sb[:, :, :], ALU.mult, ["F11", "H2"], ["H12"])

            slab, slabn = use_slab()
            qa, qan = nextQ()
            fm_proj(qa, qan, slab, slabn, lambda kc: A_fm[:, kc, :], ["H12"])
            after_slab()
            m1, m2 = F[0], F[10]
            stt(m1[:, :, :], tG[0][:, :, :], 1.0, qa[:, :, :], ALU.add, ALU.mult, ["H5", qan], ["F0"])
            slab, slabn = use_slab()
            qc, qcn = nextQ()
            fm_proj(qc, qcn, slab, slabn, lambda kc: C_fm[:, kc, :], ["H8"])
            after_slab()
            stt(m2[:, :, :], tG[1][:, :, :], 1.0, qc[:, :, :], ALU.add, ALU.mult, ["H6", qcn], ["F10"])
            merged = H[13]
            tt("pool", merged[:, :, :], m1[:, :, :], m2[:, :, :], ALU.add, ["F0", "F10"], ["H13"])
            slab, slabn = use_slab()
            q, qn = nextQ()
            qf = flat(q)
            for hf in range(2):
                for kc in range(NCH):
                    mm(qf[:, hf * 512:(hf + 1) * 512], merged[:, kc, :], slab[:, kc, hf * 512:(hf + 1) * 512],
                       kc == 0, kc == NCH - 1, ["H13", slabn], [qn], free=True)
            after_slab()
            stt(x_tm[:, :], qf, 0.5, x_tm[:, :], ALU.mult, ALU.add, [qn, "x_tm"], ["x_tm"])

            h2 = H[14]
            norm_to_fm(C_GFF, h2[:, :, :], "H14", 1)
            rl = H[15]
            for g in range(4):
                slab, slabn = use_slab()
                q, qn = nextQ()
                fm_proj(q, qn, slab, slabn, lambda kc: h2[:, kc, :], ["H14"])
                after_slab()
                act(rl[:, :, :], q[:, :, :], AF.Relu, [qn], ["H15"])
                tt("pool", r2[:, g * 8:(g + 1) * 8, :], rl[:, :, :], rl[:, :, :], ALU.mult, ["H15"], ["r2"])
            q, qn = nextQ()
            qf = flat(q)
            for g in range(4):
                slab, slabn = use_slab()
                for hf in range(2):
                    for kc in range(NCH):
                        mm(qf[:, hf * 512:(hf + 1) * 512], r2[:, g * 8 + kc, :], slab[:, kc, hf * 512:(hf + 1) * 512],
                           g == 0 and kc == 0, g == 3 and kc == NCH - 1, ["r2", slabn], [qn], free=True)
                after_slab()
            tt("dve", x_tm[:, :], qf, x_tm[:, :], ALU.add, [qn, "x_tm"], ["x_tm"])

            ytmp, osb = flat(F[0]), flat(F[11])
            S.op("pool", lambda e: e.memset(st[:, 2:3], 0.0), writes=["st"])
            act(junk[:, :], x_tm[:, :], AF.Square, ["x_tm"], ["H15", "st"], accum=st[:, 2:3])
            rstd_from_ss(2, 1e-6)
            act(ytmp, x_tm[:, :], AF.Identity, ["x_tm", "st"], ["F0"], scale=st[:, 2:3])
            tt("dve", osb, ytmp, gfin_bc[:, :], ALU.mult, ["F0", "gfin_bc"], ["F11"])
            dma("sp", out[it * TT:(it + 1) * TT, :], osb, ["F11"], ["out"])

        sems = {k: es.enter_context(nc.semaphore(f"s_{k}")) for k in ("pe", "act", "dve", "pool", "sp")}
        dsems = {(qn_, i): es.enter_context(nc.semaphore(f"d_{qn_}{i}"))
                 for qn_ in ("sp", "pool") for i in range(NSEM_DMA)}
        with nc.Block() as block:
            S.emit(nc, block, sems, dsems)
    return nc


_NC_CACHE = {}


def _prep_inputs(inp, b, ntiles):
    T = ntiles * TT
    f = lambda a: np.ascontiguousarray(np.asarray(a, dtype=np.float32))
    mu_rkv = f(inp["mu_rkv"])[0]
    mu_lora = f(inp["mu_lora"])[0]
    vrows = np.stack([
        f(inp["norm_mix_g"])[0], mu_rkv[0:D], mu_rkv[D:2 * D], mu_rkv[2 * D:3 * D],
        mu_lora[0], mu_lora[1], mu_lora[2], f(inp["k_k"])[0], f(inp["k_a"])[0],
        f(inp["r_k"])[0].reshape(-1), f(inp["ln_x_g"])[0], f(inp["ln_x_b"])[0],
        f(inp["conv_b"])[0], f(inp["conv_ln_g"])[0], f(inp["conv_ln_b"])[0], f(inp["norm_ff_g"])[0]], 0)
    m = {
        "x": f(inp["x"][b, :T]),
        "w_in": f(inp["w_in"])[0], "w_rwkv_proj": f(inp["w_rwkv_proj"])[0],
        "w_conv_proj": f(inp["w_conv_proj"])[0], "w_out": f(inp["w_out"])[0],
        "w_ff1": f(inp["w_ff1"])[0], "w_ff2": f(inp["w_ff2"])[0],
        "vrows": f(vrows), "conv_w": f(inp["conv_w"])[0],
        "decay_w0": f(inp["decay_w0"]), "aaa_a0": f(inp["aaa_a0"]), "b_gate": f(inp["b_gate"]),
        "norm_final_g": f(inp["norm_final_g"]).reshape(1, D),
        "decay_w1": f(inp["decay_w1"])[0], "decay_w2": f(inp["decay_w2"])[0],
        "aaa_a1": f(inp["aaa_a1"])[0], "aaa_a2": f(inp["aaa_a2"])[0],
        "gate_g1": f(inp["gate_g1"])[0], "gate_g2": f(inp["gate_g2"])[0],
        "cmat": _consts_np(),
    }
    return m


def run(inputs, ntiles, cores):
    if ntiles not in _NC_CACHE:
        _NC_CACHE[ntiles] = build(ntiles)
    nc = _NC_CACHE[ntiles]
    in_maps = [_prep_inputs(inputs, b, ntiles) for b in range(cores)]
    res = run_bass_kernel_spmd(nc, in_maps, core_ids=list(range(cores)))
    return np.stack([np.asarray(r["out"], dtype=np.float32) for r in res.results], 0)


def kernel(**inputs):
    return run(inputs, T_FULL // TT, 8)
```

```python
import contextlib
import numpy as np
import concourse.bass as bass
import concourse.mybir as mybir
from concourse.bass_utils import run_bass_kernel_spmd

F32 = mybir.dt.float32
BF16 = mybir.dt.bfloat16
AF = mybir.ActivationFunctionType
ALU = mybir.AluOpType

D = 1024
T_FULL = 8192
NCH = 8
TT = 128
NCST = 22
(C_GMIX, C_MUR, C_MUK, C_MUV, C_MUW, C_MUA, C_MUG, C_KK, C_KA, C_RK, C_LNG, C_LNB,
 C_CB, C_CLG, C_CLB, C_GFF, C_OMR, C_OMK, C_OMV, C_OMKA, C_CLG2, C_CLB2) = range(22)
NSLAB = 18
NSEM_DMA = 6
import os as _os
_STAGE = int(_os.environ.get("KSTAGE", "99"))
_SKIP = _os.environ.get("KSKIP", "").split(",")
_NOCONV = int(_os.environ.get("KNOCONV", "0"))


class _Op:
    __slots__ = ("eng", "fn", "waits", "count", "dma", "sem", "semval", "prewait", "needed", "f32", "free")


class Sched:
    def __init__(self):
        self.ops = {k: [] for k in ("pe", "act", "dve", "pool", "sp")}
        self.last_w = {}
        self.readers = {}
        self.ndma = {"sp": 0, "pool": 0}

    def op(self, eng, fn, reads=(), writes=(), dma=False):
        o = _Op()
        o.eng, o.fn, o.dma = eng, fn, dma
        o.f32 = False
        o.free = False
        deps = []
        for r in reads:
            w = self.last_w.get(r)
            if w is not None:
                deps.append(w)
        for w_ in writes:
            w = self.last_w.get(w_)
            if w is not None:
                deps.append(w)
            deps.extend(self.readers.get(w_, ()))
        o.waits = deps
        self.ops[eng].append(o)
        o.count = len(self.ops[eng])
        o.prewait = None
        if dma:
            n = self.ndma[eng]
            self.ndma[eng] = n + 1
            o.sem = (eng, n % NSEM_DMA)
            o.semval = 16 * (n // NSEM_DMA + 1)
            if n >= NSEM_DMA:
                o.prewait = (o.sem, 16 * (n // NSEM_DMA))
        for w_ in writes:
            self.last_w[w_] = o
            self.readers[w_] = []
        for r in reads:
            self.readers.setdefault(r, []).append(o)
        return o

    def emit(self, nc, block, sems, dsems):
        engs = {"pe": block.tensor, "act": block.scalar, "dve": block.vector,
                "pool": block.gpsimd, "sp": block.sync}
        for name, ops in self.ops.items():
            for o in ops:
                o.needed = False
        for name, ops in self.ops.items():
            prev = None
            for o in ops:
                keep = []
                for d in o.waits:
                    if name == "pe" and o.free and (not d.dma) and d.eng == "pe":
                        continue
                    d.needed = True
                    keep.append(d)
                if name == "pe" and prev is not None and (o.f32 != prev.f32):
                    prev.needed = True
                    keep.append(prev)
                o.waits = keep
                prev = o
        for name, ops in self.ops.items():
            c = 0
            for o in ops:
                if o.needed and not o.dma:
                    c += 1
                o.count = c
        for name, deco in engs.items():
            ops = self.ops[name]

            def body(e, ops=ops, name=name):
                waited = {}
                for o in ops:
                    need = {}
                    for d in o.waits:
                        if d.dma:
                            key, val = ("d",) + d.sem, d.semval
                        else:
                            key, val = ("e", d.eng), d.count
                        if need.get(key, 0) < val:
                            need[key] = val
                    if o.prewait is not None:
                        key, val = ("d",) + o.prewait[0], o.prewait[1]
                        if need.get(key, 0) < val:
                            need[key] = val
                    for key, val in need.items():
                        if waited.get(key, 0) >= val:
                            continue
                        waited[key] = val
                        s = dsems[key[1:]] if key[0] == "d" else sems[key[1]]
                        e.wait_ge(s, val)
                    ins = o.fn(e)
                    if o.dma:
                        ins.then_inc(dsems[o.sem], 16)
                    elif o.needed:
                        ins.then_inc(sems[name], 1)
                if name == "sp":
                    last = {}
                    for o in ops:
                        if o.dma:
                            last[o.sem] = o.semval
                    for k, v in last.items():
                        e.wait_ge(dsems[k], v)

            deco(body)


def _consts_np():
    i = np.arange(128)
    s, t = i[:, None], i[None, :]
    c = {}
    c["ident"] = (s == t).astype(np.float32)
    c["tri"] = np.where(s <= t, -0.5 * np.exp(-0.5), 0.0).astype(np.float32)
    c["msu"] = (t > s).astype(np.float32)
    c["mui"] = (t >= s).astype(np.float32)
    c["msl"] = (t < s).astype(np.float32)
    c["bones"] = ((s // 64) == (t // 64)).astype(np.float32)
    return np.stack([c[k] for k in ("ident", "tri", "msu", "mui", "msl", "bones")], 0)


def build(ntiles):
    nc = bass.Bass("TRN2", target_bir_lowering=False)
    T = ntiles * TT
    dt_in = {}

    def din(name, shape):
        dt_in[name] = nc.dram_tensor(name, list(shape), F32, kind="ExternalInput").ap()
        return dt_in[name]

    x = din("x", [T, D])
    w_in = din("w_in", [D, 7 * D])
    w_a = din("w_rwkv_proj", [D, D])
    w_c = din("w_conv_proj", [D, D])
    w_o = din("w_out", [D, D])
    w_f1 = din("w_ff1", [D, 4 * D])
    w_f2 = din("w_ff2", [4 * D, D])
    vrows = din("vrows", [16, D])
    convw = din("conv_w", [31, D])
    w0row = din("decay_w0", [1, D])
    a0row = din("aaa_a0", [1, D])
    bgrow = din("b_gate", [1, 2 * D])
    gfin = din("norm_final_g", [1, D])
    dw1 = din("decay_w1", [D, 64])
    dw2 = din("decay_w2", [64, D])
    aa1 = din("aaa_a1", [D, 64])
    aa2 = din("aaa_a2", [64, D])
    gg1 = din("gate_g1", [D, 128])
    gg2 = din("gate_g2", [128, D])
    cmat = din("cmat", [6, 128, 128])
    out = nc.dram_tensor("out", [T, D], F32, kind="ExternalOutput").ap()
    wbf = nc.dram_tensor("wbf", [NSLAB, 128, NCH * D], BF16, kind="ExternalOutput").ap()

    S = Sched()
    es = contextlib.ExitStack()
    with es:
        def sb(name, shape, dt):
            return es.enter_context(nc.sbuf_tensor(name, list(shape), dt))

        def ps(name, shape, dt=F32):
            return es.enter_context(nc.psum_tensor(name, list(shape), dt))

        Fs = [sb(f"F{i}", [128, NCH, TT], F32) for i in range(12) if i not in (4, 5, 8)]
        Fs = {i: t for i, t in zip([i for i in range(12) if i not in (4, 5, 8)], Fs)}
        Hs = [sb(f"H{i}", [128, NCH, TT], BF16) for i in range(16)]
        Ms = {k: sb(k, [128, 16, TT], BF16) for k in
              ("Mka", "Mbr", "Mkr", "YA", "YTA", "YB", "YTB", "PA")}
        ring = [sb(f"ring{i}", [128, NCH, D], BF16) for i in range(3)]
        Qs = [ps(f"Q{i}", [128, NCH, TT]) for i in range(4)]
        x_tm = sb("x_tm", [128, D], F32)
        h_ext = sb("h_ext", [128, NCH, TT + 1], BF16)
        u_ext = sb("u_ext", [128, NCH, TT + 30], BF16)
        lora_sb = sb("lora_sb", [128, TT], BF16)
        sg_sb = sb("sg_sb", [128, TT], BF16)
        tg_sb = sb("tg_sb", [128, TT], F32)
        r2 = sb("r2", [128, 32, TT], BF16)
        S_f = sb("S_f", [128, NCH, 64], F32)
        S_b = sb("S_b", [128, NCH, 64], BF16)
        halo = sb("halo", [128, 3, NCH, 1], F32)
        st = sb("st", [128, 8], F32)
        lnt = [sb(f"lnt{i}", [128, TT], F32) for i in range(4)]
        cst = sb("cst", [128, NCH, NCST], F32)
        cw = sb("cw", [128, NCH, 31], F32)
        cm_f = sb("cm_f", [128, 6, 128], F32)
        ident_b = sb("ident_b", [128, 128], BF16)
        bones_b = sb("bones_b", [128, 128], BF16)
        bones64_b = sb("bones64_b", [128, 128], BF16)
        onesdiv = sb("onesdiv", [128, 128], F32)
        gfin_bc = sb("gfin_bc", [128, D], F32)
        w0_row = sb("w0_row", [1, D], F32)
        a0_row = sb("a0_row", [1, D], BF16)
        bg_row = sb("bg_row", [1, 2 * D], BF16)
        ones_f = sb("ones_f", [1, 128], F32)
        ones_b = sb("ones_b", [1, 128], BF16)
        wa1 = sb("wa1", [128, NCH, 128], BF16)
        wa1mu = sb("wa1mu", [128, NCH, 128], BF16)
        g1 = sb("g1", [128, NCH, 128], BF16)
        g1mu = sb("g1mu", [128, NCH, 128], BF16)
        l2 = sb("l2", [128, D], BF16)
        g2 = sb("g2", [128, D], BF16)

        rows = Fs[0][:, :, :].rearrange("p c t -> p (c t)")
        hT = Hs[9][:, :, :].rearrange("p c t -> p (c t)")
        junk = Hs[15][:, :, :].rearrange("p c t -> p (c t)")
        ident_f = cm_f[:, 0, :]
        tri_f = cm_f[:, 1, :]
        msu, mui, msl = cm_f[:, 2, :], cm_f[:, 3, :], cm_f[:, 4, :]

        qi = [0]
        in_setup = [True]

        def nextQ():
            q = Qs[qi[0] % 4]
            qi[0] += 1
            return q, f"Q{(qi[0] - 1) % 4}"

        def cb(idx, n=TT):
            return cst[:, :, idx:idx + 1].to_broadcast([128, NCH, n])

        def mm(out_ap, lhsT, rhs, start, stop, reads, writes, free=False):
            if "pesetup" in _SKIP and in_setup[0]:
                return
            o_ = S.op("pe", lambda e: e.matmul(out_ap, lhsT, rhs, start=start, stop=stop),
                       reads=reads, writes=writes)
            o_.f32 = (lhsT.dtype == F32)
            o_.free = free

        def act(out_ap, in_ap, func, reads, writes, scale=1.0, bias=0.0, accum=None):
            if accum is None:
                S.op("act", lambda e: e.activation(out=out_ap, in_=in_ap, func=func,
                                                   bias=bias, scale=scale),
                     reads=reads, writes=writes)
            else:
                S.op("act", lambda e: e.activation(out=out_ap, in_=in_ap, func=func,
                                                   bias=bias, scale=scale, accum_out=accum),
                     reads=reads, writes=writes)

        def tt(eng, out_ap, a, b, op, reads, writes):
            S.op(eng, lambda e: e.tensor_tensor(out=out_ap, in0=a, in1=b, op=op),
                 reads=reads, writes=writes)

        def ts(eng, out_ap, a, s1, s2, op0, op1, reads, writes):
            if op1 is None:
                S.op(eng, lambda e: e.tensor_scalar(out=out_ap, in0=a, scalar1=s1, scalar2=None,
                                                    op0=op0), reads=reads, writes=writes)
            else:
                S.op(eng, lambda e: e.tensor_scalar(out=out_ap, in0=a, scalar1=s1, scalar2=s2,
                                                    op0=op0, op1=op1), reads=reads, writes=writes)

        def stt(out_ap, a, sc, b, op0, op1, reads, writes):
            S.op("dve", lambda e: e.scalar_tensor_tensor(out=out_ap, in0=a, scalar=sc, in1=b,
                                                         op0=op0, op1=op1),
                 reads=reads, writes=writes)

        def cp(eng, out_ap, in_ap, reads, writes):
            if eng == "act":
                act(out_ap, in_ap, AF.Copy, reads, writes)
            else:
                S.op(eng, lambda e: e.tensor_copy(out=out_ap, in_=in_ap), reads=reads, writes=writes)

        stg = {"n": 0}

        def dma(q, out_ap, in_ap, reads, writes, part=None, cols=None):
            if q == "pool":
                i = stg["n"] % 2
                stg["n"] += 1
                sf = flat(Fs[10 + i])
                p0, p1 = part
                if len(cols) == 2:
                    sview = sf[p0:p1, 0:cols[0] * cols[1]].rearrange("p (a b) -> p a b", a=cols[0])
                else:
                    sview = sf[p0:p1, 0:cols[0]]
                dma("sp", sview, in_ap, [], [f"F{10 + i}"])
                if out_ap.tensor.name.startswith("wbf"):
                    hv = flat(Hs[10 + i])[p0:p1, 0:cols[0]]
                    cp(("act", "dve")[i], hv, sview, [f"F{10 + i}"], [f"H{10 + i}"])
                    dma("sp", out_ap, hv, [f"H{10 + i}"], writes)
                else:
                    cp(("act", "dve")[i], out_ap, sview, [f"F{10 + i}"], writes)
                return
            S.op(q, lambda e: e.dma_start(out=out_ap, in_=in_ap), reads=reads, writes=writes, dma=True)

        def flat(tile3):
            return tile3[:, :, :].rearrange("p c t -> p (c t)")

        def slab_src(si):
            if si < 7:
                order = [0, 1, 2, 4, 3, 5, 6]
                g = order[si]
                return w_in[:, g * D:(g + 1) * D]
            if si == 7:
                return w_a
            if si == 8:
                return w_c
            if si == 9:
                return w_o
            if si < 14:
                g = si - 10
                return w_f1[:, g * D:(g + 1) * D]
            g = si - 14
            return w_f2[g * D:(g + 1) * D, :]

        for si in range(NSLAB if not _NOCONV else 0):
            src = slab_src(si).rearrange("(kc p) n -> p kc n", p=128)
            dst = wbf[si].rearrange("p (kc n) -> p kc n", kc=NCH)
            for kc in range(NCH):
                dma("pool", dst[:, kc, :], src[:, kc, :], reads=[], writes=[f"wbf{si}_{kc}"], part=(0, 128), cols=(D,))

        dma("pool", wa1[:, :, 0:64], dw1.rearrange("(kc p) n -> p kc n", p=128), [], ["wa1"], part=(0, 128), cols=(NCH, 64))
        dma("pool", wa1[:, :, 64:128], aa1.rearrange("(kc p) n -> p kc n", p=128), [], ["wa1"], part=(0, 128), cols=(NCH, 64))
        dma("pool", g1[:, :, :], gg1.rearrange("(kc p) n -> p kc n", p=128), [], ["g1"], part=(0, 128), cols=(NCH, 128))
        dma("pool", l2[0:64, :], dw2, [], ["l2"], part=(0, 64), cols=(D,))
        dma("pool", l2[64:128, :], aa2, [], ["l2"], part=(64, 128), cols=(D,))
        dma("pool", g2[:, :], gg2, [], ["g2"], part=(0, 128), cols=(D,))
        dma("pool", a0_row[:, :], a0row, [], ["a0_row"], part=(0, 1), cols=(D,))
        dma("pool", bg_row[:, 0:D], bgrow[:, 0:D], [], ["bg_row"], part=(0, 1), cols=(D,))
        dma("pool", bg_row[:, D:2 * D], bgrow[:, D:2 * D], [], ["bg_row"], part=(0, 1), cols=(D,))
        dma("sp", w0_row[:, :], w0row, [], ["w0_row"])
        dma("sp", cm_f[:, :, :], cmat.rearrange("k p n -> p k n"), [], ["cm_f"])
        dma("sp", rows[0:1, :], gfin, ["F0"], ["F0"])

        cp("dve", ident_b[:, :], ident_f, ["cm_f"], ["ident_b"])
        cp("dve", bones_b[:, :], cm_f[:, 5, :], ["cm_f"], ["bones_b"])
        ts("dve", bones64_b[:, :], cm_f[:, 5, :], 1.0 / 64, None, ALU.mult, None, ["cm_f"], ["bones64_b"])
        S.op("pool", lambda e: e.memset(onesdiv[:, :], 1.0 / D), writes=["onesdiv"])
        S.op("pool", lambda e: e.memset(ones_b[:, :], 1.0), writes=["ones_b"])
        S.op("pool", lambda e: e.memset(h_ext[:, :, :], 0.0), writes=["h_ext"])
        S.op("pool", lambda e: e.memset(u_ext[:, :, :], 0.0), writes=["u_ext"])
        S.op("pool", lambda e: e.memset(halo[:, :, :, :], 0.0), writes=["halo"])
        S.op("pool", lambda e: e.memset(S_f[:, :, :], 0.0), writes=["S_f"])
        S.op("pool", lambda e: e.memset(S_b[:, :, :], 0.0), writes=["S_b"])

        S.op("pool", lambda e: e.memset(ones_f[:, :], 1.0), writes=["ones_f"])
        q, qn = nextQ()
        qf = flat(q)
        for hf in range(2):
            mm(qf[:, hf * 512:(hf + 1) * 512], ones_f[0:1, :], rows[0:1, hf * 512:(hf + 1) * 512], True, True,
               ["ones_f", "F0"], [qn])
        cp("dve", gfin_bc[:, :], qf, [qn], ["gfin_bc"])
        dma("sp", rows[0:16, :], vrows, ["F0"], ["F0"])
        q, qn = nextQ()
        qf = flat(q)
        for c in range(NCH):
            mm(qf[:, c * 32:c * 32 + 16], rows[0:16, c * 128:(c + 1) * 128], cm_f[0:16, 0, 0:16],
               True, True, ["F0", "cm_f"], [qn])
        cp("dve", cst[:, :, 0:16], qf[:, 0:256].rearrange("p (c k) -> p c k", c=NCH)[:, :, 0:16],
           [qn], ["cst"])
        for (dst_i, src_i) in ((C_OMR, C_MUR), (C_OMK, C_MUK), (C_OMV, C_MUV), (C_OMKA, C_KA)):
            ts("dve", cst[:, :, dst_i:dst_i + 1], cst[:, :, src_i:src_i + 1], -1.0, 1.0,
               ALU.mult, ALU.add, ["cst"], ["cst"])
        for (dst_i, src_i) in ((C_CLG2, C_CLG), (C_CLB2, C_CLB)):
            ts("dve", cst[:, :, dst_i:dst_i + 1], cst[:, :, src_i:src_i + 1], 0.5, None,
               ALU.mult, None, ["cst"], ["cst"])
        dma("sp", rows[0:31, :], convw, ["F0"], ["F0"])
        q, qn = nextQ()
        qf = flat(q)
        for c in range(NCH):
            mm(qf[:, c * 32:c * 32 + 31], rows[0:31, c * 128:(c + 1) * 128], cm_f[0:31, 0, 0:31],
               True, True, ["F0", "cm_f"], [qn])
        ts("dve", cw[:, :, :], qf[:, 0:256].rearrange("p (c k) -> p c k", c=NCH)[:, :, 0:31],
           0.5, None, ALU.mult, None, [qn], ["cw"])
        tt("dve", wa1mu[:, :, 0:64], wa1[:, :, 0:64],
           cst[:, :, C_MUW:C_MUW + 1].to_broadcast([128, NCH, 64]), ALU.mult, ["wa1", "cst"], ["wa1mu"])
        tt("dve", wa1mu[:, :, 64:128], wa1[:, :, 64:128],
           cst[:, :, C_MUA:C_MUA + 1].to_broadcast([128, NCH, 64]), ALU.mult, ["wa1", "cst"], ["wa1mu"])
        tt("dve", g1mu[:, :, :], g1[:, :, :],
           cst[:, :, C_MUG:C_MUG + 1].to_broadcast([128, NCH, 128]), ALU.mult, ["g1", "cst"], ["g1mu"])

        gs = {"issued": 0, "used": 0}
        total_slabs = NSLAB * ntiles

        def issue_to(n):
            while gs["issued"] < min(n, total_slabs):
                g = gs["issued"]
                si, ri = g % NSLAB, g % 3
                dma("sp", ring[ri][:, :, :].rearrange("p c n -> p (c n)"), wbf[si],
                    [f"wbf{si}_{kc}" for kc in range(NCH)], [f"ring{ri}"])
                gs["issued"] += 1

        def use_slab():
            g = gs["used"]
            issue_to(g + 2)
            gs["used"] += 1
            return ring[g % 3], f"ring{g % 3}"

        def after_slab():
            issue_to(gs["used"] + 2)

        def rstd_from_ss(col, eps):
            ts("dve", st[:, col:col + 1], st[:, col:col + 1], 1.0 / D, eps, ALU.mult, ALU.add, ["st"], ["st"])
            act(st[:, col:col + 1], st[:, col:col + 1], AF.Sqrt, ["st"], ["st"])
            S.op("dve", lambda e: e.reciprocal(out=st[:, col:col + 1], in_=st[:, col:col + 1]),
                 reads=["st"], writes=["st"])

        def norm_to_fm(gidx, dst_ap, dst_key, col):
            S.op("pool", lambda e: e.memset(st[:, col:col + 1], 0.0), writes=["st"])
            act(junk[:, :], x_tm[:, :], AF.Square, ["x_tm"], ["H15", "st"], accum=st[:, col:col + 1])
            rstd_from_ss(col, 1e-6)
            act(hT[:, :], x_tm[:, :], AF.Identity, ["x_tm", "st"], ["H9"], scale=st[:, col:col + 1])
            q, qn = nextQ()
            for c in range(NCH):
                mm(q[:, c, :], hT[:, c * 128:(c + 1) * 128], ident_b[:, :], True, True,
                   ["H9", "ident_b"], [qn], free=True)
            tt("dve", dst_ap, q[:, :, :], cb(gidx), ALU.mult, [qn, "cst"], [dst_key])

        F = Fs
        H = Hs

        def fm_proj(q, qn, slab, slabn, rhs_fn, rhs_keys, bias_row=None, bias_off=0, bias_key=None):
            for mc in range(NCH):
                for kc in range(NCH):
                    mm(q[:, mc, :], slab[:, kc, mc * 128:(mc + 1) * 128], rhs_fn(kc),
                       kc == 0, (kc == NCH - 1) and bias_row is None, [slabn] + rhs_keys, [qn], free=True)
                if bias_row is not None:
                    mm(q[:, mc, :], bias_row[0:1, bias_off + mc * 128:bias_off + (mc + 1) * 128],
                       ones_b[0:1, :], False, True, [bias_key, "ones_b"], [qn])

        in_setup[0] = False
        for it in range(ntiles):
            if it > 0:
                cp("pool", h_ext[:, :, 0:1], h_ext[:, :, TT:TT + 1], ["h_ext"], ["h_ext"])
                cp("pool", u_ext[:, :, 0:30], u_ext[:, :, TT:TT + 30], ["u_ext"], ["u_ext"])
            dma("sp", x_tm[:, :], x[it * TT:(it + 1) * TT, :], [], ["x_tm"])
            if _STAGE == 0:
                dma("sp", out[it * TT:(it + 1) * TT, :], x_tm[:, :], ["x_tm"], ["out"])
                continue
            norm_to_fm(C_GMIX, h_ext[:, :, 1:TT + 1], "h_ext", 0)
            dh = H[0]
            tt("dve", dh[:, :, :], h_ext[:, :, 0:TT], h_ext[:, :, 1:TT + 1], ALU.subtract, ["h_ext"], ["H0"])
            hc = lambda kc: h_ext[:, kc, 1:TT + 1]

            q, qn = nextQ()
            qf = flat(q)
            for kc in range(NCH):
                mm(qf[:, 0:128], wa1[:, kc, :], hc(kc), kc == 0, False, ["wa1", "h_ext"], [qn])
            for kc in range(NCH):
                mm(qf[:, 0:128], wa1mu[:, kc, :], dh[:, kc, :], False, kc == NCH - 1, ["wa1mu", "H0"], [qn])
            for kc in range(NCH):
                mm(qf[:, 128:256], g1[:, kc, :], hc(kc), kc == 0, False, ["g1", "h_ext"], [qn])
            for kc in range(NCH):
                mm(qf[:, 128:256], g1mu[:, kc, :], dh[:, kc, :], False, kc == NCH - 1, ["g1mu", "H0"], [qn])
            act(lora_sb[0:64, :], qf[0:64, 0:128], AF.Tanh, [qn], ["lora_sb"])
            act(lora_sb[64:128, :], qf[64:128, 0:128], AF.Copy, [qn], ["lora_sb"])
            act(tg_sb[:, :], qf[:, 128:256], AF.Tanh, [qn], ["tg_sb"], scale=0.5)
            ts("dve", sg_sb[:, :], tg_sb[:, :], 0.5, 0.5, ALU.mult, ALU.add, ["tg_sb"], ["sg_sb"])

            q, qn = nextQ()
            qf = flat(q)
            for hf in range(2):
                mm(qf[:, hf * 512:(hf + 1) * 512], lora_sb[0:64, :], l2[0:64, hf * 512:(hf + 1) * 512],
                   True, False, ["lora_sb", "l2"], [qn])
                mm(qf[:, hf * 512:(hf + 1) * 512], ones_f[0:1, :], w0_row[0:1, hf * 512:(hf + 1) * 512],
                   False, True, ["ones_f", "w0_row"], [qn])
            s_tm = flat(F[0])
            act(s_tm, qf, AF.Tanh, [qn], ["F0"], scale=0.5)
            ts("dve", s_tm, s_tm, 1.0, None, ALU.add, None, ["F0"], ["F0"])
            q, qn = nextQ()
            for c in range(NCH):
                mm(q[:, c, :], s_tm[:, c * 128:(c + 1) * 128], tri_f, True, True, ["F0", "cm_f"], [qn])
            E1, E3 = F[1], F[2]
            act(E1[:, :, :], q[:, :, :], AF.Exp, [qn], ["F1"])
            act(E3[:, :, :], q[:, :, :], AF.Exp, [qn], ["F2"], scale=-1.0)

            q, qn = nextQ()
            for c in range(NCH):
                mm(q[:, c, :], l2[64:128, c * 128:(c + 1) * 128], lora_sb[64:128, :], True, False,
                   ["l2", "lora_sb"], [qn])
                mm(q[:, c, :], a0_row[0:1, c * 128:(c + 1) * 128], ones_b[0:1, :], False, True,
                   ["a0_row", "ones_b"], [qn])
            a_t = F[3]
            act(a_t[:, :, :], q[:, :, :], AF.Tanh, [qn], ["F3"], scale=0.5)
            ts("dve", a_t[:, :, :], a_t[:, :, :], 0.5, 0.5, ALU.mult, ALU.add, ["F3"], ["F3"])
            q, qn = nextQ()
            for c in range(NCH):
                mm(q[:, c, :], g2[:, c * 128:(c + 1) * 128], sg_sb[:, :], True, True, ["g2", "sg_sb"], [qn], free=True)
            g_sb = H[2]
            cp("act", g_sb[:, :, :], q[:, :, :], [qn], ["H2"])

            rkv = [F[6], F[7], H[3]]
            for j in range(3):
                slab, slabn = use_slab()
                q, qn = nextQ()
                fm_proj(q, qn, slab, slabn, hc, ["h_ext"])
                after_slab()
                tmp, t2 = F[0], F[9]
                tt("dve", tmp[:, :, :], q[:, :, :], cb(C_OMR + j), ALU.mult, [qn, "cst"], ["F0"])
                tt("dve", t2[:, :, 1:TT], q[:, :, 0:TT - 1], cb(C_MUR + j, TT - 1), ALU.mult, [qn, "cst"], ["F9"])
                tt("dve", t2[:, :, 0:1], halo[:, j, :, :], cst[:, :, C_MUR + j:C_MUR + j + 1], ALU.mult,
                   ["halo", "cst"], ["F9"])
                cp("act", halo[:, j, :, :], q[:, :, TT - 1:TT], [qn], ["halo"])
                tt("pool", rkv[j][:, :, :], tmp[:, :, :], t2[:, :, :], ALU.add, ["F0", "F9"], [("F6", "F7", "H3")[j]])
            r_t, k_t, v_t = rkv
            v_bf = v_t

            slab, slabn = use_slab()
            q, qn = nextQ()
            fm_proj(q, qn, slab, slabn, hc, ["h_ext"])
            after_slab()
            tb = H[4]
            act(tb[:, :, :], q[:, :, :], AF.Tanh, [qn], ["H4"], scale=0.5)
            slab, slabn = use_slab()
            q, qn = nextQ()
            fm_proj(q, qn, slab, slabn, hc, ["h_ext"])
            after_slab()
            stt(u_ext[:, :, 30:30 + TT], tb[:, :, :], 1.0, q[:, :, :], ALU.add, ALU.mult, ["H4", qn], ["u_ext"])
            tG = [H[5], H[6]]
            for j in range(2):
                slab, slabn = use_slab()
                q, qn = nextQ()
                fm_proj(q, qn, slab, slabn, hc, ["h_ext"], bias_row=bg_row, bias_off=j * D, bias_key="bg_row")
                after_slab()
                act(tG[j][:, :, :], q[:, :, :], AF.Tanh, [qn], [f"H{5 + j}"], scale=0.5)

            acc, prod = F[9], F[10]
            for j in range(31):
                wj = cw[:, :, j:j + 1].to_broadcast([128, NCH, TT])
                if j == 0:
                    tt("pool", acc[:, :, :], u_ext[:, :, 0:TT], wj, ALU.mult, ["u_ext", "cw"], ["F9"])
                else:
                    tt("pool", prod[:, :, :], u_ext[:, :, j:j + TT], wj, ALU.mult, ["u_ext", "cw"], ["F10"])
                    tt("pool", acc[:, :, :], acc[:, :, :], prod[:, :, :], ALU.add, ["F9", "F10"], ["F9"])
            tt("pool", acc[:, :, :], acc[:, :, :], cb(C_CB), ALU.add, ["F9", "cst"], ["F9"])
            sq = F[10]
            act(sq[:, :, :], acc[:, :, :], AF.Square, ["F9"], ["F10"])
            q, qn = nextQ()
            qf = flat(q)
            for c in range(NCH):
                mm(qf[:, 0:128], onesdiv[:, :], acc[:, c, :], c == 0, c == NCH - 1, ["onesdiv", "F9"], [qn])
            for c in range(NCH):
                mm(qf[:, 128:256], onesdiv[:, :], sq[:, c, :], c == 0, c == NCH - 1, ["onesdiv", "F10"], [qn])
            mean_s, nm2, var_s = lnt[0], lnt[1], lnt[2]
            cp("act", mean_s[:, :], qf[:, 0:128], [qn], ["lnt0"])
            stt(nm2[:, :], mean_s[:, :], -1.0, mean_s[:, :], ALU.mult, ALU.mult, ["lnt0"], ["lnt1"])
            stt(var_s[:, :], qf[:, 128:256], 1e-5, nm2[:, :], ALU.add, ALU.add, [qn, "lnt1"], ["lnt2"])
            act(var_s[:, :], var_s[:, :], AF.Sqrt, ["lnt2"], ["lnt2"])
            S.op("dve", lambda e: e.reciprocal(out=var_s[:, :], in_=var_s[:, :]), reads=["lnt2"], writes=["lnt2"])
            tt("dve", acc[:, :, :], acc[:, :, :], mean_s[:, :].unsqueeze(1).to_broadcast([128, NCH, TT]),
               ALU.subtract, ["F9", "lnt0"], ["F9"])
            tt("dve", acc[:, :, :], acc[:, :, :], var_s[:, :].unsqueeze(1).to_broadcast([128, NCH, TT]),
               ALU.mult, ["F9", "lnt2"], ["F9"])
            tt("dve", acc[:, :, :], acc[:, :, :], cb(C_CLG2), ALU.mult, ["F9", "cst"], ["F9"])
            tt("dve", acc[:, :, :], acc[:, :, :], cb(C_CLB2), ALU.add, ["F9", "cst"], ["F9"])
            th = H[7]
            act(th[:, :, :], acc[:, :, :], AF.Tanh, ["F9"], ["H7"])
            C_fm = H[8]
            stt(C_fm[:, :, :], th[:, :, :], 1.0, acc[:, :, :], ALU.add, ALU.mult, ["H7", "F9"], ["H8"])

            kkr, ksq = F[0], H[9]
            tt("dve", kkr[:, :, :], k_t[:, :, :], cb(C_KK), ALU.mult, ["F7", "cst"], ["F0"])
            act(ksq[:, :, :], kkr[:, :, :], AF.Square, ["F0"], ["H9"])
            q, qn = nextQ()
            for c in range(NCH):
                mm(q[:, c, :], bones_b[:, :], ksq[:, c, :], True, True, ["bones_b", "H9"], [qn], free=True)
            rn = F[9]
            act(rn[:, :, :], q[:, :, :], AF.Sqrt, [qn], ["F9"], bias=1e-24)
            S.op("dve", lambda e: e.reciprocal(out=rn[:, :, :], in_=rn[:, :, :]), reads=["F9"], writes=["F9"])
            kkn = F[0]
            tt("dve", kkn[:, :, :], kkr[:, :, :], rn[:, :, :], ALU.mult, ["F0", "F9"], ["F0"])
            t1 = F[9]
            tt("dve", t1[:, :, :], a_t[:, :, :], cb(C_KA), ALU.mult, ["F3", "cst"], ["F9"])
            tt("pool", t1[:, :, :], t1[:, :, :], cb(C_OMKA), ALU.add, ["F9", "cst"], ["F9"])
            kmod = F[10]
            tt("pool", kmod[:, :, :], k_t[:, :, :], t1[:, :, :], ALU.mult, ["F7", "F9"], ["F10"])
            beta = F[11]
            tt("dve", beta[:, :, :], kkn[:, :, :], a_t[:, :, :], ALU.mult, ["F0", "F3"], ["F11"])
            Rt, Bt, Kt, At, Bh, Kh = H[10], H[11], H[12], H[13], H[14], H[15]
            tt("dve", Rt[:, :, :], r_t[:, :, :], E1[:, :, :], ALU.mult, ["F6", "F1"], ["H10"])
            tt("dve", Bt[:, :, :], beta[:, :, :], E3[:, :, :], ALU.mult, ["F11", "F2"], ["H11"])
            tt("pool", Kt[:, :, :], kmod[:, :, :], E3[:, :, :], ALU.mult, ["F10", "F2"], ["H12"])
            stt(At[:, :, 1:TT], kkn[:, :, 1:TT], -1.0, E1[:, :, 0:TT - 1], ALU.mult, ALU.mult, ["F0", "F1"], ["H13"])
            ts("dve", At[:, :, 0:1], kkn[:, :, 0:1], -1.0, None, ALU.mult, None, ["F0"], ["H13"])
            wcb = E1[:, :, TT - 1:TT].to_broadcast([128, NCH, TT])
            tt("pool", Bh[:, :, :], Bt[:, :, :], wcb, ALU.mult, ["H11", "F1"], ["H14"])
            tt("pool", Kh[:, :, :], Kt[:, :, :], wcb, ALU.mult, ["H12", "F1"], ["H15"])
            rk = H[9]
            tt("dve", F[9][:, :, :], r_t[:, :, :], cb(C_RK), ALU.mult, ["F6", "cst"], ["F9"])
            tt("dve", rk[:, :, :], F[9][:, :, :], kmod[:, :, :], ALU.mult, ["F9", "F10"], ["H9"])
            q, qn = nextQ()
            for c in range(NCH):
                mm(q[:, c, :], bones_b[:, :], rk[:, c, :], True, True, ["bones_b", "H9"], [qn], free=True)
            bv = F[9]
            tt("dve", bv[:, :, :], q[:, :, :], v_t[:, :, :], ALU.mult, [qn, "H3"], ["F9"])
            V_tm, Bh_tm, Kh_tm = H[0], H[1], H[4]
            for (src, srck, dst, dstk, eng) in ((v_bf, "H3", V_tm, "H0", "act"), (Bh, "H14", Bh_tm, "H1", "act"),
                                               (Kh, "H15", Kh_tm, "H4", "dve")):
                q, qn = nextQ()
                for c in range(NCH):
                    mm(q[:, c, :], src[:, c, :], ident_b[:, :], True, True, [srck, "ident_b"], [qn], free=True)
                cp(eng, dst[:, :, :], q[:, :, :], [qn], [dstk])
            Vf, Bhf, Khf = flat(V_tm), flat(Bh_tm), flat(Kh_tm)

            def hs(h):
                return (h % 2) * 64, h // 2

            def score(lh, lk, rh_, rk_, mask, dst, dstk, eng):
                for half in range(2):
                    q, qn = nextQ()
                    for hh in range(8):
                        h = half * 8 + hh
                        rb, c = hs(h)
                        mm(q[:, hh, :], lh[rb:rb + 64, c, :], rh_[rb:rb + 64, c, :], True, True, [lk, rk_], [qn])
                    tt(eng, dst[:, half * 8:(half + 1) * 8, :], q[:, :, :],
                       mask.unsqueeze(1).to_broadcast([128, 8, TT]), ALU.mult, [qn, "cm_f"], [dstk])

            score(Bt, "H11", At, "H13", msu, Ms["YA"], "YA", "dve")
            score(At, "H13", Bt, "H11", msl, Ms["YTA"], "YTA", "dve")
            score(Kt, "H12", At, "H13", msu, Ms["Mka"], "Mka", "dve")
            score(Bt, "H11", Rt, "H10", mui, Ms["Mbr"], "Mbr", "dve")
            score(Kt, "H12", Rt, "H10", mui, Ms["Mkr"], "Mkr", "dve")

            Y, YT, Yn, YTn = "YA", "YTA", "YB", "YTB"
            P, Pn = "PA", "PA"
            tt("pool", Ms[P][:, :, :], Ms[Y][:, :, :], ident_b[:, :].unsqueeze(1).to_broadcast([128, 16, TT]),
               ALU.add, [Y, "ident_b"], [P])
            for lvl in range(1, 7):
                for half in range(2):
                    q, qn = nextQ()
                    for hh in range(8):
                        h = half * 8 + hh
                        mm(q[:, hh, :], Ms[Y][:, h, :], Ms[YT][:, h, :], True, True, [Y, YT], [qn], free=True)
                    cp("act", Ms[YTn][:, half * 8:(half + 1) * 8, :], q[:, :, :], [qn], [YTn])
                if lvl < 6:
                    for half in range(2):
                        q, qn = nextQ()
                        for hh in range(8):
                            h = half * 8 + hh
                            mm(q[:, hh, :], Ms[YT][:, h, :], Ms[Y][:, h, :], True, True, [Y, YT], [qn], free=True)
                        cp("act", Ms[Yn][:, half * 8:(half + 1) * 8, :], q[:, :, :], [qn], [Yn])
                for half in range(2):
                    q, qn = nextQ()
                    for hh in range(8):
                        h = half * 8 + hh
                        mm(q[:, hh, :], Ms[YTn][:, h, :], Ms[P][:, h, :], True, True, [YTn, P], [qn], free=True)
                    tt("dve", Ms[Pn][:, half * 8:(half + 1) * 8, :], q[:, :, :],
                       Ms[P][:, half * 8:(half + 1) * 8, :], ALU.add, [qn, P], [Pn])
                Y, Yn = Yn, Y
                YT, YTn = YTn, YT
                P, Pn = Pn, P
            Tm, Tk = Ms[P], P

            X_bf, U_bf = H[7], H[9]
            Xf, Uf = flat(X_bf), flat(U_bf)
            q, qn = nextQ()
            qf = flat(q)
            for h in range(16):
                rb, c = hs(h)
                mm(qf[:, h * 64:(h + 1) * 64], At[rb:rb + 64, c, :], S_b[rb:rb + 64, c, :], True, False,
                   ["H13", "S_b"], [qn])
                mm(qf[:, h * 64:(h + 1) * 64], Ms["Mka"][:, h, :], Vf[:, h * 64:(h + 1) * 64], False, True,
                   ["Mka", "H0"], [qn])
            cp("act", Xf, qf, [qn], ["H7"])
            q, qn = nextQ()
            qf = flat(q)
            for h in range(16):
                mm(qf[:, h * 64:(h + 1) * 64], Tm[:, h, :], Xf[:, h * 64:(h + 1) * 64], True, True,
                   [Tk, "H7"], [qn], free=True)
            cp("act", Uf, qf, [qn], ["H9"])
            qo, qon = nextQ()
            for h in range(16):
                rb, c = hs(h)
                mm(qo[rb:rb + 64, c, :], S_b[rb:rb + 64, c, :], Rt[rb:rb + 64, c, :], True, False,
                   ["S_b", "H10"], [qon])
                mm(qo[rb:rb + 64, c, :], Uf[:, h * 64:(h + 1) * 64], Ms["Mbr"][:, h, :], False, False,
                   ["H9", "Mbr"], [qon])
                mm(qo[rb:rb + 64, c, :], Vf[:, h * 64:(h + 1) * 64], Ms["Mkr"][:, h, :], False, True,
                   ["H0", "Mkr"], [qon])
            qd, qdn = nextQ()
            qdf = flat(qd)[:, 0:512].rearrange("p (c v) -> p c v", c=NCH)
            for h in range(16):
                rb, c = hs(h)
                mm(qdf[rb:rb + 64, c, :], Bhf[:, c * 128 + rb:c * 128 + rb + 64], Uf[:, h * 64:(h + 1) * 64],
                   True, False, ["H1", "H9"], [qdn])
                mm(qdf[rb:rb + 64, c, :], Khf[:, c * 128 + rb:c * 128 + rb + 64], Vf[:, h * 64:(h + 1) * 64],
                   False, True, ["H4", "H0"], [qdn])
            tt("dve", S_f[:, :, :], S_f[:, :, :], E1[:, :, TT - 1:TT].to_broadcast([128, NCH, 64]), ALU.mult,
               ["S_f", "F1"], ["S_f"])
            tt("dve", S_f[:, :, :], qdf, S_f[:, :, :], ALU.add, [qdn, "S_f"], ["S_f"])
            cp("act", S_b[:, :, :], S_f[:, :, :], ["S_f"], ["S_b"])

            o_sb, osq = H[10], H[11]
            cp("act", o_sb[:, :, :], qo[:, :, :], [qon], ["H10"])
            act(osq[:, :, :], qo[:, :, :], AF.Square, [qon], ["H11"])
            qm, qmn = nextQ()
            for c in range(NCH):
                mm(qm[:, c, :], bones64_b[:, :], o_sb[:, c, :], True, True, ["bones64_b", "H10"], [qmn], free=True)
            qq, qqn = nextQ()
            for c in range(NCH):
                mm(qq[:, c, :], bones64_b[:, :], osq[:, c, :], True, True, ["bones64_b", "H11"], [qqn], free=True)
            gm, gv, gd = F[0], F[10], F[11]
            cp("act", gm[:, :, :], qm[:, :, :], [qmn], ["F0"])
            stt(gv[:, :, :], gm[:, :, :], -1.0, gm[:, :, :], ALU.mult, ALU.mult, ["F0"], ["F10"])
            stt(gv[:, :, :], qq[:, :, :], 64e-5, gv[:, :, :], ALU.add, ALU.add, [qqn, "F10"], ["F10"])
            act(gv[:, :, :], gv[:, :, :], AF.Sqrt, ["F10"], ["F10"])
            S.op("dve", lambda e: e.reciprocal(out=gv[:, :, :], in_=gv[:, :, :]), reads=["F10"], writes=["F10"])
            tt("dve", gd[:, :, :], qo[:, :, :], gm[:, :, :], ALU.subtract, [qon, "F0"], ["F11"])
            tt("dve", gd[:, :, :], gd[:, :, :], gv[:, :, :], ALU.mult, ["F11", "F10"], ["F11"])
            tt("pool", gd[:, :, :], gd[:, :, :], cb(C_LNG), ALU.mult, ["F11", "cst"], ["F11"])
            tt("pool", gd[:, :, :], gd[:, :, :], cb(C_LNB), ALU.add, ["F11", "cst"], ["F11"])
            tt("pool", gd[:, :, :], gd[:, :, :], bv[:, :, :], ALU.add, ["F11", "F9"], ["F11"])
            A_fm = H[12]
            tt("dve", A_fm[:, :, :], gd[:, :, :], g_sb[:, :, :], ALU.mult, ["F11", "H2"], ["H12"])

            slab, slabn = use_slab()
            qa, qan = nextQ()
            fm_proj(qa, qan, slab, slabn, lambda kc: A_fm[:, kc, :], ["H12"])
            after_slab()
            m1, m2 = F[0], F[10]
            stt(m1[:, :, :], tG[0][:, :, :], 1.0, qa[:, :, :], ALU.add, ALU.mult, ["H5", qan], ["F0"])
            slab, slabn = use_slab()
            qc, qcn = nextQ()
            fm_proj(qc, qcn, slab, slabn, lambda kc: C_fm[:, kc, :], ["H8"])
            after_slab()
            stt(m2[:, :, :], tG[1][:, :, :], 1.0, qc[:, :, :], ALU.add, ALU.mult, ["H6", qcn], ["F10"])
            merged = H[13]
            tt("pool", merged[:, :, :], m1[:, :, :], m2[:, :, :], ALU.add, ["F0", "F10"], ["H13"])
            slab, slabn = use_slab()
            q, qn = nextQ()
            qf = flat(q)
            for hf in range(2):
                for kc in range(NCH):
                    mm(qf[:, hf * 512:(hf + 1) * 512], merged[:, kc, :], slab[:, kc, hf * 512:(hf + 1) * 512],
                       kc == 0, kc == NCH - 1, ["H13", slabn], [qn], free=True)
            after_slab()
            stt(x_tm[:, :], qf, 0.5, x_tm[:, :], ALU.mult, ALU.add, [qn, "x_tm"], ["x_tm"])

            h2 = H[14]
            norm_to_fm(C_GFF, h2[:, :, :], "H14", 1)
            rl = H[15]
            for g in range(4):
                slab, slabn = use_slab()
                q, qn = nextQ()
                fm_proj(q, qn, slab, slabn, lambda kc: h2[:, kc, :], ["H14"])
                after_slab()
                act(rl[:, :, :], q[:, :, :], AF.Relu, [qn], ["H15"])
                tt("pool", r2[:, g * 8:(g + 1) * 8, :], rl[:, :, :], rl[:, :, :], ALU.mult, ["H15"], ["r2"])
            q, qn = nextQ()
            qf = flat(q)
            for g in range(4):
                slab, slabn = use_slab()
                for hf in range(2):
                    for kc in range(NCH):
                        mm(qf[:, hf * 512:(hf + 1) * 512], r2[:, g * 8 + kc, :], slab[:, kc, hf * 512:(hf + 1) * 512],
                           g == 0 and kc == 0, g == 3 and kc == NCH - 1, ["r2", slabn], [qn], free=True)
                after_slab()
            tt("dve", x_tm[:, :], qf, x_tm[:, :], ALU.add, [qn, "x_tm"], ["x_tm"])

            ytmp, osb = flat(F[0]), flat(F[11])
            S.op("pool", lambda e: e.memset(st[:, 2:3], 0.0), writes=["st"])
            act(junk[:, :], x_tm[:, :], AF.Square, ["x_tm"], ["H15", "st"], accum=st[:, 2:3])
            rstd_from_ss(2, 1e-6)
            act(ytmp, x_tm[:, :], AF.Identity, ["x_tm", "st"], ["F0"], scale=st[:, 2:3])
            tt("dve", osb, ytmp, gfin_bc[:, :], ALU.mult, ["F0", "gfin_bc"], ["F11"])
            dma("sp", out[it * TT:(it + 1) * TT, :], osb, ["F11"], ["out"])

        sems = {k: es.enter_context(nc.semaphore(f"s_{k}")) for k in ("pe", "act", "dve", "pool", "sp")}
        dsems = {(qn_, i): es.enter_context(nc.semaphore(f"d_{qn_}{i}"))
                 for qn_ in ("sp", "pool") for i in range(NSEM_DMA)}
        with nc.Block() as block:
            S.emit(nc, block, sems, dsems)
    return nc


_NC_CACHE = {}


def _prep_inputs(inp, b, ntiles):
    T = ntiles * TT
    f = lambda a: np.ascontiguousarray(np.asarray(a, dtype=np.float32))
    mu_rkv = f(inp["mu_rkv"])[0]
    mu_lora = f(inp["mu_lora"])[0]
    vrows = np.stack([
        f(inp["norm_mix_g"])[0], mu_rkv[0:D], mu_rkv[D:2 * D], mu_rkv[2 * D:3 * D],
        mu_lora[0], mu_lora[1], mu_lora[2], f(inp["k_k"])[0], f(inp["k_a"])[0],
        f(inp["r_k"])[0].reshape(-1), f(inp["ln_x_g"])[0], f(inp["ln_x_b"])[0],
        f(inp["conv_b"])[0], f(inp["conv_ln_g"])[0], f(inp["conv_ln_b"])[0], f(inp["norm_ff_g"])[0]], 0)
    m = {
        "x": f(inp["x"][b, :T]),
        "w_in": f(inp["w_in"])[0], "w_rwkv_proj": f(inp["w_rwkv_proj"])[0],
        "w_conv_proj": f(inp["w_conv_proj"])[0], "w_out": f(inp["w_out"])[0],
        "w_ff1": f(inp["w_ff1"])[0], "w_ff2": f(inp["w_ff2"])[0],
        "vrows": f(vrows), "conv_w": f(inp["conv_w"])[0],
        "decay_w0": f(inp["decay_w0"]), "aaa_a0": f(inp["aaa_a0"]), "b_gate": f(inp["b_gate"]),
        "norm_final_g": f(inp["norm_final_g"]).reshape(1, D),
        "decay_w1": f(inp["decay_w1"])[0], "decay_w2": f(inp["decay_w2"])[0],
        "aaa_a1": f(inp["aaa_a1"])[0], "aaa_a2": f(inp["aaa_a2"])[0],
        "gate_g1": f(inp["gate_g1"])[0], "gate_g2": f(inp["gate_g2"])[0],
        "cmat": _consts_np(),
    }
    return m


def run(inputs, ntiles, cores):
    if ntiles not in _NC_CACHE:
        _NC_CACHE[ntiles] = build(ntiles)
    nc = _NC_CACHE[ntiles]
    in_maps = [_prep_inputs(inputs, b, ntiles) for b in range(cores)]
    res = run_bass_kernel_spmd(nc, in_maps, core_ids=list(range(cores)))
    return np.stack([np.asarray(r["out"], dtype=np.float32) for r in res.results], 0)


def kernel(**inputs):
    return run(inputs, T_FULL // TT, 8)
```

```python
import contextlib
import numpy as np
import concourse.bass as bass
import concourse.mybir as mybir
from concourse.bass_utils import run_bass_kernel_spmd

F32 = mybir.dt.float32
BF16 = mybir.dt.bfloat16
AF = mybir.ActivationFunctionType
ALU = mybir.AluOpType

D = 1024
T_FULL = 8192
NCH = 8
TT = 128
NCST = 22
(C_GMIX, C_MUR, C_MUK, C_MUV, C_MUW, C_MUA, C_MUG, C_KK, C_KA, C_RK, C_LNG, C_LNB,
 C_CB, C_CLG, C_CLB, C_GFF, C_OMR, C_OMK, C_OMV, C_OMKA, C_CLG2, C_CLB2) = range(22)
NSLAB = 18
NSEM_DMA = 6
import os as _os
_STAGE = int(_os.environ.get("KSTAGE", "99"))
_SKIP = _os.environ.get("KSKIP", "").split(",")
_NOCONV = int(_os.environ.get("KNOCONV", "0"))


class _Op:
    __slots__ = ("eng", "fn", "waits", "count", "dma", "sem", "semval", "prewait", "needed", "f32", "free")


class Sched:
    def __init__(self):
        self.ops = {k: [] for k in ("pe", "act", "dve", "pool", "sp")}
        self.last_w = {}
        self.readers = {}
        self.ndma = {"sp": 0, "pool": 0}

    def op(self, eng, fn, reads=(), writes=(), dma=False):
        o = _Op()
        o.eng, o.fn, o.dma = eng, fn, dma
        o.f32 = False
        o.free = False
        deps = []
        for r in reads:
            w = self.last_w.get(r)
            if w is not None:
                deps.append(w)
        for w_ in writes:
            w = self.last_w.get(w_)
            if w is not None:
                deps.append(w)
            deps.extend(self.readers.get(w_, ()))
        o.waits = deps
        self.ops[eng].append(o)
        o.count = len(self.ops[eng])
        o.prewait = None
        if dma:
            n = self.ndma[eng]
            self.ndma[eng] = n + 1
            o.sem = (eng, n % NSEM_DMA)
            o.semval = 16 * (n // NSEM_DMA + 1)
            if n >= NSEM_DMA:
                o.prewait = (o.sem, 16 * (n // NSEM_DMA))
        for w_ in writes:
            self.last_w[w_] = o
            self.readers[w_] = []
        for r in reads:
            self.readers.setdefault(r, []).append(o)
        return o

    def emit(self, nc, block, sems, dsems):
        engs = {"pe": block.tensor, "act": block.scalar, "dve": block.vector,
                "pool": block.gpsimd, "sp": block.sync}
        for name, ops in self.ops.items():
            for o in ops:
                o.needed = False
        for name, ops in self.ops.items():
            prev = None
            for o in ops:
                keep = []
                for d in o.waits:
                    if name == "pe" and o.free and (not d.dma) and d.eng == "pe":
                        continue
                    d.needed = True
                    keep.append(d)
                if name == "pe" and prev is not None and (o.f32 != prev.f32):
                    prev.needed = True
                    keep.append(prev)
                o.waits = keep
                prev = o
        for name, ops in self.ops.items():
            c = 0
            for o in ops:
                if o.needed and not o.dma:
                    c += 1
                o.count = c
        for name, deco in engs.items():
            ops = self.ops[name]

            def body(e, ops=ops, name=name):
                waited = {}
                for o in ops:
                    need = {}
                    for d in o.waits:
                        if d.dma:
                            key, val = ("d",) + d.sem, d.semval
                        else:
                            key, val = ("e", d.eng), d.count
                        if need.get(key, 0) < val:
                            need[key] = val
                    if o.prewait is not None:
                        key, val = ("d",) + o.prewait[0], o.prewait[1]
                        if need.get(key, 0) < val:
                            need[key] = val
                    for key, val in need.items():
                        if waited.get(key, 0) >= val:
                            continue
                        waited[key] = val
                        s = dsems[key[1:]] if key[0] == "d" else sems[key[1]]
                        e.wait_ge(s, val)
                    ins = o.fn(e)
                    if o.dma:
                        ins.then_inc(dsems[o.sem], 16)
                    elif o.needed:
                        ins.then_inc(sems[name], 1)
                if name == "sp":
                    last = {}
                    for o in ops:
                        if o.dma:
                            last[o.sem] = o.semval
                    for k, v in last.items():
                        e.wait_ge(dsems[k], v)

            deco(body)


def _consts_np():
    i = np.arange(128)
    s, t = i[:, None], i[None, :]
    c = {}
    c["ident"] = (s == t).astype(np.float32)
    c["tri"] = np.where(s <= t, -0.5 * np.exp(-0.5), 0.0).astype(np.float32)
    c["msu"] = (t > s).astype(np.float32)
    c["mui"] = (t >= s).astype(np.float32)
    c["msl"] = (t < s).astype(np.float32)
    c["bones"] = ((s // 64) == (t // 64)).astype(np.float32)
    return np.stack([c[k] for k in ("ident", "tri", "msu", "mui", "msl", "bones")], 0)


def build(ntiles):
    nc = bass.Bass("TRN2", target_bir_lowering=False)
    T = ntiles * TT
    dt_in = {}

    def din(name, shape):
        dt_in[name] = nc.dram_tensor(name, list(shape), F32, kind="ExternalInput").ap()
        return dt_in[name]

    x = din("x", [T, D])
    w_in = din("w_in", [D, 7 * D])
    w_a = din("w_rwkv_proj", [D, D])
    w_c = din("w_conv_proj", [D, D])
    w_o = din("w_out", [D, D])
    w_f1 = din("w_ff1", [D, 4 * D])
    w_f2 = din("w_ff2", [4 * D, D])
    vrows = din("vrows", [16, D])
    convw = din("conv_w", [31, D])
    w0row = din("decay_w0", [1, D])
    a0row = din("aaa_a0", [1, D])
    bgrow = din("b_gate", [1, 2 * D])
    gfin = din("norm_final_g", [1, D])
    dw1 = din("decay_w1", [D, 64])
    dw2 = din("decay_w2", [64, D])
    aa1 = din("aaa_a1", [D, 64])
    aa2 = din("aaa_a2", [64, D])
    gg1 = din("gate_g1", [D, 128])
    gg2 = din("gate_g2", [128, D])
    cmat = din("cmat", [6, 128, 128])
    out = nc.dram_tensor("out", [T, D], F32, kind="ExternalOutput").ap()
    wbf = nc.dram_tensor("wbf", [NSLAB, 128, NCH * D], BF16, kind="ExternalOutput").ap()

    S = Sched()
    es = contextlib.ExitStack()
    with es:
        def sb(name, shape, dt):
            return es.enter_context(nc.sbuf_tensor(name, list(shape), dt))

        def ps(name, shape, dt=F32):
            return es.enter_context(nc.psum_tensor(name, list(shape), dt))

        Fs = [sb(f"F{i}", [128, NCH, TT], F32) for i in range(12) if i not in (4, 5, 8)]
        Fs = {i: t for i, t in zip([i for i in range(12) if i not in (4, 5, 8)], Fs)}
        Hs = [sb(f"H{i}", [128, NCH, TT], BF16) for i in range(16)]
        Ms = {k: sb(k, [128, 16, TT], BF16) for k in
              ("Mka", "Mbr", "Mkr", "YA", "YTA", "YB", "YTB", "PA")}
        ring = [sb(f"ring{i}", [128, NCH, D], BF16) for i in range(3)]
        Qs = [ps(f"Q{i}", [128, NCH, TT]) for i in range(4)]
        x_tm = sb("x_tm", [128, D], F32)
        h_ext = sb("h_ext", [128, NCH, TT + 1], BF16)
        u_ext = sb("u_ext", [128, NCH, TT + 30], BF16)
        lora_sb = sb("lora_sb", [128, TT], BF16)
        sg_sb = sb("sg_sb", [128, TT], BF16)
        tg_sb = sb("tg_sb", [128, TT], F32)
        r2 = sb("r2", [128, 32, TT], BF16)
        S_f = sb("S_f", [128, NCH, 64], F32)
        S_b = sb("S_b", [128, NCH, 64], BF16)
        halo = sb("halo", [128, 3, NCH, 1], F32)
        st = sb("st", [128, 8], F32)
        lnt = [sb(f"lnt{i}", [128, TT], F32) for i in range(4)]
        cst = sb("cst", [128, NCH, NCST], F32)
        cw = sb("cw", [128, NCH, 31], F32)
        cm_f = sb("cm_f", [128, 6, 128], F32)
        ident_b = sb("ident_b", [128, 128], BF16)
        bones_b = sb("bones_b", [128, 128], BF16)
        bones64_b = sb("bones64_b", [128, 128], BF16)
        onesdiv = sb("onesdiv", [128, 128], F32)
        gfin_bc = sb("gfin_bc", [128, D], F32)
        w0_row = sb("w0_row", [1, D], F32)
        a0_row = sb("a0_row", [1, D], BF16)
        bg_row = sb("bg_row", [1, 2 * D], BF16)
        ones_f = sb("ones_f", [1, 128], F32)
        ones_b = sb("ones_b", [1, 128], BF16)
        wa1 = sb("wa1", [128, NCH, 128], BF16)
        wa1mu = sb("wa1mu", [128, NCH, 128], BF16)
        g1 = sb("g1", [128, NCH, 128], BF16)
        g1mu = sb("g1mu", [128, NCH, 128], BF16)
        l2 = sb("l2", [128, D], BF16)
        g2 = sb("g2", [128, D], BF16)

        rows = Fs[0][:, :, :].rearrange("p c t -> p (c t)")
        hT = Hs[9][:, :, :].rearrange("p c t -> p (c t)")
        junk = Hs[15][:, :, :].rearrange("p c t -> p (c t)")
        ident_f = cm_f[:, 0, :]
        tri_f = cm_f[:, 1, :]
        msu, mui, msl = cm_f[:, 2, :], cm_f[:, 3, :], cm_f[:, 4, :]

        qi = [0]
        in_setup = [True]

        def nextQ():
            q = Qs[qi[0] % 4]
            qi[0] += 1
            return q, f"Q{(qi[0] - 1) % 4}"

        def cb(idx, n=TT):
            return cst[:, :, idx:idx + 1].to_broadcast([128, NCH, n])

        def mm(out_ap, lhsT, rhs, start, stop, reads, writes, free=False):
            if "pesetup" in _SKIP and in_setup[0]:
                return
            o_ = S.op("pe", lambda e: e.matmul(out_ap, lhsT, rhs, start=start, stop=stop),
                       reads=reads, writes=writes)
            o_.f32 = (lhsT.dtype == F32)
            o_.free = free

        def act(out_ap, in_ap, func, reads, writes, scale=1.0, bias=0.0, accum=None):
            if accum is None:
                S.op("act", lambda e: e.activation(out=out_ap, in_=in_ap, func=func,
                                                   bias=bias, scale=scale),
                     reads=reads, writes=writes)
            else:
                S.op("act", lambda e: e.activation(out=out_ap, in_=in_ap, func=func,
                                                   bias=bias, scale=scale, accum_out=accum),
                     reads=reads, writes=writes)

        def tt(eng, out_ap, a, b, op, reads, writes):
            S.op(eng, lambda e: e.tensor_tensor(out=out_ap, in0=a, in1=b, op=op),
                 reads=reads, writes=writes)

        def ts(eng, out_ap, a, s1, s2, op0, op1, reads, writes):
            if op1 is None:
                S.op(eng, lambda e: e.tensor_scalar(out=out_ap, in0=a, scalar1=s1, scalar2=None,
                                                    op0=op0), reads=reads, writes=writes)
            else:
                S.op(eng, lambda e: e.tensor_scalar(out=out_ap, in0=a, scalar1=s1, scalar2=s2,
                                                    op0=op0, op1=op1), reads=reads, writes=writes)

        def stt(out_ap, a, sc, b, op0, op1, reads, writes):
            S.op("dve", lambda e: e.scalar_tensor_tensor(out=out_ap, in0=a, scalar=sc, in1=b,
                                                         op0=op0, op1=op1),
                 reads=reads, writes=writes)

        def cp(eng, out_ap, in_ap, reads, writes):
            if eng == "act":
                act(out_ap, in_ap, AF.Copy, reads, writes)
            else:
                S.op(eng, lambda e: e.tensor_copy(out=out_ap, in_=in_ap), reads=reads, writes=writes)

        stg = {"n": 0}

        def dma(q, out_ap, in_ap, reads, writes, part=None, cols=None):
            if q == "pool":
                i = stg["n"] % 2
                stg["n"] += 1
                sf = flat(Fs[10 + i])
                p0, p1 = part
                if len(cols) == 2:
                    sview = sf[p0:p1, 0:cols[0] * cols[1]].rearrange("p (a b) -> p a b", a=cols[0])
                else:
                    sview = sf[p0:p1, 0:cols[0]]
                dma("sp", sview, in_ap, [], [f"F{10 + i}"])
                if out_ap.tensor.name.startswith("wbf"):
                    hv = flat(Hs[10 + i])[p0:p1, 0:cols[0]]
                    cp(("act", "dve")[i], hv, sview, [f"F{10 + i}"], [f"H{10 + i}"])
                    dma("sp", out_ap, hv, [f"H{10 + i}"], writes)
                else:
                    cp(("act", "dve")[i], out_ap, sview, [f"F{10 + i}"], writes)
                return
            S.op(q, lambda e: e.dma_start(out=out_ap, in_=in_ap), reads=reads, writes=writes, dma=True)

        def flat(tile3):
            return tile3[:, :, :].rearrange("p c t -> p (c t)")

        def slab_src(si):
            if si < 7:
                order = [0, 1, 2, 4, 3, 5, 6]
                g = order[si]
                return w_in[:, g * D:(g + 1) * D]
            if si == 7:
                return w_a
            if si == 8:
                return w_c
            if si == 9:
                return w_o
            if si < 14:
                g = si - 10
                return w_f1[:, g * D:(g + 1) * D]
            g = si - 14
            return w_f2[g * D:(g + 1) * D, :]

        for si in range(NSLAB if not _NOCONV else 0):
            src = slab_src(si).rearrange("(kc p) n -> p kc n", p=128)
            dst = wbf[si].rearrange("p (kc n) -> p kc n", kc=NCH)
            for kc in range(NCH):
                dma("pool", dst[:, kc, :], src[:, kc, :], reads=[], writes=[f"wbf{si}_{kc}"], part=(0, 128), cols=(D,))

        dma("pool", wa1[:, :, 0:64], dw1.rearrange("(kc p) n -> p kc n", p=128), [], ["wa1"], part=(0, 128), cols=(NCH, 64))
        dma("pool", wa1[:, :, 64:128], aa1.rearrange("(kc p) n -> p kc n", p=128), [], ["wa1"], part=(0, 128), cols=(NCH, 64))
        dma("pool", g1[:, :, :], gg1.rearrange("(kc p) n -> p kc n", p=128), [], ["g1"], part=(0, 128), cols=(NCH, 128))
        dma("pool", l2[0:64, :], dw2, [], ["l2"], part=(0, 64), cols=(D,))
        dma("pool", l2[64:128, :], aa2, [], ["l2"], part=(64, 128), cols=(D,))
        dma("pool", g2[:, :], gg2, [], ["g2"], part=(0, 128), cols=(D,))
        dma("pool", a0_row[:, :], a0row, [], ["a0_row"], part=(0, 1), cols=(D,))
        dma("pool", bg_row[:, 0:D], bgrow[:, 0:D], [], ["bg_row"], part=(0, 1), cols=(D,))
        dma("pool", bg_row[:, D:2 * D], bgrow[:, D:2 * D], [], ["bg_row"], part=(0, 1), cols=(D,))
        dma("sp", w0_row[:, :], w0row, [], ["w0_row"])
        dma("sp", cm_f[:, :, :], cmat.rearrange("k p n -> p k n"), [], ["cm_f"])
        dma("sp", rows[0:1, :], gfin, ["F0"], ["F0"])

        cp("dve", ident_b[:, :], ident_f, ["cm_f"], ["ident_b"])
        cp("dve", bones_b[:, :], cm_f[:, 5, :], ["cm_f"], ["bones_b"])
        ts("dve", bones64_b[:, :], cm_f[:, 5, :], 1.0 / 64, None, ALU.mult, None, ["cm_f"], ["bones64_b"])
        S.op("pool", lambda e: e.memset(onesdiv[:, :], 1.0 / D), writes=["onesdiv"])
        S.op("pool", lambda e: e.memset(ones_b[:, :], 1.0), writes=["ones_b"])
        S.op("pool", lambda e: e.memset(h_ext[:, :, :], 0.0), writes=["h_ext"])
        S.op("pool", lambda e: e.memset(u_ext[:, :, :], 0.0), writes=["u_ext"])
        S.op("pool", lambda e: e.memset(halo[:, :, :, :], 0.0), writes=["halo"])
        S.op("pool", lambda e: e.memset(S_f[:, :, :], 0.0), writes=["S_f"])
        S.op("pool", lambda e: e.memset(S_b[:, :, :], 0.0), writes=["S_b"])

        S.op("pool", lambda e: e.memset(ones_f[:, :], 1.0), writes=["ones_f"])
        q, qn = nextQ()
        qf = flat(q)
        for hf in range(2):
            mm(qf[:, hf * 512:(hf + 1) * 512], ones_f[0:1, :], rows[0:1, hf * 512:(hf + 1) * 512], True, True,
               ["ones_f", "F0"], [qn])
        cp("dve", gfin_bc[:, :], qf, [qn], ["gfin_bc"])
        dma("sp", rows[0:16, :], vrows, ["F0"], ["F0"])
        q, qn = nextQ()
        qf = flat(q)
        for c in range(NCH):
            mm(qf[:, c * 32:c * 32 + 16], rows[0:16, c * 128:(c + 1) * 128], cm_f[0:16, 0, 0:16],
               True, True, ["F0", "cm_f"], [qn])
        cp("dve", cst[:, :, 0:16], qf[:, 0:256].rearrange("p (c k) -> p c k", c=NCH)[:, :, 0:16],
           [qn], ["cst"])
        for (dst_i, src_i) in ((C_OMR, C_MUR), (C_OMK, C_MUK), (C_OMV, C_MUV), (C_OMKA, C_KA)):
            ts("dve", cst[:, :, dst_i:dst_i + 1], cst[:, :, src_i:src_i + 1], -1.0, 1.0,
               ALU.mult, ALU.add, ["cst"], ["cst"])
        for (dst_i, src_i) in ((C_CLG2, C_CLG), (C_CLB2, C_CLB)):
            ts("dve", cst[:, :, dst_i:dst_i + 1], cst[:, :, src_i:src_i + 1], 0.5, None,
               ALU.mult, None, ["cst"], ["cst"])
        dma("sp", rows[0:31, :], convw, ["F0"], ["F0"])
        q, qn = nextQ()
        qf = flat(q)
        for c in range(NCH):
            mm(qf[:, c * 32:c * 32 + 31], rows[0:31, c * 128:(c + 1) * 128], cm_f[0:31, 0, 0:31],
               True, True, ["F0", "cm_f"], [qn])
        ts("dve", cw[:, :, :], qf[:, 0:256].rearrange("p (c k) -> p c k", c=NCH)[:, :, 0:31],
           0.5, None, ALU.mult, None, [qn], ["cw"])
        tt("dve", wa1mu[:, :, 0:64], wa1[:, :, 0:64],
           cst[:, :, C_MUW:C_MUW + 1].to_broadcast([128, NCH, 64]), ALU.mult, ["wa1", "cst"], ["wa1mu"])
        tt("dve", wa1mu[:, :, 64:128], wa1[:, :, 64:128],
           cst[:, :, C_MUA:C_MUA + 1].to_broadcast([128, NCH, 64]), ALU.mult, ["wa1", "cst"], ["wa1mu"])
        tt("dve", g1mu[:, :, :], g1[:, :, :],
           cst[:, :, C_MUG:C_MUG + 1].to_broadcast([128, NCH, 128]), ALU.mult, ["g1", "cst"], ["g1mu"])

        gs = {"issued": 0, "used": 0}
        total_slabs = NSLAB * ntiles

        def issue_to(n):
            while gs["issued"] < min(n, total_slabs):
                g = gs["issued"]
                si, ri = g % NSLAB, g % 3
                dma("sp", ring[ri][:, :, :].rearrange("p c n -> p (c n)"), wbf[si],
                    [f"wbf{si}_{kc}" for kc in range(NCH)], [f"ring{ri}"])
                gs["issued"] += 1

        def use_slab():
            g = gs["used"]
            issue_to(g + 2)
            gs["used"] += 1
            return ring[g % 3], f"ring{g % 3}"

        def after_slab():
            issue_to(gs["used"] + 2)

        def rstd_from_ss(col, eps):
            ts("dve", st[:, col:col + 1], st[:, col:col + 1], 1.0 / D, eps, ALU.mult, ALU.add, ["st"], ["st"])
            act(st[:, col:col + 1], st[:, col:col + 1], AF.Sqrt, ["st"], ["st"])
            S.op("dve", lambda e: e.reciprocal(out=st[:, col:col + 1], in_=st[:, col:col + 1]),
                 reads=["st"], writes=["st"])

        def norm_to_fm(gidx, dst_ap, dst_key, col):
            S.op("pool", lambda e: e.memset(st[:, col:col + 1], 0.0), writes=["st"])
            act(junk[:, :], x_tm[:, :], AF.Square, ["x_tm"], ["H15", "st"], accum=st[:, col:col + 1])
            rstd_from_ss(col, 1e-6)
            act(hT[:, :], x_tm[:, :], AF.Identity, ["x_tm", "st"], ["H9"], scale=st[:, col:col + 1])
            q, qn = nextQ()
            for c in range(NCH):
                mm(q[:, c, :], hT[:, c * 128:(c + 1) * 128], ident_b[:, :], True, True,
                   ["H9", "ident_b"], [qn], free=True)
            tt("dve", dst_ap, q[:, :, :], cb(gidx), ALU.mult, [qn, "cst"], [dst_key])

        F = Fs
        H = Hs

        def fm_proj(q, qn, slab, slabn, rhs_fn, rhs_keys, bias_row=None, bias_off=0, bias_key=None):
            for mc in range(NCH):
                for kc in range(NCH):
                    mm(q[:, mc, :], slab[:, kc, mc * 128:(mc + 1) * 128], rhs_fn(kc),
                       kc == 0, (kc == NCH - 1) and bias_row is None, [slabn] + rhs_keys, [qn], free=True)
                if bias_row is not None:
                    mm(q[:, mc, :], bias_row[0:1, bias_off + mc * 128:bias_off + (mc + 1) * 128],
                       ones_b[0:1, :], False, True, [bias_key, "ones_b"], [qn])

        in_setup[0] = False
        for it in range(ntiles):
            if it > 0:
                cp("pool", h_ext[:, :, 0:1], h_ext[:, :, TT:TT + 1], ["h_ext"], ["h_ext"])
                cp("pool", u_ext[:, :, 0:30], u_ext[:, :, TT:TT + 30], ["u_ext"], ["u_ext"])
            dma("sp", x_tm[:, :], x[it * TT:(it + 1) * TT, :], [], ["x_tm"])
            if _STAGE == 0:
                dma("sp", out[it * TT:(it + 1) * TT, :], x_tm[:, :], ["x_tm"], ["out"])
                continue
            norm_to_fm(C_GMIX, h_ext[:, :, 1:TT + 1], "h_ext", 0)
            dh = H[0]
            tt("dve", dh[:, :, :], h_ext[:, :, 0:TT], h_ext[:, :, 1:TT + 1], ALU.subtract, ["h_ext"], ["H0"])
            hc = lambda kc: h_ext[:, kc, 1:TT + 1]

            q, qn = nextQ()
            qf = flat(q)
            for kc in range(NCH):
                mm(qf[:, 0:128], wa1[:, kc, :], hc(kc), kc == 0, False, ["wa1", "h_ext"], [qn])
            for kc in range(NCH):
                mm(qf[:, 0:128], wa1mu[:, kc, :], dh[:, kc, :], False, kc == NCH - 1, ["wa1mu", "H0"], [qn])
            for kc in range(NCH):
                mm(qf[:, 128:256], g1[:, kc, :], hc(kc), kc == 0, False, ["g1", "h_ext"], [qn])
            for kc in range(NCH):
                mm(qf[:, 128:256], g1mu[:, kc, :], dh[:, kc, :], False, kc == NCH - 1, ["g1mu", "H0"], [qn])
            act(lora_sb[0:64, :], qf[0:64, 0:128], AF.Tanh, [qn], ["lora_sb"])
            act(lora_sb[64:128, :], qf[64:128, 0:128], AF.Copy, [qn], ["lora_sb"])
            act(tg_sb[:, :], qf[:, 128:256], AF.Tanh, [qn], ["tg_sb"], scale=0.5)
            ts("dve", sg_sb[:, :], tg_sb[:, :], 0.5, 0.5, ALU.mult, ALU.add, ["tg_sb"], ["sg_sb"])

            q, qn = nextQ()
            qf = flat(q)
            for hf in range(2):
                mm(qf[:, hf * 512:(hf + 1) * 512], lora_sb[0:64, :], l2[0:64, hf * 512:(hf + 1) * 512],
                   True, False, ["lora_sb", "l2"], [qn])
                mm(qf[:, hf * 512:(hf + 1) * 512], ones_f[0:1, :], w0_row[0:1, hf * 512:(hf + 1) * 512],
                   False, True, ["ones_f", "w0_row"], [qn])
            s_tm = flat(F[0])
            act(s_tm, qf, AF.Tanh, [qn], ["F0"], scale=0.5)
            ts("dve", s_tm, s_tm, 1.0, None, ALU.add, None, ["F0"], ["F0"])
            q, qn = nextQ()
            for c in range(NCH):
                mm(q[:, c, :], s_tm[:, c * 128:(c + 1) * 128], tri_f, True, True, ["F0", "cm_f"], [qn])
            E1, E3 = F[1], F[2]
            act(E1[:, :, :], q[:, :, :], AF.Exp, [qn], ["F1"])
            act(E3[:, :, :], q[:, :, :], AF.Exp, [qn], ["F2"], scale=-1.0)

            q, qn = nextQ()
            for c in range(NCH):
                mm(q[:, c, :], l2[64:128, c * 128:(c + 1) * 128], lora_sb[64:128, :], True, False,
                   ["l2", "lora_sb"], [qn])
                mm(q[:, c, :], a0_row[0:1, c * 128:(c + 1) * 128], ones_b[0:1, :], False, True,
                   ["a0_row", "ones_b"], [qn])
            a_t = F[3]
            act(a_t[:, :, :], q[:, :, :], AF.Tanh, [qn], ["F3"], scale=0.5)
            ts("dve", a_t[:, :, :], a_t[:, :, :], 0.5, 0.5, ALU.mult, ALU.add, ["F3"], ["F3"])
            q, qn = nextQ()
            for c in range(NCH):
                mm(q[:, c, :], g2[:, c * 128:(c + 1) * 128], sg_sb[:, :], True, True, ["g2", "sg_sb"], [qn], free=True)
            g_sb = H[2]
            cp("act", g_sb[:, :, :], q[:, :, :], [qn], ["H2"])

            rkv = [F[6], F[7], H[3]]
            for j in range(3):
                slab, slabn = use_slab()
                q, qn = nextQ()
                fm_proj(q, qn, slab, slabn, hc, ["h_ext"])
                after_slab()
                tmp, t2 = F[0], F[9]
                tt("dve", tmp[:, :, :], q[:, :, :], cb(C_OMR + j), ALU.mult, [qn, "cst"], ["F0"])
                tt("dve", t2[:, :, 1:TT], q[:, :, 0:TT - 1], cb(C_MUR + j, TT - 1), ALU.mult, [qn, "cst"], ["F9"])
                tt("dve", t2[:, :, 0:1], halo[:, j, :, :], cst[:, :, C_MUR + j:C_MUR + j + 1], ALU.mult,
                   ["halo", "cst"], ["F9"])
                cp("act", halo[:, j, :, :], q[:, :, TT - 1:TT], [qn], ["halo"])
                tt("pool", rkv[j][:, :, :], tmp[:, :, :], t2[:, :, :], ALU.add, ["F0", "F9"], [("F6", "F7", "H3")[j]])
            r_t, k_t, v_t = rkv
            v_bf = v_t

            slab, slabn = use_slab()
            q, qn = nextQ()
            fm_proj(q, qn, slab, slabn, hc, ["h_ext"])
            after_slab()
            tb = H[4]
            act(tb[:, :, :], q[:, :, :], AF.Tanh, [qn], ["H4"], scale=0.5)
            slab, slabn = use_slab()
            q, qn = nextQ()
            fm_proj(q, qn, slab, slabn, hc, ["h_ext"])
            after_slab()
            stt(u_ext[:, :, 30:30 + TT], tb[:, :, :], 1.0, q[:, :, :], ALU.add, ALU.mult, ["H4", qn], ["u_ext"])
            tG = [H[5], H[6]]
            for j in range(2):
                slab, slabn = use_slab()
                q, qn = nextQ()
                fm_proj(q, qn, slab, slabn, hc, ["h_ext"], bias_row=bg_row, bias_off=j * D, bias_key="bg_row")
                after_slab()
                act(tG[j][:, :, :], q[:, :, :], AF.Tanh, [qn], [f"H{5 + j}"], scale=0.5)

            acc, prod = F[9], F[10]
            for j in range(31):
                wj = cw[:, :, j:j + 1].to_broadcast([128, NCH, TT])
                if j == 0:
                    tt("pool", acc[:, :, :], u_ext[:, :, 0:TT], wj, ALU.mult, ["u_ext", "cw"], ["F9"])
                else:
                    tt("pool", prod[:, :, :], u_ext[:, :, j:j + TT], wj, ALU.mult, ["u_ext", "cw"], ["F10"])
                    tt("pool", acc[:, :, :], acc[:, :, :], prod[:, :, :], ALU.add, ["F9", "F10"], ["F9"])
            tt("pool", acc[:, :, :], acc[:, :, :], cb(C_CB), ALU.add, ["F9", "cst"], ["F9"])
            kkr, ksq = F[0], H[9]
            tt("dve", kkr[:, :, :], k_t[:, :, :], cb(C_KK), ALU.mult, ["F7", "cst"], ["F0"])
            act(ksq[:, :, :], kkr[:, :, :], AF.Square, ["F0"], ["H9"])
            q, qn = nextQ()
            for c in range(NCH):
                mm(q[:, c, :], bones_b[:, :], ksq[:, c, :], True, True, ["bones_b", "H9"], [qn], free=True)
            rn = F[11]
            act(rn[:, :, :], q[:, :, :], AF.Sqrt, [qn], ["F11"], bias=1e-24)
            S.op("dve", lambda e: e.reciprocal(out=rn[:, :, :], in_=rn[:, :, :]), reads=["F11"], writes=["F11"])
            kkn = F[0]
            tt("dve", kkn[:, :, :], kkr[:, :, :], rn[:, :, :], ALU.mult, ["F0", "F11"], ["F0"])
            t1 = F[11]
            tt("dve", t1[:, :, :], a_t[:, :, :], cb(C_KA), ALU.mult, ["F3", "cst"], ["F11"])
            tt("dve", t1[:, :, :], t1[:, :, :], cb(C_OMKA), ALU.add, ["F11", "cst"], ["F11"])
            kmod = F[7]
            tt("dve", kmod[:, :, :], k_t[:, :, :], t1[:, :, :], ALU.mult, ["F7", "F11"], ["F7"])
            beta = F[11]
            tt("dve", beta[:, :, :], kkn[:, :, :], a_t[:, :, :], ALU.mult, ["F0", "F3"], ["F11"])
            Rt, Bt, Kt, At, Bh, Kh = H[10], H[11], H[12], H[13], H[14], H[15]
            tt("dve", Rt[:, :, :], r_t[:, :, :], E1[:, :, :], ALU.mult, ["F6", "F1"], ["H10"])
            tt("dve", Bt[:, :, :], beta[:, :, :], E3[:, :, :], ALU.mult, ["F11", "F2"], ["H11"])
            tt("dve", Kt[:, :, :], kmod[:, :, :], E3[:, :, :], ALU.mult, ["F7", "F2"], ["H12"])
            stt(At[:, :, 1:TT], kkn[:, :, 1:TT], -1.0, E1[:, :, 0:TT - 1], ALU.mult, ALU.mult, ["F0", "F1"], ["H13"])
            ts("dve", At[:, :, 0:1], kkn[:, :, 0:1], -1.0, None, ALU.mult, None, ["F0"], ["H13"])
            wcb = E1[:, :, TT - 1:TT].to_broadcast([128, NCH, TT])
            tt("dve", Bh[:, :, :], Bt[:, :, :], wcb, ALU.mult, ["H11", "F1"], ["H14"])
            tt("dve", Kh[:, :, :], Kt[:, :, :], wcb, ALU.mult, ["H12", "F1"], ["H15"])
            rk = H[9]
            tt("dve", F[3][:, :, :], r_t[:, :, :], cb(C_RK), ALU.mult, ["F6", "cst"], ["F3"])
            tt("dve", rk[:, :, :], F[3][:, :, :], kmod[:, :, :], ALU.mult, ["F3", "F7"], ["H9"])
            q, qn = nextQ()
            for c in range(NCH):
                mm(q[:, c, :], bones_b[:, :], rk[:, c, :], True, True, ["bones_b", "H9"], [qn], free=True)
            bv = F[3]
            tt("dve", bv[:, :, :], q[:, :, :], v_t[:, :, :], ALU.mult, [qn, "H3"], ["F3"])
            V_tm, Bh_tm, Kh_tm = H[0], H[1], H[4]
            for (src, srck, dst, dstk, eng) in ((v_bf, "H3", V_tm, "H0", "act"), (Bh, "H14", Bh_tm, "H1", "act"),
                                               (Kh, "H15", Kh_tm, "H4", "dve")):
                q, qn = nextQ()
                for c in range(NCH):
                    mm(q[:, c, :], src[:, c, :], ident_b[:, :], True, True, [srck, "ident_b"], [qn], free=True)
                cp(eng, dst[:, :, :], q[:, :, :], [qn], [dstk])
            Vf, Bhf, Khf = flat(V_tm), flat(Bh_tm), flat(Kh_tm)

            def hs(h):
                return (h % 2) * 64, h // 2

            def score(lh, lk, rh_, rk_, mask, dst, dstk, eng):
                for half in range(2):
                    q, qn = nextQ()
                    for hh in range(8):
                        h = half * 8 + hh
                        rb, c = hs(h)
                        mm(q[:, hh, :], lh[rb:rb + 64, c, :], rh_[rb:rb + 64, c, :], True, True, [lk, rk_], [qn])
                    tt(eng, dst[:, half * 8:(half + 1) * 8, :], q[:, :, :],
                       mask.unsqueeze(1).to_broadcast([128, 8, TT]), ALU.mult, [qn, "cm_f"], [dstk])

            score(Bt, "H11", At, "H13", msu, Ms["YA"], "YA", "dve")
            score(At, "H13", Bt, "H11", msl, Ms["YTA"], "YTA", "dve")
            score(Kt, "H12", At, "H13", msu, Ms["Mka"], "Mka", "dve")
            score(Bt, "H11", Rt, "H10", mui, Ms["Mbr"], "Mbr", "dve")
            score(Kt, "H12", Rt, "H10", mui, Ms["Mkr"], "Mkr", "dve")

            Y, YT, Yn, YTn = "YA", "YTA", "YB", "YTB"
            P, Pn = "PA", "PA"
            tt("dve", Ms[P][:, :, :], Ms[Y][:, :, :], ident_b[:, :].unsqueeze(1).to_broadcast([128, 16, TT]),
               ALU.add, [Y, "ident_b"], [P])
            for lvl in range(1, 7):
                for half in range(2):
                    q, qn = nextQ()
                    for hh in range(8):
                        h = half * 8 + hh
                        mm(q[:, hh, :], Ms[Y][:, h, :], Ms[YT][:, h, :], True, True, [Y, YT], [qn], free=True)
                    cp("act", Ms[YTn][:, half * 8:(half + 1) * 8, :], q[:, :, :], [qn], [YTn])
                if lvl < 6:
                    for half in range(2):
                        q, qn = nextQ()
                        for hh in range(8):
                            h = half * 8 + hh
                            mm(q[:, hh, :], Ms[YT][:, h, :], Ms[Y][:, h, :], True, True, [Y, YT], [qn], free=True)
                        cp("act", Ms[Yn][:, half * 8:(half + 1) * 8, :], q[:, :, :], [qn], [Yn])
                for half in range(2):
                    q, qn = nextQ()
                    for hh in range(8):
                        h = half * 8 + hh
                        mm(q[:, hh, :], Ms[YTn][:, h, :], Ms[P][:, h, :], True, True, [YTn, P], [qn], free=True)
                    tt("dve", Ms[Pn][:, half * 8:(half + 1) * 8, :], q[:, :, :],
                       Ms[P][:, half * 8:(half + 1) * 8, :], ALU.add, [qn, P], [Pn])
                Y, Yn = Yn, Y
                YT, YTn = YTn, YT
                P, Pn = Pn, P
            Tm, Tk = Ms[P], P

            X_bf, U_bf = H[7], H[9]
            Xf, Uf = flat(X_bf), flat(U_bf)
            q, qn = nextQ()
            qf = flat(q)
            for h in range(16):
                rb, c = hs(h)
                mm(qf[:, h * 64:(h + 1) * 64], At[rb:rb + 64, c, :], S_b[rb:rb + 64, c, :], True, False,
                   ["H13", "S_b"], [qn])
                mm(qf[:, h * 64:(h + 1) * 64], Ms["Mka"][:, h, :], Vf[:, h * 64:(h + 1) * 64], False, True,
                   ["Mka", "H0"], [qn])
            cp("act", Xf, qf, [qn], ["H7"])
            q, qn = nextQ()
            qf = flat(q)
            for h in range(16):
                mm(qf[:, h * 64:(h + 1) * 64], Tm[:, h, :], Xf[:, h * 64:(h + 1) * 64], True, True,
                   [Tk, "H7"], [qn], free=True)
            cp("act", Uf, qf, [qn], ["H9"])
            qo, qon = nextQ()
            for h in range(16):
                rb, c = hs(h)
                mm(qo[rb:rb + 64, c, :], S_b[rb:rb + 64, c, :], Rt[rb:rb + 64, c, :], True, False,
                   ["S_b", "H10"], [qon])
                mm(qo[rb:rb + 64, c, :], Uf[:, h * 64:(h + 1) * 64], Ms["Mbr"][:, h, :], False, False,
                   ["H9", "Mbr"], [qon])
                mm(qo[rb:rb + 64, c, :], Vf[:, h * 64:(h + 1) * 64], Ms["Mkr"][:, h, :], False, True,
                   ["H0", "Mkr"], [qon])
            qd, qdn = nextQ()
            qdf = flat(qd)[:, 0:512].rearrange("p (c v) -> p c v", c=NCH)
            for h in range(16):
                rb, c = hs(h)
                mm(qdf[rb:rb + 64, c, :], Bhf[:, c * 128 + rb:c * 128 + rb + 64], Uf[:, h * 64:(h + 1) * 64],
                   True, False, ["H1", "H9"], [qdn])
                mm(qdf[rb:rb + 64, c, :], Khf[:, c * 128 + rb:c * 128 + rb + 64], Vf[:, h * 64:(h + 1) * 64],
                   False, True, ["H4", "H0"], [qdn])
            tt("dve", S_f[:, :, :], S_f[:, :, :], E1[:, :, TT - 1:TT].to_broadcast([128, NCH, 64]), ALU.mult,
               ["S_f", "F1"], ["S_f"])
            tt("dve", S_f[:, :, :], qdf, S_f[:, :, :], ALU.add, [qdn, "S_f"], ["S_f"])
            cp("act", S_b[:, :, :], S_f[:, :, :], ["S_f"], ["S_b"])

            o_sb, osq = H[10], H[11]
            cp("act", o_sb[:, :, :], qo[:, :, :], [qon], ["H10"])
            act(osq[:, :, :], qo[:, :, :], AF.Square, [qon], ["H11"])
            qm, qmn = nextQ()
            for c in range(NCH):
                mm(qm[:, c, :], bones64_b[:, :], o_sb[:, c, :], True, True, ["bones64_b", "H10"], [qmn], free=True)
            qq, qqn = nextQ()
            for c in range(NCH):
                mm(qq[:, c, :], bones64_b[:, :], osq[:, c, :], True, True, ["bones64_b", "H11"], [qqn], free=True)
            gm, gv, gd = F[0], F[10], F[11]
            cp("act", gm[:, :, :], qm[:, :, :], [qmn], ["F0"])
            stt(gv[:, :, :], gm[:, :, :], -1.0, gm[:, :, :], ALU.mult, ALU.mult, ["F0"], ["F10"])
            stt(gv[:, :, :], qq[:, :, :], 64e-5, gv[:, :, :], ALU.add, ALU.add, [qqn, "F10"], ["F10"])
            act(gv[:, :, :], gv[:, :, :], AF.Sqrt, ["F10"], ["F10"])
            S.op("dve", lambda e: e.reciprocal(out=gv[:, :, :], in_=gv[:, :, :]), reads=["F10"], writes=["F10"])
            tt("dve", gd[:, :, :], qo[:, :, :], gm[:, :, :], ALU.subtract, [qon, "F0"], ["F11"])
            tt("dve", gd[:, :, :], gd[:, :, :], gv[:, :, :], ALU.mult, ["F11", "F10"], ["F11"])
            tt("pool", gd[:, :, :], gd[:, :, :], cb(C_LNG), ALU.mult, ["F11", "cst"], ["F11"])
            tt("pool", gd[:, :, :], gd[:, :, :], cb(C_LNB), ALU.add, ["F11", "cst"], ["F11"])
            tt("pool", gd[:, :, :], gd[:, :, :], bv[:, :, :], ALU.add, ["F11", "F3"], ["F11"])
            A_fm = H[12]
            tt("dve", A_fm[:, :, :], gd[:, :, :], g_sb[:, :, :], ALU.mult, ["F11", "H2"], ["H12"])

            sq = F[10]
            act(sq[:, :, :], acc[:, :, :], AF.Square, ["F9"], ["F10"])
            q, qn = nextQ()
            qf = flat(q)
            for c in range(NCH):
                mm(qf[:, 0:128], onesdiv[:, :], acc[:, c, :], c == 0, c == NCH - 1, ["onesdiv", "F9"], [qn])
            for c in range(NCH):
                mm(qf[:, 128:256], onesdiv[:, :], sq[:, c, :], c == 0, c == NCH - 1, ["onesdiv", "F10"], [qn])
            mean_s, nm2, var_s = lnt[0], lnt[1], lnt[2]
            cp("act", mean_s[:, :], qf[:, 0:128], [qn], ["lnt0"])
            stt(nm2[:, :], mean_s[:, :], -1.0, mean_s[:, :], ALU.mult, ALU.mult, ["lnt0"], ["lnt1"])
            stt(var_s[:, :], qf[:, 128:256], 1e-5, nm2[:, :], ALU.add, ALU.add, [qn, "lnt1"], ["lnt2"])
            act(var_s[:, :], var_s[:, :], AF.Sqrt, ["lnt2"], ["lnt2"])
            S.op("dve", lambda e: e.reciprocal(out=var_s[:, :], in_=var_s[:, :]), reads=["lnt2"], writes=["lnt2"])
            tt("dve", acc[:, :, :], acc[:, :, :], mean_s[:, :].unsqueeze(1).to_broadcast([128, NCH, TT]),
               ALU.subtract, ["F9", "lnt0"], ["F9"])
            tt("dve", acc[:, :, :], acc[:, :, :], var_s[:, :].unsqueeze(1).to_broadcast([128, NCH, TT]),
               ALU.mult, ["F9", "lnt2"], ["F9"])
            tt("dve", acc[:, :, :], acc[:, :, :], cb(C_CLG2), ALU.mult, ["F9", "cst"], ["F9"])
            tt("dve", acc[:, :, :], acc[:, :, :], cb(C_CLB2), ALU.add, ["F9", "cst"], ["F9"])
            th = H[7]
            act(th[:, :, :], acc[:, :, :], AF.Tanh, ["F9"], ["H7"])
            C_fm = H[8]
            stt(C_fm[:, :, :], th[:, :, :], 1.0, acc[:, :, :], ALU.add, ALU.mult, ["H7", "F9"], ["H8"])

            slab, slabn = use_slab()
            qa, qan = nextQ()
            fm_proj(qa, qan, slab, slabn, lambda kc: A_fm[:, kc, :], ["H12"])
            after_slab()
            m1, m2 = F[0], F[10]
            stt(m1[:, :, :], tG[0][:, :, :], 1.0, qa[:, :, :], ALU.add, ALU.mult, ["H5", qan], ["F0"])
            slab, slabn = use_slab()
            qc, qcn = nextQ()
            fm_proj(qc, qcn, slab, slabn, lambda kc: C_fm[:, kc, :], ["H8"])
            after_slab()
            stt(m2[:, :, :], tG[1][:, :, :], 1.0, qc[:, :, :], ALU.add, ALU.mult, ["H6", qcn], ["F10"])
            merged = H[13]
            tt("pool", merged[:, :, :], m1[:, :, :], m2[:, :, :], ALU.add, ["F0", "F10"], ["H13"])
            slab, slabn = use_slab()
            q, qn = nextQ()
            qf = flat(q)
            for hf in range(2):
                for kc in range(NCH):
                    mm(qf[:, hf * 512:(hf + 1) * 512], merged[:, kc, :], slab[:, kc, hf * 512:(hf + 1) * 512],
                       kc == 0, kc == NCH - 1, ["H13", slabn], [qn], free=True)
            after_slab()
            stt(x_tm[:, :], qf, 0.5, x_tm[:, :], ALU.mult, ALU.add, [qn, "x_tm"], ["x_tm"])

            h2 = H[14]
            norm_to_fm(C_GFF, h2[:, :, :], "H14", 1)
            rl = H[15]
            for g in range(4):
                slab, slabn = use_slab()
                q, qn = nextQ()
                fm_proj(q, qn, slab, slabn, lambda kc: h2[:, kc, :], ["H14"])
                after_slab()
                act(rl[:, :, :], q[:, :, :], AF.Relu, [qn], ["H15"])
                tt("pool", r2[:, g * 8:(g + 1) * 8, :], rl[:, :, :], rl[:, :, :], ALU.mult, ["H15"], ["r2"])
            q, qn = nextQ()
            qf = flat(q)
            for g in range(4):
                slab, slabn = use_slab()
                for hf in range(2):
                    for kc in range(NCH):
                        mm(qf[:, hf * 512:(hf + 1) * 512], r2[:, g * 8 + kc, :], slab[:, kc, hf * 512:(hf + 1) * 512],
                           g == 0 and kc == 0, g == 3 and kc == NCH - 1, ["r2", slabn], [qn], free=True)
                after_slab()
            tt("dve", x_tm[:, :], qf, x_tm[:, :], ALU.add, [qn, "x_tm"], ["x_tm"])

            ytmp, osb = flat(F[0]), flat(F[11])
            S.op("pool", lambda e: e.memset(st[:, 2:3], 0.0), writes=["st"])
            act(junk[:, :], x_tm[:, :], AF.Square, ["x_tm"], ["H15", "st"], accum=st[:, 2:3])
            rstd_from_ss(2, 1e-6)
            act(ytmp, x_tm[:, :], AF.Identity, ["x_tm", "st"], ["F0"], scale=st[:, 2:3])
            tt("dve", osb, ytmp, gfin_bc[:, :], ALU.mult, ["F0", "gfin_bc"], ["F11"])
            dma("sp", out[it * TT:(it + 1) * TT, :], osb, ["F11"], ["out"])

        sems = {k: es.enter_context(nc.semaphore(f"s_{k}")) for k in ("pe", "act", "dve", "pool", "sp")}
        dsems = {(qn_, i): es.enter_context(nc.semaphore(f"d_{qn_}{i}"))
                 for qn_ in ("sp", "pool") for i in range(NSEM_DMA)}
        with nc.Block() as block:
            S.emit(nc, block, sems, dsems)
    return nc


_NC_CACHE = {}


def _prep_inputs(inp, b, ntiles):
    T = ntiles * TT
    f = lambda a: np.ascontiguousarray(np.asarray(a, dtype=np.float32))
    mu_rkv = f(inp["mu_rkv"])[0]
    mu_lora = f(inp["mu_lora"])[0]
    vrows = np.stack([
        f(inp["norm_mix_g"])[0], mu_rkv[0:D], mu_rkv[D:2 * D], mu_rkv[2 * D:3 * D],
        mu_lora[0], mu_lora[1], mu_lora[2], f(inp["k_k"])[0], f(inp["k_a"])[0],
        f(inp["r_k"])[0].reshape(-1), f(inp["ln_x_g"])[0], f(inp["ln_x_b"])[0],
        f(inp["conv_b"])[0], f(inp["conv_ln_g"])[0], f(inp["conv_ln_b"])[0], f(inp["norm_ff_g"])[0]], 0)
    m = {
        "x": f(inp["x"][b, :T]),
        "w_in": f(inp["w_in"])[0], "w_rwkv_proj": f(inp["w_rwkv_proj"])[0],
        "w_conv_proj": f(inp["w_conv_proj"])[0], "w_out": f(inp["w_out"])[0],
        "w_ff1": f(inp["w_ff1"])[0], "w_ff2": f(inp["w_ff2"])[0],
        "vrows": f(vrows), "conv_w": f(inp["conv_w"])[0],
        "decay_w0": f(inp["decay_w0"]), "aaa_a0": f(inp["aaa_a0"]), "b_gate": f(inp["b_gate"]),
        "norm_final_g": f(inp["norm_final_g"]).reshape(1, D),
        "decay_w1": f(inp["decay_w1"])[0], "decay_w2": f(inp["decay_w2"])[0],
        "aaa_a1": f(inp["aaa_a1"])[0], "aaa_a2": f(inp["aaa_a2"])[0],
        "gate_g1": f(inp["gate_g1"])[0], "gate_g2": f(inp["gate_g2"])[0],
        "cmat": _consts_np(),
    }
    return m


def run(inputs, ntiles, cores):
    if ntiles not in _NC_CACHE:
        _NC_CACHE[ntiles] = build(ntiles)
    nc = _NC_CACHE[ntiles]
    in_maps = [_prep_inputs(inputs, b, ntiles) for b in range(cores)]
    res = run_bass_kernel_spmd(nc, in_maps, core_ids=list(range(cores)))
    return np.stack([np.asarray(r["out"], dtype=np.float32) for r in res.results], 0)


def kernel(**inputs):
    return run(inputs, T_FULL // TT, 8)
```

```python
import contextlib
import numpy as np
import concourse.bass as bass
import concourse.mybir as mybir
from concourse.bass_utils import run_bass_kernel_spmd

F32 = mybir.dt.float32
BF16 = mybir.dt.bfloat16
AF = mybir.ActivationFunctionType
ALU = mybir.AluOpType

D = 1024
T_FULL = 8192
NCH = 8
TT = 128
NCST = 22
(C_GMIX, C_MUR, C_MUK, C_MUV, C_MUW, C_MUA, C_MUG, C_KK, C_KA, C_RK, C_LNG, C_LNB,
 C_CB, C_CLG, C_CLB, C_GFF, C_OMR, C_OMK, C_OMV, C_OMKA, C_CLG2, C_CLB2) = range(22)
NSLAB = 18
NSEM_DMA = 6
import os as _os
_STAGE = int(_os.environ.get("KSTAGE", "99"))
_SKIP = _os.environ.get("KSKIP", "").split(",")
_NOCONV = int(_os.environ.get("KNOCONV", "0"))


class _Op:
    __slots__ = ("eng", "fn", "waits", "count", "dma", "sem", "semval", "prewait", "needed", "f32", "free")


class Sched:
    def __init__(self):
        self.ops = {k: [] for k in ("pe", "act", "dve", "pool", "sp")}
        self.last_w = {}
        self.readers = {}
        self.ndma = {"sp": 0, "pool": 0}

    def op(self, eng, fn, reads=(), writes=(), dma=False):
        o = _Op()
        o.eng, o.fn, o.dma = eng, fn, dma
        o.f32 = False
        o.free = False
        deps = []
        for r in reads:
            w = self.last_w.get(r)
            if w is not None:
                deps.append(w)
        for w_ in writes:
            w = self.last_w.get(w_)
            if w is not None:
                deps.append(w)
            deps.extend(self.readers.get(w_, ()))
        o.waits = deps
        self.ops[eng].append(o)
        o.count = len(self.ops[eng])
        o.prewait = None
        if dma:
            n = self.ndma[eng]
            self.ndma[eng] = n + 1
            o.sem = (eng, n % NSEM_DMA)
            o.semval = 16 * (n // NSEM_DMA + 1)
            if n >= NSEM_DMA:
                o.prewait = (o.sem, 16 * (n // NSEM_DMA))
        for w_ in writes:
            self.last_w[w_] = o
            self.readers[w_] = []
        for r in reads:
            self.readers.setdefault(r, []).append(o)
        return o

    def emit(self, nc, block, sems, dsems):
        engs = {"pe": block.tensor, "act": block.scalar, "dve": block.vector,
                "pool": block.gpsimd, "sp": block.sync}
        for name, ops in self.ops.items():
            for o in ops:
                o.needed = False
        for name, ops in self.ops.items():
            prev = None
            for o in ops:
                keep = []
                for d in o.waits:
                    if name == "pe" and o.free and (not d.dma) and d.eng == "pe":
                        continue
                    d.needed = True
                    keep.append(d)
                if name == "pe" and prev is not None and (o.f32 != prev.f32):
                    prev.needed = True
                    keep.append(prev)
                o.waits = keep
                prev = o
        for name, ops in self.ops.items():
            c = 0
            for o in ops:
                if o.needed and not o.dma:
                    c += 1
                o.count = c
        for name, deco in engs.items():
            ops = self.ops[name]

            def body(e, ops=ops, name=name):
                waited = {}
                for o in ops:
                    need = {}
                    for d in o.waits:
                        if d.dma:
                            key, val = ("d",) + d.sem, d.semval
                        else:
                            key, val = ("e", d.eng), d.count
                        if need.get(key, 0) < val:
                            need[key] = val
                    if o.prewait is not None:
                        key, val = ("d",) + o.prewait[0], o.prewait[1]
                        if need.get(key, 0) < val:
                            need[key] = val
                    for key, val in need.items():
                        if waited.get(key, 0) >= val:
                            continue
                        waited[key] = val
                        s = dsems[key[1:]] if key[0] == "d" else sems[key[1]]
                        e.wait_ge(s, val)
                    ins = o.fn(e)
                    if o.dma:
                        ins.then_inc(dsems[o.sem], 16)
                    elif o.needed:
                        ins.then_inc(sems[name], 1)
                if name == "sp":
                    last = {}
                    for o in ops:
                        if o.dma:
                            last[o.sem] = o.semval
                    for k, v in last.items():
                        e.wait_ge(dsems[k], v)

            deco(body)


def _consts_np():
    i = np.arange(128)
    s, t = i[:, None], i[None, :]
    c = {}
    c["ident"] = (s == t).astype(np.float32)
    c["tri"] = np.where(s <= t, -0.5 * np.exp(-0.5), 0.0).astype(np.float32)
    c["msu"] = (t > s).astype(np.float32)
    c["mui"] = (t >= s).astype(np.float32)
    c["msl"] = (t < s).astype(np.float32)
    c["bones"] = ((s // 64) == (t // 64)).astype(np.float32)
    return np.stack([c[k] for k in ("ident", "tri", "msu", "mui", "msl", "bones")], 0)


def build(ntiles):
    nc = bass.Bass("TRN2", target_bir_lowering=False)
    T = ntiles * TT
    dt_in = {}

    def din(name, shape):
        dt_in[name] = nc.dram_tensor(name, list(shape), F32, kind="ExternalInput").ap()
        return dt_in[name]

    x = din("x", [T, D])
    w_in = din("w_in", [D, 7 * D])
    w_a = din("w_rwkv_proj", [D, D])
    w_c = din("w_conv_proj", [D, D])
    w_o = din("w_out", [D, D])
    w_f1 = din("w_ff1", [D, 4 * D])
    w_f2 = din("w_ff2", [4 * D, D])
    vrows = din("vrows", [16, D])
    convw = din("conv_w", [31, D])
    w0row = din("decay_w0", [1, D])
    a0row = din("aaa_a0", [1, D])
    bgrow = din("b_gate", [1, 2 * D])
    gfin = din("norm_final_g", [1, D])
    dw1 = din("decay_w1", [D, 64])
    dw2 = din("decay_w2", [64, D])
    aa1 = din("aaa_a1", [D, 64])
    aa2 = din("aaa_a2", [64, D])
    gg1 = din("gate_g1", [D, 128])
    gg2 = din("gate_g2", [128, D])
    cmat = din("cmat", [6, 128, 128])
    out = nc.dram_tensor("out", [T, D], F32, kind="ExternalOutput").ap()
    wbf = nc.dram_tensor("wbf", [NSLAB, 128, NCH * D], BF16, kind="ExternalOutput").ap()

    S = Sched()
    es = contextlib.ExitStack()
    with es:
        def sb(name, shape, dt):
            return es.enter_context(nc.sbuf_tensor(name, list(shape), dt))

        def ps(name, shape, dt=F32):
            return es.enter_context(nc.psum_tensor(name, list(shape), dt))

        Fs = [sb(f"F{i}", [128, NCH, TT], F32) for i in range(12) if i not in (4, 5, 8)]
        Fs = {i: t for i, t in zip([i for i in range(12) if i not in (4, 5, 8)], Fs)}
        Hs = [sb(f"H{i}", [128, NCH, TT], BF16) for i in range(16)]
        Ms = {k: sb(k, [128, 16, TT], BF16) for k in
              ("Mka", "Mbr", "Mkr", "YA", "YTA", "YB", "YTB", "PA")}
        ring = [sb(f"ring{i}", [128, NCH, D], BF16) for i in range(3)]
        Qs = [ps(f"Q{i}", [128, NCH, TT]) for i in range(4)]
        x_tm = sb("x_tm", [128, D], F32)
        h_ext = sb("h_ext", [128, NCH, TT + 1], BF16)
        u_ext = sb("u_ext", [128, NCH, TT + 30], BF16)
        lora_sb = sb("lora_sb", [128, TT], BF16)
        sg_sb = sb("sg_sb", [128, TT], BF16)
        tg_sb = sb("tg_sb", [128, TT], F32)
        r2 = sb("r2", [128, 32, TT], BF16)
        S_f = sb("S_f", [128, NCH, 64], F32)
        S_b = sb("S_b", [128, NCH, 64], BF16)
        halo = sb("halo", [128, 3, NCH, 1], F32)
        st = sb("st", [128, 8], F32)
        lnt = [sb(f"lnt{i}", [128, TT], F32) for i in range(4)]
        cst = sb("cst", [128, NCH, NCST], F32)
        cw = sb("cw", [128, NCH, 31], F32)
        cm_f = sb("cm_f", [128, 6, 128], F32)
        ident_b = sb("ident_b", [128, 128], BF16)
        bones_b = sb("bones_b", [128, 128], BF16)
        bones64_b = sb("bones64_b", [128, 128], BF16)
        onesdiv = sb("onesdiv", [128, 128], F32)
        gfin_bc = sb("gfin_bc", [128, D], F32)
        w0_row = sb("w0_row", [1, D], F32)
        a0_row = sb("a0_row", [1, D], BF16)
        bg_row = sb("bg_row", [1, 2 * D], BF16)
        ones_f = sb("ones_f", [1, 128], F32)
        ones_b = sb("ones_b", [1, 128], BF16)
        wa1 = sb("wa1", [128, NCH, 128], BF16)
        wa1mu = sb("wa1mu", [128, NCH, 128], BF16)
        g1 = sb("g1", [128, NCH, 128], BF16)
        g1mu = sb("g1mu", [128, NCH, 128], BF16)
        l2 = sb("l2", [128, D], BF16)
        g2 = sb("g2", [128, D], BF16)

        rows = Fs[0][:, :, :].rearrange("p c t -> p (c t)")
        hT = Hs[9][:, :, :].rearrange("p c t -> p (c t)")
        junk = Hs[15][:, :, :].rearrange("p c t -> p (c t)")
        ident_f = cm_f[:, 0, :]
        tri_f = cm_f[:, 1, :]
        msu, mui, msl = cm_f[:, 2, :], cm_f[:, 3, :], cm_f[:, 4, :]

        qi = [0]
        in_setup = [True]

        def nextQ():
            q = Qs[qi[0] % 4]
            qi[0] += 1
            return q, f"Q{(qi[0] - 1) % 4}"

        def cb(idx, n=TT):
            return cst[:, :, idx:idx + 1].to_broadcast([128, NCH, n])

        def mm(out_ap, lhsT, rhs, start, stop, reads, writes, free=False):
            if "pesetup" in _SKIP and in_setup[0]:
                return
            o_ = S.op("pe", lambda e: e.matmul(out_ap, lhsT, rhs, start=start, stop=stop),
                       reads=reads, writes=writes)
            o_.f32 = (lhsT.dtype == F32)
            o_.free = free

        def act(out_ap, in_ap, func, reads, writes, scale=1.0, bias=0.0, accum=None):
            if accum is None:
                S.op("act", lambda e: e.activation(out=out_ap, in_=in_ap, func=func,
                                                   bias=bias, scale=scale),
                     reads=reads, writes=writes)
            else:
                S.op("act", lambda e: e.activation(out=out_ap, in_=in_ap, func=func,
                                                   bias=bias, scale=scale, accum_out=accum),
                     reads=reads, writes=writes)

        def tt(eng, out_ap, a, b, op, reads, writes):
            S.op(eng, lambda e: e.tensor_tensor(out=out_ap, in0=a, in1=b, op=op),
                 reads=reads, writes=writes)

        def ts(eng, out_ap, a, s1, s2, op0, op1, reads, writes):
            if op1 is None:
                S.op(eng, lambda e: e.tensor_scalar(out=out_ap, in0=a, scalar1=s1, scalar2=None,
                                                    op0=op0), reads=reads, writes=writes)
            else:
                S.op(eng, lambda e: e.tensor_scalar(out=out_ap, in0=a, scalar1=s1, scalar2=s2,
                                                    op0=op0, op1=op1), reads=reads, writes=writes)

        def stt(out_ap, a, sc, b, op0, op1, reads, writes):
            S.op("dve", lambda e: e.scalar_tensor_tensor(out=out_ap, in0=a, scalar=sc, in1=b,
                                                         op0=op0, op1=op1),
                 reads=reads, writes=writes)

        def cp(eng, out_ap, in_ap, reads, writes):
            if eng == "act":
                act(out_ap, in_ap, AF.Copy, reads, writes)
            else:
                S.op(eng, lambda e: e.tensor_copy(out=out_ap, in_=in_ap), reads=reads, writes=writes)

        stg = {"n": 0}

        def dma(q, out_ap, in_ap, reads, writes, part=None, cols=None):
            if q == "pool":
                i = stg["n"] % 2
                stg["n"] += 1
                sf = flat(Fs[10 + i])
                p0, p1 = part
                if len(cols) == 2:
                    sview = sf[p0:p1, 0:cols[0] * cols[1]].rearrange("p (a b) -> p a b", a=cols[0])
                else:
                    sview = sf[p0:p1, 0:cols[0]]
                dma("sp", sview, in_ap, [], [f"F{10 + i}"])
                if out_ap.tensor.name.startswith("wbf"):
                    hv = flat(Hs[10 + i])[p0:p1, 0:cols[0]]
                    cp(("act", "dve")[i], hv, sview, [f"F{10 + i}"], [f"H{10 + i}"])
                    dma("sp", out_ap, hv, [f"H{10 + i}"], writes)
                else:
                    cp(("act", "dve")[i], out_ap, sview, [f"F{10 + i}"], writes)
                return
            S.op(q, lambda e: e.dma_start(out=out_ap, in_=in_ap), reads=reads, writes=writes, dma=True)

        def flat(tile3):
            return tile3[:, :, :].rearrange("p c t -> p (c t)")

        def slab_src(si):
            if si < 7:
                order = [0, 1, 2, 4, 3, 5, 6]
                g = order[si]
                return w_in[:, g * D:(g + 1) * D]
            if si == 7:
                return w_a
            if si == 8:
                return w_c
            if si == 9:
                return w_o
            if si < 14:
                g = si - 10
                return w_f1[:, g * D:(g + 1) * D]
            g = si - 14
            return w_f2[g * D:(g + 1) * D, :]

        for si in range(NSLAB if not _NOCONV else 0):
            src = slab_src(si).rearrange("(kc p) n -> p kc n", p=128)
            dst = wbf[si].rearrange("p (kc n) -> p kc n", kc=NCH)
            for kc in range(NCH):
                dma("pool", dst[:, kc, :], src[:, kc, :], reads=[], writes=[f"wbf{si}_{kc}"], part=(0, 128), cols=(D,))

        dma("pool", wa1[:, :, 0:64], dw1.rearrange("(kc p) n -> p kc n", p=128), [], ["wa1"], part=(0, 128), cols=(NCH, 64))
        dma("pool", wa1[:, :, 64:128], aa1.rearrange("(kc p) n -> p kc n", p=128), [], ["wa1"], part=(0, 128), cols=(NCH, 64))
        dma("pool", g1[:, :, :], gg1.rearrange("(kc p) n -> p kc n", p=128), [], ["g1"], part=(0, 128), cols=(NCH, 128))
        dma("pool", l2[0:64, :], dw2, [], ["l2"], part=(0, 64), cols=(D,))
        dma("pool", l2[64:128, :], aa2, [], ["l2"], part=(64, 128), cols=(D,))
        dma("pool", g2[:, :], gg2, [], ["g2"], part=(0, 128), cols=(D,))
        dma("pool", a0_row[:, :], a0row, [], ["a0_row"], part=(0, 1), cols=(D,))
        dma("pool", bg_row[:, 0:D], bgrow[:, 0:D], [], ["bg_row"], part=(0, 1), cols=(D,))
        dma("pool", bg_row[:, D:2 * D], bgrow[:, D:2 * D], [], ["bg_row"], part=(0, 1), cols=(D,))
        dma("sp", w0_row[:, :], w0row, [], ["w0_row"])
        dma("sp", cm_f[:, :, :], cmat.rearrange("k p n -> p k n"), [], ["cm_f"])
        dma("sp", rows[0:1, :], gfin, ["F0"], ["F0"])

        cp("dve", ident_b[:, :], ident_f, ["cm_f"], ["ident_b"])
        cp("dve", bones_b[:, :], cm_f[:, 5, :], ["cm_f"], ["bones_b"])
        ts("dve", bones64_b[:, :], cm_f[:, 5, :], 1.0 / 64, None, ALU.mult, None, ["cm_f"], ["bones64_b"])
        S.op("pool", lambda e: e.memset(onesdiv[:, :], 1.0 / D), writes=["onesdiv"])
        S.op("pool", lambda e: e.memset(ones_b[:, :], 1.0), writes=["ones_b"])
        S.op("pool", lambda e: e.memset(h_ext[:, :, :], 0.0), writes=["h_ext"])
        S.op("pool", lambda e: e.memset(u_ext[:, :, :], 0.0), writes=["u_ext"])
        S.op("pool", lambda e: e.memset(halo[:, :, :, :], 0.0), writes=["halo"])
        S.op("pool", lambda e: e.memset(S_f[:, :, :], 0.0), writes=["S_f"])
        S.op("pool", lambda e: e.memset(S_b[:, :, :], 0.0), writes=["S_b"])

        S.op("pool", lambda e: e.memset(ones_f[:, :], 1.0), writes=["ones_f"])
        q, qn = nextQ()
        qf = flat(q)
        for hf in range(2):
            mm(qf[:, hf * 512:(hf + 1) * 512], ones_f[0:1, :], rows[0:1, hf * 512:(hf + 1) * 512], True, True,
               ["ones_f", "F0"], [qn])
        cp("dve", gfin_bc[:, :], qf, [qn], ["gfin_bc"])
        dma("sp", rows[0:16, :], vrows, ["F0"], ["F0"])
        q, qn = nextQ()
        qf = flat(q)
        for c in range(NCH):
            mm(qf[:, c * 32:c * 32 + 16], rows[0:16, c * 128:(c + 1) * 128], cm_f[0:16, 0, 0:16],
               True, True, ["F0", "cm_f"], [qn])
        cp("dve", cst[:, :, 0:16], qf[:, 0:256].rearrange("p (c k) -> p c k", c=NCH)[:, :, 0:16],
           [qn], ["cst"])
        for (dst_i, src_i) in ((C_OMR, C_MUR), (C_OMK, C_MUK), (C_OMV, C_MUV), (C_OMKA, C_KA)):
            ts("dve", cst[:, :, dst_i:dst_i + 1], cst[:, :, src_i:src_i + 1], -1.0, 1.0,
               ALU.mult, ALU.add, ["cst"], ["cst"])
        for (dst_i, src_i) in ((C_CLG2, C_CLG), (C_CLB2, C_CLB)):
            ts("dve", cst[:, :, dst_i:dst_i + 1], cst[:, :, src_i:src_i + 1], 0.5, None,
               ALU.mult, None, ["cst"], ["cst"])
        dma("sp", rows[0:31, :], convw, ["F0"], ["F0"])
        q, qn = nextQ()
        qf = flat(q)
        for c in range(NCH):
            mm(qf[:, c * 32:c * 32 + 31], rows[0:31, c * 128:(c + 1) * 128], cm_f[0:31, 0, 0:31],
               True, True, ["F0", "cm_f"], [qn])
        ts("dve", cw[:, :, :], qf[:, 0:256].rearrange("p (c k) -> p c k", c=NCH)[:, :, 0:31],
           0.5, None, ALU.mult, None, [qn], ["cw"])
        tt("dve", wa1mu[:, :, 0:64], wa1[:, :, 0:64],
           cst[:, :, C_MUW:C_MUW + 1].to_broadcast([128, NCH, 64]), ALU.mult, ["wa1", "cst"], ["wa1mu"])
        tt("dve", wa1mu[:, :, 64:128], wa1[:, :, 64:128],
           cst[:, :, C_MUA:C_MUA + 1].to_broadcast([128, NCH, 64]), ALU.mult, ["wa1", "cst"], ["wa1mu"])
        tt("dve", g1mu[:, :, :], g1[:, :, :],
           cst[:, :, C_MUG:C_MUG + 1].to_broadcast([128, NCH, 128]), ALU.mult, ["g1", "cst"], ["g1mu"])

        gs = {"issued": 0, "used": 0}
        total_slabs = NSLAB * ntiles

        def issue_to(n):
            while gs["issued"] < min(n, total_slabs):
                g = gs["issued"]
                si, ri = g % NSLAB, g % 3
                dma("sp", ring[ri][:, :, :].rearrange("p c n -> p (c n)"), wbf[si],
                    [f"wbf{si}_{kc}" for kc in range(NCH)], [f"ring{ri}"])
                gs["issued"] += 1

        def use_slab():
            g = gs["used"]
            issue_to(g + 2)
            gs["used"] += 1
            return ring[g % 3], f"ring{g % 3}"

        def after_slab():
            issue_to(gs["used"] + 2)

        def rstd_from_ss(col, eps):
            ts("dve", st[:, col:col + 1], st[:, col:col + 1], 1.0 / D, eps, ALU.mult, ALU.add, ["st"], ["st"])
            act(st[:, col:col + 1], st[:, col:col + 1], AF.Sqrt, ["st"], ["st"])
            S.op("dve", lambda e: e.reciprocal(out=st[:, col:col + 1], in_=st[:, col:col + 1]),
                 reads=["st"], writes=["st"])

        def norm_to_fm(gidx, dst_ap, dst_key, col):
            S.op("pool", lambda e: e.memset(st[:, col:col + 1], 0.0), writes=["st"])
            act(junk[:, :], x_tm[:, :], AF.Square, ["x_tm"], ["H15", "st"], accum=st[:, col:col + 1])
            rstd_from_ss(col, 1e-6)
            act(hT[:, :], x_tm[:, :], AF.Identity, ["x_tm", "st"], ["H9"], scale=st[:, col:col + 1])
            q, qn = nextQ()
            for c in range(NCH):
                mm(q[:, c, :], hT[:, c * 128:(c + 1) * 128], ident_b[:, :], True, True,
                   ["H9", "ident_b"], [qn], free=True)
            tt("dve", dst_ap, q[:, :, :], cb(gidx), ALU.mult, [qn, "cst"], [dst_key])

        F = Fs
        H = Hs

        def fm_proj(q, qn, slab, slabn, rhs_fn, rhs_keys, bias_row=None, bias_off=0, bias_key=None):
            for mc in range(NCH):
                for kc in range(NCH):
                    mm(q[:, mc, :], slab[:, kc, mc * 128:(mc + 1) * 128], rhs_fn(kc),
                       kc == 0, (kc == NCH - 1) and bias_row is None, [slabn] + rhs_keys, [qn], free=True)
                if bias_row is not None:
                    mm(q[:, mc, :], bias_row[0:1, bias_off + mc * 128:bias_off + (mc + 1) * 128],
                       ones_b[0:1, :], False, True, [bias_key, "ones_b"], [qn])

        in_setup[0] = False
        for it in range(ntiles):
            if it > 0:
                cp("pool", h_ext[:, :, 0:1], h_ext[:, :, TT:TT + 1], ["h_ext"], ["h_ext"])
                cp("pool", u_ext[:, :, 0:30], u_ext[:, :, TT:TT + 30], ["u_ext"], ["u_ext"])
            dma("sp", x_tm[:, :], x[it * TT:(it + 1) * TT, :], [], ["x_tm"])
            if _STAGE == 0:
                dma("sp", out[it * TT:(it + 1) * TT, :], x_tm[:, :], ["x_tm"], ["out"])
                continue
            norm_to_fm(C_GMIX, h_ext[:, :, 1:TT + 1], "h_ext", 0)
            dh = H[0]
            tt("dve", dh[:, :, :], h_ext[:, :, 0:TT], h_ext[:, :, 1:TT + 1], ALU.subtract, ["h_ext"], ["H0"])
            hc = lambda kc: h_ext[:, kc, 1:TT + 1]

            q, qn = nextQ()
            qf = flat(q)
            for kc in range(NCH):
                mm(qf[:, 0:128], wa1[:, kc, :], hc(kc), kc == 0, False, ["wa1", "h_ext"], [qn], free=True)
            for kc in range(NCH):
                mm(qf[:, 0:128], wa1mu[:, kc, :], dh[:, kc, :], False, kc == NCH - 1, ["wa1mu", "H0"], [qn], free=True)
            for kc in range(NCH):
                mm(qf[:, 128:256], g1[:, kc, :], hc(kc), kc == 0, False, ["g1", "h_ext"], [qn], free=True)
            for kc in range(NCH):
                mm(qf[:, 128:256], g1mu[:, kc, :], dh[:, kc, :], False, kc == NCH - 1, ["g1mu", "H0"], [qn], free=True)
            act(lora_sb[0:64, :], qf[0:64, 0:128], AF.Tanh, [qn], ["lora_sb"])
            act(lora_sb[64:128, :], qf[64:128, 0:128], AF.Copy, [qn], ["lora_sb"])
            act(tg_sb[:, :], qf[:, 128:256], AF.Tanh, [qn], ["tg_sb"], scale=0.5)
            ts("dve", sg_sb[:, :], tg_sb[:, :], 0.5, 0.5, ALU.mult, ALU.add, ["tg_sb"], ["sg_sb"])

            q, qn = nextQ()
            qf = flat(q)
            for hf in range(2):
                mm(qf[:, hf * 512:(hf + 1) * 512], lora_sb[0:64, :], l2[0:64, hf * 512:(hf + 1) * 512],
                   True, False, ["lora_sb", "l2"], [qn])
                mm(qf[:, hf * 512:(hf + 1) * 512], ones_f[0:1, :], w0_row[0:1, hf * 512:(hf + 1) * 512],
                   False, True, ["ones_f", "w0_row"], [qn])
            s_tm = flat(F[0])
            act(s_tm, qf, AF.Tanh, [qn], ["F0"], scale=0.5)
            ts("dve", s_tm, s_tm, 1.0, None, ALU.add, None, ["F0"], ["F0"])
            q, qn = nextQ()
            for c in range(NCH):
                mm(q[:, c, :], s_tm[:, c * 128:(c + 1) * 128], tri_f, True, True, ["F0", "cm_f"], [qn])
            E1, E3 = F[1], F[2]
            act(E1[:, :, :], q[:, :, :], AF.Exp, [qn], ["F1"])
            act(E3[:, :, :], q[:, :, :], AF.Exp, [qn], ["F2"], scale=-1.0)

            q, qn = nextQ()
            for c in range(NCH):
                mm(q[:, c, :], l2[64:128, c * 128:(c + 1) * 128], lora_sb[64:128, :], True, False,
                   ["l2", "lora_sb"], [qn])
                mm(q[:, c, :], a0_row[0:1, c * 128:(c + 1) * 128], ones_b[0:1, :], False, True,
                   ["a0_row", "ones_b"], [qn])
            a_t = F[3]
            act(a_t[:, :, :], q[:, :, :], AF.Tanh, [qn], ["F3"], scale=0.5)
            ts("dve", a_t[:, :, :], a_t[:, :, :], 0.5, 0.5, ALU.mult, ALU.add, ["F3"], ["F3"])
            q, qn = nextQ()
            for c in range(NCH):
                mm(q[:, c, :], g2[:, c * 128:(c + 1) * 128], sg_sb[:, :], True, True, ["g2", "sg_sb"], [qn], free=True)
            g_sb = H[2]
            cp("act", g_sb[:, :, :], q[:, :, :], [qn], ["H2"])

            rkv = [F[6], F[7], H[3]]
            for j in range(3):
                slab, slabn = use_slab()
                q, qn = nextQ()
                fm_proj(q, qn, slab, slabn, hc, ["h_ext"])
                after_slab()
                tmp, t2 = F[0], F[9]
                tt("dve", tmp[:, :, :], q[:, :, :], cb(C_OMR + j), ALU.mult, [qn, "cst"], ["F0"])
                tt("dve", t2[:, :, 1:TT], q[:, :, 0:TT - 1], cb(C_MUR + j, TT - 1), ALU.mult, [qn, "cst"], ["F9"])
                tt("dve", t2[:, :, 0:1], halo[:, j, :, :], cst[:, :, C_MUR + j:C_MUR + j + 1], ALU.mult,
                   ["halo", "cst"], ["F9"])
                cp("act", halo[:, j, :, :], q[:, :, TT - 1:TT], [qn], ["halo"])
                tt("pool", rkv[j][:, :, :], tmp[:, :, :], t2[:, :, :], ALU.add, ["F0", "F9"], [("F6", "F7", "H3")[j]])
            r_t, k_t, v_t = rkv
            v_bf = v_t

            slab, slabn = use_slab()
            q, qn = nextQ()
            fm_proj(q, qn, slab, slabn, hc, ["h_ext"])
            after_slab()
            tb = H[4]
            act(tb[:, :, :], q[:, :, :], AF.Tanh, [qn], ["H4"], scale=0.5)
            slab, slabn = use_slab()
            q, qn = nextQ()
            fm_proj(q, qn, slab, slabn, hc, ["h_ext"])
            after_slab()
            stt(u_ext[:, :, 30:30 + TT], tb[:, :, :], 1.0, q[:, :, :], ALU.add, ALU.mult, ["H4", qn], ["u_ext"])
            tG = [H[5], H[6]]
            for j in range(2):
                slab, slabn = use_slab()
                q, qn = nextQ()
                fm_proj(q, qn, slab, slabn, hc, ["h_ext"], bias_row=bg_row, bias_off=j * D, bias_key="bg_row")
                after_slab()
                act(tG[j][:, :, :], q[:, :, :], AF.Tanh, [qn], [f"H{5 + j}"], scale=0.5)

            acc, prod = F[9], F[10]
            for j in range(31):
                wj = cw[:, :, j:j + 1].to_broadcast([128, NCH, TT])
                if j == 0:
                    tt("pool", acc[:, :, :], u_ext[:, :, 0:TT], wj, ALU.mult, ["u_ext", "cw"], ["F9"])
                else:
                    tt("pool", prod[:, :, :], u_ext[:, :, j:j + TT], wj, ALU.mult, ["u_ext", "cw"], ["F10"])
                    tt("pool", acc[:, :, :], acc[:, :, :], prod[:, :, :], ALU.add, ["F9", "F10"], ["F9"])
            tt("pool", acc[:, :, :], acc[:, :, :], cb(C_CB), ALU.add, ["F9", "cst"], ["F9"])
            kkr, ksq = F[0], H[9]
            tt("dve", kkr[:, :, :], k_t[:, :, :], cb(C_KK), ALU.mult, ["F7", "cst"], ["F0"])
            act(ksq[:, :, :], kkr[:, :, :], AF.Square, ["F0"], ["H9"])
            q, qn = nextQ()
            for c in range(NCH):
                mm(q[:, c, :], bones_b[:, :], ksq[:, c, :], True, True, ["bones_b", "H9"], [qn], free=True)
            rn = F[11]
            act(rn[:, :, :], q[:, :, :], AF.Sqrt, [qn], ["F11"], bias=1e-24)
            S.op("dve", lambda e: e.reciprocal(out=rn[:, :, :], in_=rn[:, :, :]), reads=["F11"], writes=["F11"])
            kkn = F[0]
            tt("dve", kkn[:, :, :], kkr[:, :, :], rn[:, :, :], ALU.mult, ["F0", "F11"], ["F0"])
            t1 = F[11]
            tt("dve", t1[:, :, :], a_t[:, :, :], cb(C_KA), ALU.mult, ["F3", "cst"], ["F11"])
            tt("dve", t1[:, :, :], t1[:, :, :], cb(C_OMKA), ALU.add, ["F11", "cst"], ["F11"])
            kmod = F[7]
            tt("dve", kmod[:, :, :], k_t[:, :, :], t1[:, :, :], ALU.mult, ["F7", "F11"], ["F7"])
            beta = F[11]
            tt("dve", beta[:, :, :], kkn[:, :, :], a_t[:, :, :], ALU.mult, ["F0", "F3"], ["F11"])
            Rt, Bt, Kt, At, Bh, Kh = H[10], H[11], H[12], H[13], H[14], H[15]
            tt("dve", Rt[:, :, :], r_t[:, :, :], E1[:, :, :], ALU.mult, ["F6", "F1"], ["H10"])
            tt("dve", Bt[:, :, :], beta[:, :, :], E3[:, :, :], ALU.mult, ["F11", "F2"], ["H11"])
            tt("dve", Kt[:, :, :], kmod[:, :, :], E3[:, :, :], ALU.mult, ["F7", "F2"], ["H12"])
            stt(At[:, :, 1:TT], kkn[:, :, 1:TT], -1.0, E1[:, :, 0:TT - 1], ALU.mult, ALU.mult, ["F0", "F1"], ["H13"])
            ts("dve", At[:, :, 0:1], kkn[:, :, 0:1], -1.0, None, ALU.mult, None, ["F0"], ["H13"])
            wcb = E1[:, :, TT - 1:TT].to_broadcast([128, NCH, TT])
            tt("dve", Bh[:, :, :], Bt[:, :, :], wcb, ALU.mult, ["H11", "F1"], ["H14"])
            tt("dve", Kh[:, :, :], Kt[:, :, :], wcb, ALU.mult, ["H12", "F1"], ["H15"])
            rk = H[9]
            tt("dve", F[3][:, :, :], r_t[:, :, :], cb(C_RK), ALU.mult, ["F6", "cst"], ["F3"])
            tt("dve", rk[:, :, :], F[3][:, :, :], kmod[:, :, :], ALU.mult, ["F3", "F7"], ["H9"])
            q, qn = nextQ()
            for c in range(NCH):
                mm(q[:, c, :], bones_b[:, :], rk[:, c, :], True, True, ["bones_b", "H9"], [qn], free=True)
            bv = F[3]
            tt("dve", bv[:, :, :], q[:, :, :], v_t[:, :, :], ALU.mult, [qn, "H3"], ["F3"])
            V_tm, Bh_tm, Kh_tm = H[0], H[1], H[4]
            for (src, srck, dst, dstk, eng) in ((v_bf, "H3", V_tm, "H0", "act"), (Bh, "H14", Bh_tm, "H1", "act"),
                                               (Kh, "H15", Kh_tm, "H4", "dve")):
                q, qn = nextQ()
                for c in range(NCH):
                    mm(q[:, c, :], src[:, c, :], ident_b[:, :], True, True, [srck, "ident_b"], [qn], free=True)
                cp(eng, dst[:, :, :], q[:, :, :], [qn], [dstk])
            Vf, Bhf, Khf = flat(V_tm), flat(Bh_tm), flat(Kh_tm)

            def hs(h):
                return (h % 2) * 64, h // 2

            def score(lh, lk, rh_, rk_, mask, dst, dstk, eng):
                for half in range(2):
                    q, qn = nextQ()
                    for hh in range(8):
                        h = half * 8 + hh
                        rb, c = hs(h)
                        mm(q[:, hh, :], lh[rb:rb + 64, c, :], rh_[rb:rb + 64, c, :], True, True, [lk, rk_], [qn])
                    tt(eng, dst[:, half * 8:(half + 1) * 8, :], q[:, :, :],
                       mask.unsqueeze(1).to_broadcast([128, 8, TT]), ALU.mult, [qn, "cm_f"], [dstk])

            score(Bt, "H11", At, "H13", msu, Ms["YA"], "YA", "dve")
            score(At, "H13", Bt, "H11", msl, Ms["YTA"], "YTA", "dve")
            score(Kt, "H12", At, "H13", msu, Ms["Mka"], "Mka", "dve")
            score(Bt, "H11", Rt, "H10", mui, Ms["Mbr"], "Mbr", "dve")
            score(Kt, "H12", Rt, "H10", mui, Ms["Mkr"], "Mkr", "dve")

            Y, YT, Yn, YTn = "YA", "YTA", "YB", "YTB"
            P, Pn = "PA", "PA"
            tt("dve", Ms[P][:, :, :], Ms[Y][:, :, :], ident_b[:, :].unsqueeze(1).to_broadcast([128, 16, TT]),
               ALU.add, [Y, "ident_b"], [P])
            for lvl in range(1, 7):
                for half in range(2):
                    q, qn = nextQ()
                    for hh in range(8):
                        h = half * 8 + hh
                        mm(q[:, hh, :], Ms[Y][:, h, :], Ms[YT][:, h, :], True, True, [Y, YT], [qn], free=True)
                    cp("act", Ms[YTn][:, half * 8:(half + 1) * 8, :], q[:, :, :], [qn], [YTn])
                if lvl < 6:
                    for half in range(2):
                        q, qn = nextQ()
                        for hh in range(8):
                            h = half * 8 + hh
                            mm(q[:, hh, :], Ms[YT][:, h, :], Ms[Y][:, h, :], True, True, [Y, YT], [qn], free=True)
                        cp("act", Ms[Yn][:, half * 8:(half + 1) * 8, :], q[:, :, :], [qn], [Yn])
                for half in range(2):
                    q, qn = nextQ()
                    for hh in range(8):
                        h = half * 8 + hh
                        mm(q[:, hh, :], Ms[YTn][:, h, :], Ms[P][:, h, :], True, True, [YTn, P], [qn], free=True)
                    tt("dve", Ms[Pn][:, half * 8:(half + 1) * 8, :], q[:, :, :],
                       Ms[P][:, half * 8:(half + 1) * 8, :], ALU.add, [qn, P], [Pn])
                Y, Yn = Yn, Y
                YT, YTn = YTn, YT
                P, Pn = Pn, P
            Tm, Tk = Ms[P], P

            X_bf, U_bf = H[7], H[9]
            Xf, Uf = flat(X_bf), flat(U_bf)
            q, qn = nextQ()
            qf = flat(q)
            for h in range(16):
                rb, c = hs(h)
                mm(qf[:, h * 64:(h + 1) * 64], At[rb:rb + 64, c, :], S_b[rb:rb + 64, c, :], True, False,
                   ["H13", "S_b"], [qn])
                mm(qf[:, h * 64:(h + 1) * 64], Ms["Mka"][:, h, :], Vf[:, h * 64:(h + 1) * 64], False, True,
                   ["Mka", "H0"], [qn])
            cp("act", Xf, qf, [qn], ["H7"])
            q, qn = nextQ()
            qf = flat(q)
            for h in range(16):
                mm(qf[:, h * 64:(h + 1) * 64], Tm[:, h, :], Xf[:, h * 64:(h + 1) * 64], True, True,
                   [Tk, "H7"], [qn], free=True)
            cp("act", Uf, qf, [qn], ["H9"])
            qo, qon = nextQ()
            for h in range(16):
                rb, c = hs(h)
                mm(qo[rb:rb + 64, c, :], S_b[rb:rb + 64, c, :], Rt[rb:rb + 64, c, :], True, False,
                   ["S_b", "H10"], [qon])
                mm(qo[rb:rb + 64, c, :], Uf[:, h * 64:(h + 1) * 64], Ms["Mbr"][:, h, :], False, False,
                   ["H9", "Mbr"], [qon])
                mm(qo[rb:rb + 64, c, :], Vf[:, h * 64:(h + 1) * 64], Ms["Mkr"][:, h, :], False, True,
                   ["H0", "Mkr"], [qon])
            qd, qdn = nextQ()
            qdf = flat(qd)[:, 0:512].rearrange("p (c v) -> p c v", c=NCH)
            for h in range(16):
                rb, c = hs(h)
                mm(qdf[rb:rb + 64, c, :], Bhf[:, c * 128 + rb:c * 128 + rb + 64], Uf[:, h * 64:(h + 1) * 64],
                   True, False, ["H1", "H9"], [qdn])
                mm(qdf[rb:rb + 64, c, :], Khf[:, c * 128 + rb:c * 128 + rb + 64], Vf[:, h * 64:(h + 1) * 64],
                   False, True, ["H4", "H0"], [qdn])
            tt("dve", S_f[:, :, :], S_f[:, :, :], E1[:, :, TT - 1:TT].to_broadcast([128, NCH, 64]), ALU.mult,
               ["S_f", "F1"], ["S_f"])
            tt("dve", S_f[:, :, :], qdf, S_f[:, :, :], ALU.add, [qdn, "S_f"], ["S_f"])
            cp("act", S_b[:, :, :], S_f[:, :, :], ["S_f"], ["S_b"])

            o_sb, osq = H[10], H[11]
            cp("act", o_sb[:, :, :], qo[:, :, :], [qon], ["H10"])
            act(osq[:, :, :], qo[:, :, :], AF.Square, [qon], ["H11"])
            qm, qmn = nextQ()
            for c in range(NCH):
                mm(qm[:, c, :], bones64_b[:, :], o_sb[:, c, :], True, True, ["bones64_b", "H10"], [qmn], free=True)
            qq, qqn = nextQ()
            for c in range(NCH):
                mm(qq[:, c, :], bones64_b[:, :], osq[:, c, :], True, True, ["bones64_b", "H11"], [qqn], free=True)
            gm, gv, gd = F[0], F[10], F[11]
            cp("act", gm[:, :, :], qm[:, :, :], [qmn], ["F0"])
            stt(gv[:, :, :], gm[:, :, :], -1.0, gm[:, :, :], ALU.mult, ALU.mult, ["F0"], ["F10"])
            stt(gv[:, :, :], qq[:, :, :], 64e-5, gv[:, :, :], ALU.add, ALU.add, [qqn, "F10"], ["F10"])
            act(gv[:, :, :], gv[:, :, :], AF.Sqrt, ["F10"], ["F10"])
            S.op("dve", lambda e: e.reciprocal(out=gv[:, :, :], in_=gv[:, :, :]), reads=["F10"], writes=["F10"])
            tt("dve", gd[:, :, :], qo[:, :, :], gm[:, :, :], ALU.subtract, [qon, "F0"], ["F11"])
            tt("dve", gd[:, :, :], gd[:, :, :], gv[:, :, :], ALU.mult, ["F11", "F10"], ["F11"])
            tt("pool", gd[:, :, :], gd[:, :, :], cb(C_LNG), ALU.mult, ["F11", "cst"], ["F11"])
            tt("pool", gd[:, :, :], gd[:, :, :], cb(C_LNB), ALU.add, ["F11", "cst"], ["F11"])
            tt("pool", gd[:, :, :], gd[:, :, :], bv[:, :, :], ALU.add, ["F11", "F3"], ["F11"])
            A_fm = H[12]
            tt("dve", A_fm[:, :, :], gd[:, :, :], g_sb[:, :, :], ALU.mult, ["F11", "H2"], ["H12"])

            sq = F[10]
            act(sq[:, :, :], acc[:, :, :], AF.Square, ["F9"], ["F10"])
            q, qn = nextQ()
            qf = flat(q)
            for c in range(NCH):
                mm(qf[:, 0:128], onesdiv[:, :], acc[:, c, :], c == 0, c == NCH - 1, ["onesdiv", "F9"], [qn])
            for c in range(NCH):
                mm(qf[:, 128:256], onesdiv[:, :], sq[:, c, :], c == 0, c == NCH - 1, ["onesdiv", "F10"], [qn])
            mean_s, nm2, var_s = lnt[0], lnt[1], lnt[2]
            cp("act", mean_s[:, :], qf[:, 0:128], [qn], ["lnt0"])
            stt(nm2[:, :], mean_s[:, :], -1.0, mean_s[:, :], ALU.mult, ALU.mult, ["lnt0"], ["lnt1"])
            stt(var_s[:, :], qf[:, 128:256], 1e-5, nm2[:, :], ALU.add, ALU.add, [qn, "lnt1"], ["lnt2"])
            act(var_s[:, :], var_s[:, :], AF.Sqrt, ["lnt2"], ["lnt2"])
            S.op("dve", lambda e: e.reciprocal(out=var_s[:, :], in_=var_s[:, :]), reads=["lnt2"], writes=["lnt2"])
            tt("dve", acc[:, :, :], acc[:, :, :], mean_s[:, :].unsqueeze(1).to_broadcast([128, NCH, TT]),
               ALU.subtract, ["F9", "lnt0"], ["F9"])
            tt("dve", acc[:, :, :], acc[:, :, :], var_s[:, :].unsqueeze(1).to_broadcast([128, NCH, TT]),
               ALU.mult, ["F9", "lnt2"], ["F9"])
            tt("dve", acc[:, :, :], acc[:, :, :], cb(C_CLG2), ALU.mult, ["F9", "cst"], ["F9"])
            tt("dve", acc[:, :, :], acc[:, :, :], cb(C_CLB2), ALU.add, ["F9", "cst"], ["F9"])
            th = H[7]
            act(th[:, :, :], acc[:, :, :], AF.Tanh, ["F9"], ["H7"])
            C_fm = H[8]
            stt(C_fm[:, :, :], th[:, :, :], 1.0, acc[:, :, :], ALU.add, ALU.mult, ["H7", "F9"], ["H8"])

            slab, slabn = use_slab()
            qa, qan = nextQ()
            fm_proj(qa, qan, slab, slabn, lambda kc: A_fm[:, kc, :], ["H12"])
            after_slab()
            m1, m2 = F[0], F[10]
            stt(m1[:, :, :], tG[0][:, :, :], 1.0, qa[:, :, :], ALU.add, ALU.mult, ["H5", qan], ["F0"])
            slab, slabn = use_slab()
            qc, qcn = nextQ()
            fm_proj(qc, qcn, slab, slabn, lambda kc: C_fm[:, kc, :], ["H8"])
            after_slab()
            stt(m2[:, :, :], tG[1][:, :, :], 1.0, qc[:, :, :], ALU.add, ALU.mult, ["H6", qcn], ["F10"])
            merged = H[13]
            tt("pool", merged[:, :, :], m1[:, :, :], m2[:, :, :], ALU.add, ["F0", "F10"], ["H13"])
            slab, slabn = use_slab()
            q, qn = nextQ()
            qf = flat(q)
            for hf in range(2):
                for kc in range(NCH):
                    mm(qf[:, hf * 512:(hf + 1) * 512], merged[:, kc, :], slab[:, kc, hf * 512:(hf + 1) * 512],
                       kc == 0, kc == NCH - 1, ["H13", slabn], [qn], free=True)
            after_slab()
            stt(x_tm[:, :], qf, 0.5, x_tm[:, :], ALU.mult, ALU.add, [qn, "x_tm"], ["x_tm"])

            h2 = H[14]
            norm_to_fm(C_GFF, h2[:, :, :], "H14", 1)
            rl = H[15]
            for g in range(4):
                slab, slabn = use_slab()
                q, qn = nextQ()
                fm_proj(q, qn, slab, slabn, lambda kc: h2[:, kc, :], ["H14"])
                after_slab()
                act(rl[:, :, :], q[:, :, :], AF.Relu, [qn], ["H15"])
                tt("pool", r2[:, g * 8:(g + 1) * 8, :], rl[:, :, :], rl[:, :, :], ALU.mult, ["H15"], ["r2"])
            q, qn = nextQ()
            qf = flat(q)
            for g in range(4):
                slab, slabn = use_slab()
                for hf in range(2):
                    for kc in range(NCH):
                        mm(qf[:, hf * 512:(hf + 1) * 512], r2[:, g * 8 + kc, :], slab[:, kc, hf * 512:(hf + 1) * 512],
                           g == 0 and kc == 0, g == 3 and kc == NCH - 1, ["r2", slabn], [qn], free=True)
                after_slab()
            tt("dve", x_tm[:, :], qf, x_tm[:, :], ALU.add, [qn, "x_tm"], ["x_tm"])

            ytmp, osb = flat(F[0]), flat(F[11])
            S.op("pool", lambda e: e.memset(st[:, 2:3], 0.0), writes=["st"])
            act(junk[:, :], x_tm[:, :], AF.Square, ["x_tm"], ["H15", "st"], accum=st[:, 2:3])
            rstd_from_ss(2, 1e-6)
            act(ytmp, x_tm[:, :], AF.Identity, ["x_tm", "st"], ["F0"], scale=st[:, 2:3])
            tt("dve", osb, ytmp, gfin_bc[:, :], ALU.mult, ["F0", "gfin_bc"], ["F11"])
            dma("sp", out[it * TT:(it + 1) * TT, :], osb, ["F11"], ["out"])

        sems = {k: es.enter_context(nc.semaphore(f"s_{k}")) for k in ("pe", "act", "dve", "pool", "sp")}
        dsems = {(qn_, i): es.enter_context(nc.semaphore(f"d_{qn_}{i}"))
                 for qn_ in ("sp", "pool") for i in range(NSEM_DMA)}
        with nc.Block() as block:
            S.emit(nc, block, sems, dsems)
    return nc


_NC_CACHE = {}


def _prep_inputs(inp, b, ntiles):
    T = ntiles * TT
    f = lambda a: np.ascontiguousarray(np.asarray(a, dtype=np.float32))
    mu_rkv = f(inp["mu_rkv"])[0]
    mu_lora = f(inp["mu_lora"])[0]
    vrows = np.stack([
        f(inp["norm_mix_g"])[0], mu_rkv[0:D], mu_rkv[D:2 * D], mu_rkv[2 * D:3 * D],
        mu_lora[0], mu_lora[1], mu_lora[2], f(inp["k_k"])[0], f(inp["k_a"])[0],
        f(inp["r_k"])[0].reshape(-1), f(inp["ln_x_g"])[0], f(inp["ln_x_b"])[0],
        f(inp["conv_b"])[0], f(inp["conv_ln_g"])[0], f(inp["conv_ln_b"])[0], f(inp["norm_ff_g"])[0]], 0)
    m = {
        "x": f(inp["x"][b, :T]),
        "w_in": f(inp["w_in"])[0], "w_rwkv_proj": f(inp["w_rwkv_proj"])[0],
        "w_conv_proj": f(inp["w_conv_proj"])[0], "w_out": f(inp["w_out"])[0],
        "w_ff1": f(inp["w_ff1"])[0], "w_ff2": f(inp["w_ff2"])[0],
        "vrows": f(vrows), "conv_w": f(inp["conv_w"])[0],
        "decay_w0": f(inp["decay_w0"]), "aaa_a0": f(inp["aaa_a0"]), "b_gate": f(inp["b_gate"]),
        "norm_final_g": f(inp["norm_final_g"]).reshape(1, D),
        "decay_w1": f(inp["decay_w1"])[0], "decay_w2": f(inp["decay_w2"])[0],
        "aaa_a1": f(inp["aaa_a1"])[0], "aaa_a2": f(inp["aaa_a2"])[0],
        "gate_g1": f(inp["gate_g1"])[0], "gate_g2": f(inp["gate_g2"])[0],
        "cmat": _consts_np(),
    }
    return m


def run(inputs, ntiles, cores):
    if ntiles not in _NC_CACHE:
        _NC_CACHE[ntiles] = build(ntiles)
    nc = _NC_CACHE[ntiles]
    in_maps = [_prep_inputs(inputs, b, ntiles) for b in range(cores)]
    res = run_bass_kernel_spmd(nc, in_maps, core_ids=list(range(cores)))
    return np.stack([np.asarray(r["out"], dtype=np.float32) for r in res.results], 0)


def kernel(**inputs):
    return run(inputs, T_FULL // TT, 8)
```
